# Optimizing a Trainium2 kernel written in Bass

```python
import math
import jax, jax.numpy as jnp
from jax import lax
import numpy as np

D_MODEL = 2048
BATCH = 4
SEQ = 2048
DEPTH = 2

CHUNK = 64
Q_BLOCK = 128
A_WIDTH = D_MODEL // 4
A_HEAD_DIM = 128
A_HALF = A_HEAD_DIM // 2
A_HEADS = A_WIDTH // A_HEAD_DIM
B_WIDTH = 3 * D_MODEL // 8
B_HEAD_DIM = 128
B_HEADS = B_WIDTH // B_HEAD_DIM
B_PAST_CHUNKS = 8
B_BAND = B_PAST_CHUNKS + 1
REL_CLIP = 256
C_WIDTH = D_MODEL - A_WIDTH - B_WIDTH
C_HEADS = 4
C_HEAD_DIM = C_WIDTH // C_HEADS
CONV_K = 4
MIX_WIDTH = A_WIDTH + B_WIDTH + C_WIDTH
D_FF = (8 * D_MODEL + 3 * 256 - 1) // (3 * 256) * 256
SPLITS = [A_WIDTH] * 3 + [B_WIDTH] * 3 + [C_WIDTH] * 4 + [C_HEADS] * 2
IN_WIDTH = sum(SPLITS)
SPLIT_POINTS = [int(v) for v in np.cumsum(SPLITS)[:-1]]
DEEPNORM_ALPHA = (2.0 * DEPTH) ** 0.25
DEEPNORM_BETA = (8.0 * DEPTH) ** -0.25
ALIBI_SLOPES = tuple(2.0 ** (-8.0 * (h + 1) / A_HEADS) for h in range(A_HEADS))
NEG = -1e30

kernel_name = "hybrid_diffattn_chunkband_mlstm_deepnorm"


def layer_norm(x, g, b, eps=1e-5):
    xf = x.astype(jnp.float32)
    mu = xf.mean(-1, keepdims=True)
    var = jnp.mean(jnp.square(xf - mu), -1, keepdims=True)
    return ((xf - mu) * lax.rsqrt(var + eps) * g.astype(jnp.float32) + b.astype(jnp.float32)).astype(x.dtype)


def rms_norm(x, g, eps=1e-6):
    xf = x.astype(jnp.float32)
    ms = jnp.mean(jnp.square(xf), -1, keepdims=True)
    return (xf * lax.rsqrt(ms + eps) * g.astype(jnp.float32)).astype(x.dtype)


def causal_depthwise_conv(x, w):
    return lax.conv_general_dilated(
        x, w[:, None, :], window_strides=(1,), padding=[(CONV_K - 1, 0)],
        dimension_numbers=("NWC", "WIO", "NWC"), feature_group_count=x.shape[-1])


def diff_attention(q, k, v, lam, subln, layer_idx):
    B, S, _ = q.shape
    nb = S // Q_BLOCK
    q = q.reshape(B, S, A_HEADS, 2, A_HALF)
    k = k.reshape(B, S, A_HEADS, 2, A_HALF)
    vh = v.reshape(B, S, A_HEADS, A_HEAD_DIM).transpose(0, 2, 1, 3)
    qb = q.reshape(B, nb, Q_BLOCK, A_HEADS, 2, A_HALF).transpose(1, 0, 3, 4, 2, 5)
    kt = k.transpose(0, 2, 3, 1, 4)
    lam_init = 0.8 - 0.6 * math.exp(-0.3 * layer_idx)
    lf = lam.astype(jnp.float32)
    lam_full = jnp.exp(jnp.sum(lf[0] * lf[1])) - jnp.exp(jnp.sum(lf[2] * lf[3])) + lam_init
    slopes = jnp.asarray(ALIBI_SLOPES, dtype=jnp.float32)
    key_pos = jnp.arange(S)
    key_chunk = key_pos // CHUNK
    scale = A_HALF ** -0.5

    def block(args):
        bi, qblk = args
        qpos = bi * Q_BLOCK + jnp.arange(Q_BLOCK)
        s = jnp.einsum("bhmqd,bhmkd->bhmqk", qblk, kt).astype(jnp.float32) * scale
        dist = jnp.abs(qpos[:, None] - key_pos[None, :]).astype(jnp.float32)
        allowed = key_chunk[None, :] <= (qpos // CHUNK)[:, None]
        bias = jnp.where(allowed[None], -slopes[:, None, None] * dist[None], NEG)
        p = jax.nn.softmax(s + bias[None, :, None], axis=-1)
        w = p[:, :, 0] - lam_full * p[:, :, 1]
        return jnp.einsum("bhqk,bhkd->bhqd", w.astype(vh.dtype), vh)

    out = lax.map(block, (jnp.arange(nb), qb))
    out = out.transpose(1, 0, 3, 2, 4).reshape(B, S, A_HEADS, A_HEAD_DIM)
    out = rms_norm(out, subln) * (1.0 - lam_init)
    return out.reshape(B, S, A_WIDTH)


def chunk_band_attention(q, k, v, rel_bias):
    B, S, _ = q.shape
    nc = S // CHUNK

    def heads(t):
        return t.reshape(B, nc, CHUNK, B_HEADS, B_HEAD_DIM).transpose(0, 3, 1, 2, 4)

    qh, kh, vh = heads(q), heads(k), heads(v)
    pad = ((0, 0), (0, 0), (B_PAST_CHUNKS, 0), (0, 0), (0, 0))
    idx = jnp.arange(nc)[:, None] + jnp.arange(B_BAND)[None, :]

    def band(t):
        return jnp.pad(t, pad)[:, :, idx].reshape(B, B_HEADS, nc, B_BAND * CHUNK, B_HEAD_DIM)

    kb, vb = band(kh), band(vh)
    s = jnp.einsum("bhcqd,bhckd->bhcqk", qh, kb).astype(jnp.float32) * B_HEAD_DIM ** -0.5
    rel = (B_PAST_CHUNKS * CHUNK + jnp.arange(CHUNK))[:, None] - jnp.arange(B_BAND * CHUNK)[None, :]
    rel_idx = jnp.clip(rel, -REL_CLIP, REL_CLIP) + REL_CLIP
    bias = rel_bias.astype(jnp.float32)[:, rel_idx]
    valid = jnp.repeat(idx - B_PAST_CHUNKS >= 0, CHUNK, axis=1)
    s = jnp.where(valid[None, None, :, None, :], s + bias[None, :, None], NEG)
    p = jax.nn.softmax(s, axis=-1)
    out = jnp.einsum("bhcqk,bhckd->bhcqd", p.astype(vb.dtype), vb)
    return out.transpose(0, 2, 3, 1, 4).reshape(B, S, B_WIDTH)


def mlstm(q, k, v, o_pre, i_pre, f_pre, conv_w, norm_g):
    B, S, _ = q.shape
    nc = S // CHUNK
    out_dtype = v.dtype
    qk = jax.nn.silu(causal_depthwise_conv(jnp.concatenate([q, k], axis=-1), conv_w))
    q, k = jnp.split(qk, 2, axis=-1)

    def heads(t):
        return t.astype(jnp.float32).reshape(B, nc, CHUNK, C_HEADS, C_HEAD_DIM).transpose(1, 0, 3, 2, 4)

    def gates(t):
        return t.astype(jnp.float32).reshape(B, nc, CHUNK, C_HEADS).transpose(1, 0, 3, 2)

    qh, kh, vh = heads(q), heads(k) * C_HEAD_DIM ** -0.5, heads(v)
    ig = gates(i_pre)
    lfg = jax.nn.log_sigmoid(gates(f_pre))
    tri = jnp.tril(jnp.ones((CHUNK, CHUNK), dtype=bool))

    def step(carry, xs):
        C, n, m = carry
        qc, kc, vc, ic, fc = xs
        b = jnp.cumsum(fc, axis=-1)
        D = jnp.where(tri, b[..., :, None] - b[..., None, :] + ic[..., None, :], NEG)
        inter = b + m[..., None]
        m_t = jnp.maximum(inter, D.max(-1))
        Sw = jnp.einsum("bhtd,bhsd->bhts", qc, kc) * jnp.exp(D - m_t[..., None])
        w_inter = jnp.exp(inter - m_t)
        num = jnp.einsum("bhts,bhsd->bhtd", Sw, vc) + w_inter[..., None] * jnp.einsum("bhvk,bhtk->bhtv", C, qc)
        den = Sw.sum(-1) + w_inter * jnp.einsum("bhk,bhtk->bht", n, qc)
        h = num / jnp.maximum(jnp.abs(den), jnp.exp(-m_t))[..., None]
        bL = b[..., -1]
        g = bL[..., None] - b + ic
        m_new = jnp.maximum(bL + m, g.max(-1))
        wk = jnp.exp(g - m_new[..., None])
        decay = jnp.exp(bL + m - m_new)
        C = decay[..., None, None] * C + jnp.einsum("bhs,bhsv,bhsk->bhvk", wk, vc, kc)
        n = decay[..., None] * n + jnp.einsum("bhs,bhsk->bhk", wk, kc)
        return (C, n, m_new), h

    init = (jnp.zeros((B, C_HEADS, C_HEAD_DIM, C_HEAD_DIM), jnp.float32),
            jnp.zeros((B, C_HEADS, C_HEAD_DIM), jnp.float32),
            jnp.zeros((B, C_HEADS), jnp.float32))
    _, h = lax.scan(step, init, (qh, kh, vh, ig, lfg))
    h = h.transpose(1, 0, 3, 2, 4).reshape(B, S, C_HEADS, C_HEAD_DIM)
    h = rms_norm(h, norm_g).reshape(B, S, C_WIDTH)
    return (jax.nn.sigmoid(o_pre.astype(jnp.float32)) * h).astype(out_dtype)


def setup_inputs(seed: int = 0) -> dict:
    key = jax.random.key(seed)
    ks = jax.random.split(key, 20)
    x = jax.random.normal(ks[0], (BATCH, SEQ, D_MODEL), jnp.float32)
    col_scale = np.ones((IN_WIDTH,), np.float32)
    col_scale[2 * A_WIDTH:3 * A_WIDTH] = DEEPNORM_BETA
    off_b = 3 * A_WIDTH
    col_scale[off_b + 2 * B_WIDTH:off_b + 3 * B_WIDTH] = DEEPNORM_BETA
    off_c = off_b + 3 * B_WIDTH
    col_scale[off_c + 2 * C_WIDTH:off_c + 3 * C_WIDTH] = DEEPNORM_BETA
    w_in = jax.random.normal(ks[1], (DEPTH, D_MODEL, IN_WIDTH), jnp.float32) * (D_MODEL ** -0.5) * jnp.asarray(col_scale)
    i_bias = 0.1 * jax.random.normal(ks[2], (DEPTH, C_HEADS), jnp.float32)
    f_bias = jnp.linspace(3.0, 6.0, C_HEADS, dtype=jnp.float32)[None] + 0.1 * jax.random.normal(ks[3], (DEPTH, C_HEADS), jnp.float32)
    gate_bias = jnp.concatenate([i_bias, f_bias], axis=-1)
    conv_w = jax.random.normal(ks[4], (DEPTH, CONV_K, 2 * C_WIDTH), jnp.float32) * CONV_K ** -0.5
    lam = 0.1 * jax.random.normal(ks[5], (DEPTH, 4, A_HALF), jnp.float32)
    subln_a = 1.0 + 0.02 * jax.random.normal(ks[6], (DEPTH, A_HEAD_DIM), jnp.float32)
    norm_c = 1.0 + 0.02 * jax.random.normal(ks[7], (DEPTH, C_HEAD_DIM), jnp.float32)
    rel_bias = 0.1 * jax.random.normal(ks[8], (DEPTH, B_HEADS, 2 * REL_CLIP + 1), jnp.float32)
    w_out = jax.random.normal(ks[9], (DEPTH, MIX_WIDTH, D_MODEL), jnp.float32) * (MIX_WIDTH ** -0.5) * DEEPNORM_BETA
    ln1_g = 1.0 + 0.02 * jax.random.normal(ks[10], (DEPTH, D_MODEL), jnp.float32)
    ln1_b = 0.02 * jax.random.normal(ks[11], (DEPTH, D_MODEL), jnp.float32)
    w_gate = jax.random.normal(ks[12], (DEPTH, D_MODEL, D_FF), jnp.float32) * D_MODEL ** -0.5
    w_up = jax.random.normal(ks[13], (DEPTH, D_MODEL, D_FF), jnp.float32) * D_MODEL ** -0.5
    w_down = jax.random.normal(ks[14], (DEPTH, D_FF, D_MODEL), jnp.float32) * (D_FF ** -0.5) * DEEPNORM_BETA
    ln2_g = 1.0 + 0.02 * jax.random.normal(ks[15], (DEPTH, D_MODEL), jnp.float32)
    ln2_b = 0.02 * jax.random.normal(ks[16], (DEPTH, D_MODEL), jnp.float32)
    return {"x": x, "w_in": w_in, "gate_bias": gate_bias, "conv_w": conv_w, "lam": lam,
            "subln_a": subln_a, "norm_c": norm_c, "rel_bias": rel_bias, "w_out": w_out,
            "ln1_g": ln1_g, "ln1_b": ln1_b, "w_gate": w_gate, "w_up": w_up, "w_down": w_down,
            "ln2_g": ln2_g, "ln2_b": ln2_b}


def reference(x, w_in, gate_bias, conv_w, lam, subln_a, norm_c, rel_bias, w_out,
              ln1_g, ln1_b, w_gate, w_up, w_down, ln2_g, ln2_b):
    for li in range(DEPTH):
        proj = x @ w_in[li]
        qa, ka, va, qb, kb, vb, qc, kc, vc, oc, ic, fc = jnp.split(proj, SPLIT_POINTS, axis=-1)
        ic = ic + gate_bias[li, :C_HEADS]
        fc = fc + gate_bias[li, C_HEADS:]
        ya = diff_attention(qa, ka, va, lam[li], subln_a[li], li)
        yb = chunk_band_attention(qb, kb, vb, rel_bias[li])
        yc = mlstm(qc, kc, vc, oc, ic, fc, conv_w[li], norm_c[li])
        mix = jnp.concatenate([ya, yb, yc], axis=-1) @ w_out[li]
        x = layer_norm(DEEPNORM_ALPHA * x + mix, ln1_g[li], ln1_b[li])
        ffn = (jax.nn.silu(x @ w_gate[li]) * (x @ w_up[li])) @ w_down[li]
        x = layer_norm(DEEPNORM_ALPHA * x + ffn, ln2_g[li], ln2_b[li])
    return x
```

```python
from contextlib import ExitStack
import numpy as np
import concourse.bass as bass
import concourse.mybir as mybir

F32 = mybir.dt.float32
BF16 = mybir.dt.bfloat16
AF = mybir.ActivationFunctionType
ALU = mybir.AluOpType
AX = mybir.AxisListType

ENGS = ("pe", "act", "dve", "pool", "sp")


class Op:
    __slots__ = ("eng", "fn", "deps", "signal", "dma_group", "dma_n", "idx", "sigcount", "inc")

    def __init__(self, eng, fn):
        self.eng = eng
        self.fn = fn
        self.deps = {}
        self.signal = False
        self.dma_group = None
        self.dma_n = 0
        self.idx = -1
        self.sigcount = 0
        self.inc = 16


class Sched:
    def __init__(self, nc):
        self.nc = nc
        self.streams = {e: [] for e in ENGS}
        self.last_write = {}
        self.readers = {}
        self.dma_counts = {}
        self.group_inc = {}
        self.seen = {e: {} for e in ENGS}

    def _add_dep(self, op, tok):
        if tok is None:
            return
        if tok[0] == "c":
            _, e, i = tok
            if e == op.eng and e != "pool":
                pass
            cur = op.deps.get(("c", e), -1)
            if i > cur:
                op.deps[("c", e)] = i
        else:
            _, g, n = tok
            cur = op.deps.get(("d", g), 0)
            if n > cur:
                op.deps[("d", g)] = n

    def add(self, eng, fn, reads=(), writes=(), dma_group=None, inc=16):
        op = Op(eng, fn)
        op.inc = inc
        if dma_group is not None:
            self.group_inc[dma_group] = inc
        op.idx = len(self.streams[eng])
        for k in reads:
            w = self.last_write.get(k)
            if w is not None:
                self._add_dep(op, w)
        for k in writes:
            w = self.last_write.get(k)
            if w is not None:
                if not (w[0] == "c" and w[1] == eng and dma_group is None):
                    self._add_dep(op, w)
            for r in self.readers.get(k, ()):
                if not (r[0] == "c" and r[1] == eng and dma_group is None):
                    self._add_dep(op, r)
        if dma_group is not None:
            n = self.dma_counts.get(dma_group, 0) + 1
            self.dma_counts[dma_group] = n
            op.dma_group = dma_group
            op.dma_n = n
            tok = ("d", dma_group, n)
        else:
            tok = ("c", eng, op.idx)
        for k in reads:
            self.readers.setdefault(k, []).append(tok)
        for k in writes:
            self.last_write[k] = tok
            self.readers[k] = []
        seen = self.seen[eng]
        for key in list(op.deps.keys()):
            v = op.deps[key]
            if key[0] == "c" and key[1] == eng and dma_group is None:
                pass
            if seen.get(key, -1 if key[0] == "c" else 0) >= v:
                del op.deps[key]
            else:
                seen[key] = v
        self.streams[eng].append(op)
        return op

    def barrier(self, only_active=True):
        toks = []
        for e in ENGS:
            for op in reversed(self.streams[e]):
                if op.fn is not None and op.dma_group is None:
                    toks.append(("c", e, op.idx))
                    break
        dtoks = [("d", g, n) for g, n in self.dma_counts.items()]
        for e in ENGS:
            if only_active and not self.streams[e]:
                continue
            op = Op(e, None)
            op.idx = len(self.streams[e])
            for t in toks:
                if t[1] != e:
                    self._add_dep(op, t)
            for t in dtoks:
                self._add_dep(op, t)
            seen = self.seen[e]
            for key in list(op.deps.keys()):
                v = op.deps[key]
                if seen.get(key, -1 if key[0] == "c" else 0) >= v:
                    del op.deps[key]
                else:
                    seen[key] = v
            self.streams[e].append(op)
        self.last_write.clear()
        self.readers.clear()

    def emit(self, final_waits=()):
        nc = self.nc
        for e in ENGS:
            for op in self.streams[e]:
                for key, v in op.deps.items():
                    if key[0] == "c":
                        self.streams[key[1]][v].signal = True
        for e in ENGS:
            c = 0
            for op in self.streams[e]:
                if op.signal:
                    c += 1
                op.sigcount = c
        with ExitStack() as es:
            sems = {}
            for e in ENGS:
                sems[("c", e)] = es.enter_context(nc.semaphore("s_" + e))
            for g in self.dma_counts:
                sems[("d", g)] = es.enter_context(nc.semaphore("d_" + str(g)))
            block = es.enter_context(nc.Block())
            streams = self.streams

            def run(e, engobj):
                for op in streams[e]:
                    for key, v in op.deps.items():
                        if key[0] == "c":
                            tgt = streams[key[1]][v]
                            assert tgt.signal
                            engobj.wait_ge(sems[key], tgt.sigcount)
                        else:
                            engobj.wait_ge(sems[key], self.group_inc[key[1]] * v)
                    if op.fn is None:
                        continue
                    ins = op.fn(engobj)
                    if op.dma_group is not None:
                        if op.inc == 16:
                            ins.then_inc(sems[("d", op.dma_group)], 16)
                        else:
                            ins.then_inc(sems[("d", op.dma_group)])
                    elif op.signal:
                        ins.then_inc(sems[("c", e)], 1)

            @block.tensor
            def _(eng):
                run("pe", eng)

            @block.scalar
            def _(eng):
                run("act", eng)

            @block.vector
            def _(eng):
                run("dve", eng)

            @block.gpsimd
            def _(eng):
                run("pool", eng)

            @block.sync
            def _(eng):
                run("sp", eng)

from concourse.bass_utils import run_bass_kernel_spmd
import ml_dtypes

D = 2048
SEQ = 2048
NB = 4
TOK = 1024
DFF = 5632
NFF = DFF // 128
DEPTH = 2
ALPHA = (2.0 * DEPTH) ** 0.25
LN_EPS = 1e-5
FF_SPLITS = [(0, 6), (6, 12), (12, 18), (18, 22)]


class PsumRot:
    def __init__(self, banks):
        self.banks = banks
        self.i = 0

    def get(self):
        b = self.banks[self.i % len(self.banks)]
        self.i += 1
        return b


def emit_layernorm(S, nc, res, outT, lnp, gi, bi, ones, sq, stats, tmpv, psab, ntok, tag):
    inv = 1.0 / D
    ntg = ntok // 512
    for tg in range(ntg):
        ts = slice(tg * 512, (tg + 1) * 512)
        mean, rstd = stats[tg]
        (psA, kA), (psB, kB) = psab[tg]
        km, kr = ("lnmean", tg), ("lnrstd", tg)
        for c in range(16):
            q = c % 2
            S.add("act", lambda e, c=c, q=q, ts=ts: e.activation(out=sq[q][:], in_=res[:, c, ts], func=AF.Square),
                  reads=[("res", c, tg)], writes=[("sq", q)])
            S.add("pe", lambda e, c=c, ts=ts, psA=psA: e.matmul(psA[:], lhsT=ones[:], rhs=res[:, c, ts],
                                                                 start=(c == 0), stop=(c == 15)),
                  reads=[("res", c, tg), "ones"], writes=[kA])
            S.add("pe", lambda e, c=c, q=q, psB=psB: e.matmul(psB[:], lhsT=ones[:], rhs=sq[q][:],
                                                               start=(c == 0), stop=(c == 15)),
                  reads=[("sq", q), "ones"], writes=[kB])
        S.add("dve", lambda e, mean=mean, psA=psA: e.tensor_scalar(out=mean[:], in0=psA[:], scalar1=inv, scalar2=None, op0=ALU.mult),
              reads=[kA], writes=[km])
        S.add("dve", lambda e, mean=mean: e.tensor_tensor(out=tmpv[:], in0=mean[:], in1=mean[:], op=ALU.mult),
              reads=[km], writes=["tmpv"])
        S.add("dve", lambda e, rstd=rstd, psB=psB: e.scalar_tensor_tensor(out=rstd[:], in0=psB[:], scalar=inv, in1=tmpv[:],
                                                                          op0=ALU.mult, op1=ALU.subtract),
              reads=[kB, "tmpv"], writes=[kr])
        S.add("dve", lambda e, rstd=rstd: e.tensor_scalar(out=rstd[:], in0=rstd[:], scalar1=LN_EPS, scalar2=None, op0=ALU.add),
              reads=[kr], writes=[kr])
        S.add("act", lambda e, rstd=rstd: e.activation(out=tmpv[:], in_=rstd[:], func=AF.Sqrt),
              reads=[kr], writes=["tmpv"])
        S.add("dve", lambda e, rstd=rstd: e.reciprocal(out=rstd[:], in_=tmpv[:]),
              reads=["tmpv"], writes=[kr])
    for tg in range(ntg):
        ts = slice(tg * 512, (tg + 1) * 512)
        mean, rstd = stats[tg]
        km, kr = ("lnmean", tg), ("lnrstd", tg)
        for c in range(16):
            S.add("dve", lambda e, c=c, ts=ts, mean=mean: e.tensor_tensor(out=res[:, c, ts], in0=res[:, c, ts], in1=mean[:],
                                                                          op=ALU.subtract),
                  reads=[("res", c, tg), km], writes=[("res", c, tg)])
            S.add("pool", lambda e, c=c, ts=ts, rstd=rstd: e.tensor_tensor(out=res[:, c, ts], in0=res[:, c, ts], in1=rstd[:],
                                                                           op=ALU.mult),
                  reads=[("res", c, tg), kr], writes=[("res", c, tg)])
            S.add("act", lambda e, c=c, ts=ts: e.activation(out=res[:, c, ts], in_=res[:, c, ts], func=AF.Identity,
                                                            scale=lnp[:, gi, c:c + 1], bias=lnp[:, bi, c:c + 1]),
                  reads=[("res", c, tg), "lnp"], writes=[("res", c, tg)])
            if outT is not None:
                S.add("act", lambda e, c=c, ts=ts: e.copy(out=outT[:, c, ts], in_=res[:, c, ts]),
                      reads=[("res", c, tg)], writes=[("xT", c, tg)])


def emit_phase_b(S, nc, es, io, ntok=TOK, ff_splits=None):
    ff_splits = ff_splits or FF_SPLITS
    NTG = ntok // 512
    res = es.enter_context(nc.sbuf_tensor("res", [128, 16, ntok], F32))
    x1T = es.enter_context(nc.sbuf_tensor("x1T", [128, 16, ntok], BF16))
    big = es.enter_context(nc.sbuf_tensor("big", [128, 16, ntok], BF16))
    wbuf = [es.enter_context(nc.sbuf_tensor(f"wbuf{i}", [128, 16, 256], BF16)) for i in range(6)]
    wd = [es.enter_context(nc.sbuf_tensor(f"wd{i}", [128, 12, 256], BF16)) for i in range(2)]
    ones = es.enter_context(nc.sbuf_tensor("ones", [128, 128], F32))
    lnp = es.enter_context(nc.sbuf_tensor("lnp_sb", [128, 4, 16], F32))
    sq = [es.enter_context(nc.sbuf_tensor(f"sq{i}", [128, 512], F32)) for i in range(2)]
    stats = [(es.enter_context(nc.sbuf_tensor(f"mean{i}", [128, 512], F32)),
              es.enter_context(nc.sbuf_tensor(f"rstd{i}", [128, 512], F32))) for i in range(NTG)]
    tmpv = es.enter_context(nc.sbuf_tensor("tmpv", [128, 512], F32))
    sg = [es.enter_context(nc.sbuf_tensor(f"sg{i}", [128, 512], F32)) for i in range(2)]
    pbanks = [es.enter_context(nc.psum_tensor(f"pb{i}", [128, 512], F32)) for i in range(8)]
    psab = [((pbanks[6], ("ps", 6)), (pbanks[7], ("ps", 7))), ((pbanks[4], ("ps", 4)), (pbanks[5], ("ps", 5)))][:NTG]
    rot = PsumRot(list(range(6)))

    S.add("pool", lambda e: e.memset(ones[:], 1.0), writes=["ones"])
    S.add("sp", lambda e: e.dma_start(out=lnp[:], in_=io["lnp"]), writes=["lnp"], dma_group="lnp")
    for c4 in range(4):
        S.add("sp", lambda e, c4=c4: e.dma_start(
            out=res[:, c4 * 4:(c4 + 1) * 4, :],
            in_=io["xres"].rearrange("(c p) t -> p c t", p=128)[:, c4 * 4:(c4 + 1) * 4, :]),
            writes=[("res", c, tg) for c in range(c4 * 4, c4 * 4 + 4) for tg in range(NTG)],
            dma_group=("xres", c4))
    if io.get("ygath") is None:
        S.add("sp", lambda e: e.dma_start(out=big[:], in_=io["yT"].rearrange("(c p) t -> p c t", p=128)),
              writes=[("big", c) for c in range(16)], dma_group="yT")
    else:
        selb = es.enter_context(nc.sbuf_tensor("selB", [128, 2], F32))
        S.add("sp", lambda e: e.dma_start(out=selb[:], in_=io["sel"]), writes=["selB"], dma_group="selB")
        yo = io["yown"].rearrange("(c p) t -> p c t", p=128)
        for hf in range(2):
            S.add("sp", lambda e, hf=hf: e.dma_start(out=big[:, hf * 8:hf * 8 + 8, :], in_=yo), reads=["yown"],
                  writes=[("big", c) for c in range(hf * 8, hf * 8 + 8)], dma_group=("yT", hf))
        S.add("sp", lambda e: e.dma_start(out=x1T[:], in_=io["ygath"].rearrange("(c p) t -> p c t", p=128)), reads=["ygath"],
              writes=[("xT", c, tg) for c in range(16) for tg in range(NTG)], dma_group="ygl")
        for hf in range(2):
            hs = slice(hf * 8, hf * 8 + 8)
            kb = [("big", c) for c in range(hf * 8, hf * 8 + 8)]
            kx = [("xT", c, tg) for c in range(hf * 8, hf * 8 + 8) for tg in range(NTG)]
            S.add("dve", lambda e, hs=hs, hf=hf: e.tensor_scalar(out=big[:, hs, :], in0=big[:, hs, :], scalar1=selb[:, hf:hf + 1], scalar2=None, op0=ALU.mult),
                  reads=kb + ["selB"], writes=kb)
            S.add("dve", lambda e, hs=hs, hf=hf: e.scalar_tensor_tensor(out=big[:, hs, :], in0=x1T[:, hs, :], scalar=selb[:, 1 - hf:2 - hf], in1=big[:, hs, :],
                                                                        op0=ALU.mult, op1=ALU.add),
                  reads=kb + kx + ["selB"], writes=kb)

    wctr = [0]

    def load_w(src_ap, tag):
        s = wctr[0] % 6
        wctr[0] += 1
        S.add("pool", lambda e, s=s: e.dma_start(out=wbuf[s][:], in_=src_ap),
              writes=[("wbuf", s)], dma_group=("wbuf", s))
        return s

    wo = io["w_out"].rearrange("(kc p) n -> p kc n", p=128)
    for cg in range(8):
        s = load_w(wo[:, :, cg * 256:(cg + 1) * 256], "wo")
        for cc in range(2):
            col = cg * 2 + cc
            for tg in range(NTG):
                ts = slice(tg * 512, (tg + 1) * 512)
                b = rot.get()
                for kc in range(16):
                    S.add("pe", lambda e, s=s, cc=cc, kc=kc, b=b, ts=ts: e.matmul(
                        pbanks[b][:], lhsT=wbuf[s][:, kc, cc * 128:(cc + 1) * 128], rhs=big[:, kc, ts],
                        start=(kc == 0), stop=(kc == 15)),
                        reads=[("wbuf", s), ("big", kc)], writes=[("ps", b)])
                S.add("dve", lambda e, col=col, b=b, ts=ts: e.scalar_tensor_tensor(
                    out=res[:, col, ts], in0=res[:, col, ts], scalar=ALPHA, in1=pbanks[b][:],
                    op0=ALU.mult, op1=ALU.add),
                    reads=[("ps", b), ("res", col, tg)], writes=[("res", col, tg)])
    emit_layernorm(S, nc, res, x1T, lnp, 0, 1, ones, sq, stats, tmpv, psab, ntok, "ln1")
    wg = io["w_gate"].rearrange("(kc p) n -> p kc n", p=128)
    wu = io["w_up"].rearrange("(kc p) n -> p kc n", p=128)
    sgc = [0]
    wdc = [0]
    for si, (sl0, sl1) in enumerate(ff_splits):
        nch = (sl1 - sl0) * 2
        for sl in range(sl0, sl1):
            sgt = load_w(wg[:, :, sl * 256:(sl + 1) * 256], "wg")
            sut = load_w(wu[:, :, sl * 256:(sl + 1) * 256], "wu")
            for cc in range(2):
                j = (sl - sl0) * 2 + cc
                for tg in range(NTG):
                    ts = slice(tg * 512, (tg + 1) * 512)
                    bg = rot.get()
                    bu = rot.get()
                    for kc in range(16):
                        S.add("pe", lambda e, s=sgt, cc=cc, kc=kc, b=bg, ts=ts: e.matmul(
                            pbanks[b][:], lhsT=wbuf[s][:, kc, cc * 128:(cc + 1) * 128], rhs=x1T[:, kc, ts],
                            start=(kc == 0), stop=(kc == 15)),
                            reads=[("wbuf", sgt), ("xT", kc, tg)], writes=[("ps", bg)])
                    for kc in range(16):
                        S.add("pe", lambda e, s=sut, cc=cc, kc=kc, b=bu, ts=ts: e.matmul(
                            pbanks[b][:], lhsT=wbuf[s][:, kc, cc * 128:(cc + 1) * 128], rhs=x1T[:, kc, ts],
                            start=(kc == 0), stop=(kc == 15)),
                            reads=[("wbuf", sut), ("xT", kc, tg)], writes=[("ps", bu)])
                    q = sgc[0] % 2
                    sgc[0] += 1
                    S.add("act", lambda e, q=q, b=bg: e.activation(out=sg[q][:], in_=pbanks[b][:], func=AF.Silu),
                          reads=[("ps", bg)], writes=[("sg", q)])
                    S.add("dve", lambda e, q=q, b=bu, j=j, ts=ts: e.tensor_tensor(
                        out=big[:, j, ts], in0=sg[q][:], in1=pbanks[b][:], op=ALU.mult),
                        reads=[("sg", q), ("ps", bu)], writes=[("big", j)])
        r0 = sl0 * 256
        for cp in range(8):
            s = wdc[0] % 2
            wdc[0] += 1
            src = io["w_down"][r0:r0 + nch * 128, cp * 256:(cp + 1) * 256].rearrange("(j p) n -> p j n", p=128)
            S.add("pool", lambda e, s=s, src=src, nch=nch: e.dma_start(out=wd[s][:, 0:nch, :], in_=src),
                  writes=[("wd", s)], dma_group=("wd", s))
            for c2 in range(2):
                col = cp * 2 + c2
                for tg in range(NTG):
                    ts = slice(tg * 512, (tg + 1) * 512)
                    b = rot.get()
                    for j in range(nch):
                        S.add("pe", lambda e, s=s, c2=c2, j=j, b=b, ts=ts, nch=nch: e.matmul(
                            pbanks[b][:], lhsT=wd[s][:, j, c2 * 128:(c2 + 1) * 128], rhs=big[:, j, ts],
                            start=(j == 0), stop=(j == nch - 1)),
                            reads=[("wd", s), ("big", j)], writes=[("ps", b)])
                    if si == 0:
                        S.add("dve", lambda e, col=col, b=b, ts=ts: e.scalar_tensor_tensor(
                            out=res[:, col, ts], in0=res[:, col, ts], scalar=ALPHA, in1=pbanks[b][:],
                            op0=ALU.mult, op1=ALU.add),
                            reads=[("ps", b), ("res", col, tg)], writes=[("res", col, tg)])
                    else:
                        S.add("dve", lambda e, col=col, b=b, ts=ts: e.tensor_tensor(
                            out=res[:, col, ts], in0=res[:, col, ts], in1=pbanks[b][:], op=ALU.add),
                            reads=[("ps", b), ("res", col, tg)], writes=[("res", col, tg)])
    xsend = io.get("xsend")
    emit_layernorm(S, nc, res, x1T if xsend is not None else None, lnp, 2, 3, ones, sq, stats, tmpv, psab, ntok, "ln2")
    for c4 in range(4):
        S.add("sp", lambda e, c4=c4: e.dma_start(
            out=io["xo"].rearrange("(c p) t -> p c t", p=128)[:, c4 * 4:(c4 + 1) * 4, :],
            in_=res[:, c4 * 4:(c4 + 1) * 4, :]),
            reads=[("res", c, tg) for c in range(c4 * 4, c4 * 4 + 4) for tg in range(NTG)],
            writes=[io.get("xo_key", "xo_out")], dma_group=("xo", c4))
    if xsend is not None:
        for fh in range(2):
            S.add("sp", lambda e, fh=fh: e.dma_start(out=xsend[fh].rearrange("(c p) t -> p c t", p=128), in_=x1T[:, fh * 8:fh * 8 + 8, :]),
                  reads=[("xT", c, tg) for c in range(fh * 8, fh * 8 + 8) for tg in range(NTG)], writes=["xsend"], dma_group=("xsend", fh))


def build_phase_b(ntok=TOK, ff_splits=None, dff=DFF):
    nc = bass.Bass("TRN2", target_bir_lowering=False)
    io = {
        "yT": nc.dram_tensor("yT", [D, ntok], BF16, kind="ExternalInput").ap(),
        "xres": nc.dram_tensor("xres", [D, ntok], F32, kind="ExternalInput").ap(),
        "w_out": nc.dram_tensor("w_out", [D, D], F32, kind="ExternalInput").ap(),
        "w_gate": nc.dram_tensor("w_gate", [D, dff], F32, kind="ExternalInput").ap(),
        "w_up": nc.dram_tensor("w_up", [D, dff], F32, kind="ExternalInput").ap(),
        "w_down": nc.dram_tensor("w_down", [dff, D], F32, kind="ExternalInput").ap(),
        "lnp": nc.dram_tensor("lnp", [128, 4, 16], F32, kind="ExternalInput").ap(),
        "xo": nc.dram_tensor("xo", [D, ntok], F32, kind="ExternalOutput").ap(),
    }
    S = Sched(nc)
    with ExitStack() as es:
        emit_phase_b(S, nc, es, io, ntok, ff_splits)
        S.barrier()
        S.emit()
    return nc


def lnp_pack(g1, b1, g2, b2):
    return np.ascontiguousarray(np.stack([v.reshape(16, 128).T for v in (g1, b1, g2, b2)], axis=1)).astype(np.float32)


A_HEADS_PER = 2
B_HEADS_PER = 3
C_HEADS_PER = 2
CH = 192
QA, KA, VA_, QB, KB, VB_, QC, KC, VC_, OC, IC, FC = 0, 512, 1024, 1536, 2304, 3072, 3840, 4608, 5376, 6144, 6912, 6916
QK_BLOCKS = (["Aq0", "Ak0", "Aq1", "Ak1", "Bq0", "Bk0", "Bq1", "Bk1", "Bq2", "Bk2"]
             + ["Cq0a", "Cq1a", "Cqb", "Ck0a", "Ck1a", "Ckb"])
NQKB = len(QK_BLOCKS)
V_SEGS = [("Av", 0, 256), ("Bv01", 256, 512), ("Bv2", 512, 640), ("Cv0", 640, 832), ("Cv1", 832, 1024),
          ("Coa", 1024, 1280), ("Cob", 1280, 1536)]
NV = 1536
SLOPES = [2.0 ** (-8.0 * (h + 1) / 4) for h in range(4)]
NEGM = -30000.0


def lam_init_of(li):
    import math
    return 0.8 - 0.6 * math.exp(-0.3 * li)


def emit_phase_a(S, nc, es, io, li):
    lam_init = lam_init_of(li)
    qk = {}
    for name in QK_BLOCKS:
        qk[name] = es.enter_context(nc.sbuf_tensor("qk_" + name, [128, SEQ], BF16))
    vA = es.enter_context(nc.sbuf_tensor("vA", [128, 16, 2, 129], BF16))
    vB = es.enter_context(nc.sbuf_tensor("vB", [128, 16, 3, 129], BF16))
    vC = es.enter_context(nc.sbuf_tensor("vC", [128, 16, 2, 193], BF16))
    og = es.enter_context(nc.sbuf_tensor("og", [128, 16, 384], BF16))
    graw = es.enter_context(nc.sbuf_tensor("graw", [128, 16, 4], F32))
    spm = es.enter_context(nc.sbuf_tensor("spm_sb", [128, 740], F32))
    ident = es.enter_context(nc.sbuf_tensor("ident_sb", [128, 128], BF16))
    tri = es.enter_context(nc.sbuf_tensor("tri_sb", [128, 128], F32))
    ones = es.enter_context(nc.sbuf_tensor("onesA", [128, 128], F32))
    GB0, CW0, SL0, NC0, LM0 = 0, 4, 36, 164, 356

    S.add("sp", lambda e: e.dma_start(out=spm[:, 0:612], in_=io["spm"]), writes=["spm"], dma_group="spm")
    S.add("pool", lambda e: e.dma_start(out=ident[:], in_=io["ident"]), writes=["ident"], dma_group="cst")
    S.add("sp", lambda e: e.dma_start(out=tri[:], in_=io["tri"]), writes=["tri"], dma_group="cst2")
    S.add("pool", lambda e: e.memset(ones[:], 1.0), writes=["onesA"])
    S.add("pool", lambda e: e.memset(vA[:, :, :, 128:129], 1.0), writes=["vA1"])
    S.add("pool", lambda e: e.memset(vB[:, :, :, 128:129], 1.0), writes=["vB1"])
    S.add("pool", lambda e: e.memset(vC[:, :, :, 192:193], 1.0), writes=["vC1"])

    with ExitStack() as ps_:
        xT = ps_.enter_context(nc.sbuf_tensor("xTa", [128, 16, SEQ], BF16))
        wst = [ps_.enter_context(nc.sbuf_tensor(f"wst{i}", [128, 16, 256], BF16)) for i in range(2)]
        craw = ps_.enter_context(nc.sbuf_tensor("craw", [128, SEQ + 4], F32))
        cacc = ps_.enter_context(nc.sbuf_tensor("cacc", [128, 512], F32))
        pb = [ps_.enter_context(nc.psum_tensor(f"pa{i}", [128, 512], F32)) for i in range(8)]
        rot = PsumRot(list(range(8)))
        if io.get("xgath") is None:
            xsrc = io["xT"].rearrange("(c p) t -> p c t", p=128)
            for c4 in range(8):
                S.add("pool", lambda e, c4=c4: e.dma_start(out=xT[:, c4 * 2:c4 * 2 + 2, :], in_=xsrc[:, c4 * 2:c4 * 2 + 2, :]),
                      writes=[("xT", c4 * 2), ("xT", c4 * 2 + 1)], dma_group=("xTl", c4))
        else:
            for fh in range(2):
                xg = io["xgath"][fh].rearrange("(r c p) t -> p r c t", r=2, p=128)
                for c4 in range(2):
                    for rr in range(2):
                        cc0 = fh * 8 + c4 * 4
                        S.add("sp", lambda e, c4=c4, rr=rr, cc0=cc0, xg=xg: e.dma_start(
                            out=xT[:, cc0:cc0 + 4, rr * 1024:(rr + 1) * 1024], in_=xg[:, rr, c4 * 4:c4 * 4 + 4, :]),
                            reads=["xgath"], writes=[("xT", cc0 + i) for i in range(4)], dma_group=("xTl", fh * 4 + c4 * 2 + rr))
        S.add("pool", lambda e: e.memset(craw[:, 0:4], 0.0), writes=["craw_pad"])
        wctr = [0]

        def load_w(src_ap, ncols):
            s = wctr[0] % 2
            wctr[0] += 1
            S.add("pool", lambda e, s=s, src_ap=src_ap, ncols=ncols: e.dma_start(out=wst[s][:, :, 0:ncols], in_=src_ap),
                  writes=[("wst", s)], dma_group=("wst", s))
            return s

        evc = [0]
        wqk = io["wqk"].rearrange("(kc p) n -> p kc n", p=128)
        import os as _os
        _stop = int(_os.environ.get("PA_STOP", "9"))
        for sl in range(NQKB // 2 if _stop >= 1 else 0):
            s = load_w(wqk[:, :, sl * 256:(sl + 1) * 256], 256)
            for bi2 in range(2):
                bidx = sl * 2 + bi2
                name = QK_BLOCKS[bidx]
                isC = name[0] == "C"
                for tg in range(4):
                    ts = slice(tg * 512, (tg + 1) * 512)
                    b = rot.get()
                    for kc in range(16):
                        S.add("pe", lambda e, s=s, bi2=bi2, kc=kc, b=b, ts=ts: e.matmul(
                            pb[b][:], lhsT=wst[s][:, kc, bi2 * 128:(bi2 + 1) * 128], rhs=xT[:, kc, ts],
                            start=(kc == 0), stop=(kc == 15)),
                            reads=[("wst", s), ("xT", kc)], writes=[("ps", b)])
                    if isC:
                        cts = slice(4 + tg * 512, 4 + (tg + 1) * 512)
                        eng = "act" if evc[0] % 2 == 0 else "dve"
                        evc[0] += 1
                        if eng == "act":
                            S.add("act", lambda e, b=b, cts=cts: e.copy(out=craw[:, cts], in_=pb[b][:]),
                                  reads=[("ps", b)], writes=[("craw", tg)])
                        else:
                            S.add("dve", lambda e, b=b, cts=cts: e.tensor_copy(out=craw[:, cts], in_=pb[b][:]),
                                  reads=[("ps", b)], writes=[("craw", tg)])
                    else:
                        if name[1] == "q":
                            scl = 0.125 if name[0] == "A" else 128.0 ** -0.5
                            S.add("act", lambda e, b=b, ts=ts, name=name, scl=scl: e.activation(
                                out=qk[name][:, ts], in_=pb[b][:], func=AF.Identity, scale=scl),
                                reads=[("ps", b)], writes=[("qk", name)])
                        else:
                            S.add("dve", lambda e, b=b, ts=ts, name=name: e.tensor_copy(out=qk[name][:, ts], in_=pb[b][:]),
                                  reads=[("ps", b)], writes=[("qk", name)])
                    if isC:
                        cb = QK_BLOCKS.index(name) - 10
                        crd = [("craw", tg), "craw_pad", "spm"] + ([("craw", tg - 1)] if tg > 0 else [])
                        o0 = tg * 512
                        S.add("dve", lambda e, cb=cb, o0=o0: e.tensor_scalar(
                            out=cacc[:], in0=craw[:, 1 + o0:1 + o0 + 512], scalar1=spm[:, CW0 + cb * 4:CW0 + cb * 4 + 1], scalar2=None,
                            op0=ALU.mult), reads=crd, writes=["cacc"])
                        for j in range(1, 4):
                            S.add("dve", lambda e, cb=cb, j=j, o0=o0: e.scalar_tensor_tensor(
                                out=cacc[:], in0=craw[:, 1 + j + o0:1 + j + o0 + 512], scalar=spm[:, CW0 + cb * 4 + j:CW0 + cb * 4 + j + 1],
                                in1=cacc[:], op0=ALU.mult, op1=ALU.add), reads=crd + ["cacc"], writes=["cacc"])
                        S.add("act", lambda e, name=name, ts=ts: e.activation(out=qk[name][:, ts], in_=cacc[:], func=AF.Silu),
                              reads=["cacc"], writes=[("qk", name)])
        wv = io["wv"].rearrange("(kc p) n -> p kc n", p=128)
        for (sname, c0, c1) in ([V_SEGS[int(i)] for i in _os.environ.get('PA_SEGS', '0123456')] if _stop >= 2 else []):
            n = c1 - c0
            s = load_w(wv[:, :, c0:c1], n)
            for tb in range(16):
                tsl = slice(tb * 128, (tb + 1) * 128)
                b = rot.get()
                for kc in range(16):
                    S.add("pe", lambda e, s=s, kc=kc, b=b, tsl=tsl, n=n: e.matmul(
                        pb[b][:, 0:n], lhsT=xT[:, kc, tsl], rhs=wst[s][:, kc, 0:n],
                        start=(kc == 0), stop=(kc == 15)),
                        reads=[("wst", s), ("xT", kc)], writes=[("ps", b)])
                rd = [("ps", b)]
                if sname == "Av":
                    S.add("dve", lambda e, b=b, tb=tb: e.tensor_copy(
                        out=vA[:, tb, :, 0:128], in_=pb[b][:, 0:256].rearrange("p (h d) -> p h d", h=2)),
                        reads=rd, writes=[("vA", tb)])
                elif sname == "Bv01":
                    S.add("act", lambda e, b=b, tb=tb: e.copy(
                        out=vB[:, tb, 0:2, 0:128], in_=pb[b][:, 0:256].rearrange("p (h d) -> p h d", h=2)),
                        reads=rd, writes=[("vB", tb)])
                elif sname == "Bv2":
                    S.add("dve", lambda e, b=b, tb=tb: e.tensor_copy(out=vB[:, tb, 2, 0:128], in_=pb[b][:, 0:128]),
                          reads=rd, writes=[("vB", tb)])
                elif sname in ("Cv0", "Cv1"):
                    h = int(sname[2])
                    S.add("act", lambda e, b=b, tb=tb, h=h: e.copy(out=vC[:, tb, h, 0:192], in_=pb[b][:, 0:192]),
                          reads=rd, writes=[("vC", tb)])
                elif sname == "Coa":
                    S.add("act", lambda e, b=b, tb=tb: e.activation(out=og[:, tb, 0:256], in_=pb[b][:, 0:256], func=AF.Sigmoid),
                          reads=rd, writes=[("og", tb)])
                else:
                    S.add("act", lambda e, b=b, tb=tb: e.activation(out=og[:, tb, 256:384], in_=pb[b][:, 0:128], func=AF.Sigmoid),
                          reads=rd, writes=[("og", tb), ("psrd", b)])
                    S.add("act", lambda e, b=b, tb=tb: e.copy(out=graw[:, tb, :], in_=pb[b][:, 128:132]),
                          reads=rd + [("psrd", b)], writes=["graw"])
        S.barrier()
    yT = es.enter_context(nc.sbuf_tensor("yTa", [128, 8, SEQ], BF16))
    with ExitStack() as ms:
        T = dict(qk=qk, vA=vA, vB=vB, vC=vC, og=og, graw=graw, spm=spm, ident=ident, tri=tri, ones=ones, yT=yT,
                 GB0=GB0, SL0=SL0, NC0=NC0, LM0=LM0, lam_init=lam_init)
        T["sc"] = [ms.enter_context(nc.psum_tensor(f"sc{i}", [128, 1024], F32)) for i in range(2)]
        T["acc"] = [ms.enter_context(nc.psum_tensor(f"acc{i}", [128, 512], F32)) for i in range(2)]
        T["tr"] = ms.enter_context(nc.psum_tensor("trp", [128, 1024], BF16))
        T["misc"] = ms.enter_context(nc.psum_tensor("miscp", [128, 512], F32))
        T["ytok"] = ms.enter_context(nc.sbuf_tensor("ytok", [128, 16, 128], BF16))
        T["et"] = [ms.enter_context(nc.sbuf_tensor(f"et{i}", [128, 1024], BF16)) for i in range(2)]
        T["sm"] = ms.enter_context(nc.sbuf_tensor("smallA", [128, 64], F32))
        import os as _os
        _mix = _os.environ.get("PA_MIX", "abc")
        if "a" in _mix:
            with ExitStack() as sa:
                emit_mixer_a(S, nc, sa, io, T)
                S.barrier()
        if "b" in _mix:
            with ExitStack() as sb:
                emit_mixer_b(S, nc, sb, io, T)
                S.barrier()
        if "c" in _mix:
            with ExitStack() as sc_:
                emit_mixer_c(S, nc, sc_, io, T)
                S.barrier()
        if io.get("ysend") is None:
            for c in range(8):
                S.add("sp", lambda e, c=c: e.dma_start(out=io["yT"][c * 128:(c + 1) * 128, :], in_=yT[:, c, :]),
                      reads=[("yT", c)], dma_group=("yo", c % 4))
        else:
            sel = ms.enter_context(nc.sbuf_tensor("selA", [128, 2], F32))
            tO = ms.enter_context(nc.sbuf_tensor("tOwn", [128, 8, 1024], BF16))
            tP = ms.enter_context(nc.sbuf_tensor("tPart", [128, 8, 1024], BF16))
            S.add("sp", lambda e: e.dma_start(out=sel[:], in_=io["sel"]), writes=["selA"], dma_group="selA")
            ry = [("yT", c) for c in range(8)] + ["selA"]
            S.add("dve", lambda e: e.tensor_scalar(out=tO[:], in0=yT[:, :, 0:1024], scalar1=sel[:, 0:1], scalar2=None, op0=ALU.mult),
                  reads=ry, writes=["tOwn"])
            S.add("dve", lambda e: e.scalar_tensor_tensor(out=tO[:], in0=yT[:, :, 1024:2048], scalar=sel[:, 1:2], in1=tO[:], op0=ALU.mult, op1=ALU.add),
                  reads=ry + ["tOwn"], writes=["tOwn"])
            S.add("dve", lambda e: e.tensor_scalar(out=tP[:], in0=yT[:, :, 0:1024], scalar1=sel[:, 1:2], scalar2=None, op0=ALU.mult),
                  reads=ry, writes=["tPart"])
            S.add("dve", lambda e: e.scalar_tensor_tensor(out=tP[:], in0=yT[:, :, 1024:2048], scalar=sel[:, 0:1], in1=tP[:], op0=ALU.mult, op1=ALU.add),
                  reads=ry + ["tPart"], writes=["tPart"])
            S.add("sp", lambda e: e.dma_start(out=io["yown"].rearrange("(c p) t -> p c t", p=128), in_=tO[:]),
                  reads=["tOwn"], writes=["yown"], dma_group="yown")
            S.add("sp", lambda e: e.dma_start(out=io["ysend"].rearrange("(c p) t -> p c t", p=128), in_=tP[:]),
                  reads=["tPart"], writes=["ysend"], dma_group="ysend")
        S.barrier()


def emit_transposes_to_yT(S, T, src, nblk_feat, chunk0, tag):
    tr, ident, yT = T["tr"], T["ident"], T["yT"]
    for f in range(nblk_feat):
        for half in range(2):
            for t8 in range(8):
                tb = half * 8 + t8
                S.add("pe", lambda e, f=f, tb=tb, t8=t8: e.transpose(
                    out=tr[:, t8 * 128:(t8 + 1) * 128], in_=src[:, tb, f * 128:(f + 1) * 128], identity=ident[:]),
                    reads=[(tag, tb), "ident"], writes=["tr"])
            eng = "dve" if (f + half) % 2 == 0 else "act"
            if eng == "dve":
                S.add("dve", lambda e, f=f, half=half: e.tensor_copy(
                    out=yT[:, chunk0 + f, half * 1024:(half + 1) * 1024], in_=tr[:]),
                    reads=["tr"], writes=[("yT", chunk0 + f)])
            else:
                S.add("act", lambda e, f=f, half=half: e.copy(
                    out=yT[:, chunk0 + f, half * 1024:(half + 1) * 1024], in_=tr[:]),
                    reads=["tr"], writes=[("yT", chunk0 + f)])


def emit_mixer_a(S, nc, sa, io, T):
    qk, vA, spm, ident, sm = T["qk"], T["vA"], T["spm"], T["ident"], T["sm"]
    sc, acc, et, ytok = T["sc"], T["acc"], T["et"], T["ytok"]
    SL0, LM0, lam_init = T["SL0"], T["LM0"], T["lam_init"]
    kaug = sa.enter_context(nc.sbuf_tensor("kaug_sb", [4, 2, SEQ], BF16))
    qaug = sa.enter_context(nc.sbuf_tensor("qaug_sb", [4, 2, SEQ], BF16))
    diag = sa.enter_context(nc.sbuf_tensor("diagA", [128, 2, 128], BF16))
    accA = sa.enter_context(nc.sbuf_tensor("accA", [128, 16, 2, 129], F32))
    rD = sa.enter_context(nc.sbuf_tensor("rDA", [128, 16, 2], F32))
    ss = sa.enter_context(nc.sbuf_tensor("ssA", [128, 16], F32))
    lt = sa.enter_context(nc.sbuf_tensor("ltA", [128, 64], F32))
    S.add("pool", lambda e: e.dma_start(out=kaug[:], in_=io["kaug"]), writes=["kaug"], dma_group="augk")
    S.add("pool", lambda e: e.dma_start(out=qaug[:], in_=io["qaug"]), writes=["qaug"], dma_group="augq")
    S.add("pool", lambda e: e.dma_start(out=diag[:], in_=io["diag"]), writes=["diag"], dma_group="augd")
    for i in range(2):
        S.add("dve", lambda e, i=i: e.tensor_tensor(out=lt[:], in0=spm[:, LM0 + 128 * i:LM0 + 128 * i + 64],
                                                    in1=spm[:, LM0 + 128 * i + 64:LM0 + 128 * i + 128], op=ALU.mult),
              reads=["spm"], writes=["ltA"])
        S.add("dve", lambda e, i=i: e.reduce_sum(out=sm[:, 1 + i:2 + i], in_=lt[:], axis=AX.X),
              reads=["ltA"], writes=[("sm", 1 + i)])
    S.add("act", lambda e: e.activation(out=sm[:, 1:3], in_=sm[:, 1:3], func=AF.Exp),
          reads=[("sm", 1), ("sm", 2)], writes=[("sm", 1), ("sm", 2)])
    S.add("dve", lambda e: e.tensor_tensor(out=sm[:, 0:1], in0=sm[:, 1:2], in1=sm[:, 2:3], op=ALU.subtract),
          reads=[("sm", 1), ("sm", 2)], writes=[("sm", 0)])
    S.add("dve", lambda e: e.tensor_scalar(out=sm[:, 0:1], in0=sm[:, 0:1], scalar1=lam_init, scalar2=None, op0=ALU.add),
          reads=[("sm", 0)], writes=[("sm", 0)])
    def mk_scores(hl, qb, m, g0, g1, w):
        qT, kT = qk["Aq%d" % hl], qk["Ak%d" % hl]
        qs = slice(qb * 128, (qb + 1) * 128)
        pp = slice(m * 64, (m + 1) * 64)

        def fn():
            for kb in range(g0, g1):
                cs = slice((kb - g0) * 128, (kb - g0 + 1) * 128)
                ks = slice(kb * 128, (kb + 1) * 128)
                S.add("pe", lambda e, cs=cs, ks=ks: e.matmul(sc[w][:, cs], lhsT=kT[pp, ks], rhs=qT[pp, qs], start=True, stop=False),
                      reads=[("qk", "Aq%d" % hl), ("qk", "Ak%d" % hl)], writes=[("sc", w)])
                if kb < qb:
                    S.add("pe", lambda e, cs=cs, ks=ks: e.matmul(sc[w][:, cs], lhsT=kaug[:, hl, ks], rhs=qaug[:, hl, qs], start=False, stop=True),
                          reads=["kaug", "qaug"], writes=[("sc", w)])
                else:
                    S.add("pe", lambda e, cs=cs: e.matmul(sc[w][:, cs], lhsT=ident[:], rhs=diag[:, hl, :], start=False, stop=True),
                          reads=["ident", "diag"], writes=[("sc", w)])
            ncol = (g1 - g0) * 128
            S.add("act", lambda e: e.activation(out=et[w][:, 0:ncol], in_=sc[w][:, 0:ncol], func=AF.Exp),
                  reads=[("sc", w)], writes=[("et", w)])
        return fn

    def mk_pv(hl, qb, m, g0, g1, w):
        def fn():
            for kb in range(g0, g1):
                cs = slice((kb - g0) * 128, (kb - g0 + 1) * 128)
                S.add("pe", lambda e, cs=cs, kb=kb: e.matmul(acc[m][:, 0:129], lhsT=et[w][:, cs], rhs=vA[:, kb, hl, :],
                                                              start=(kb == 0), stop=(kb == qb)),
                      reads=[("et", w), ("vA", kb), "vA1"], writes=[("acc", m)])
            if g1 == qb + 1:
                if m == 0:
                    S.add("act", lambda e: e.copy(out=accA[:, qb, 0, :], in_=acc[0][:, 0:129]),
                          reads=[("acc", 0)], writes=[("accA", qb)])
                else:
                    S.add("dve", lambda e: e.tensor_copy(out=accA[:, qb, 1, :], in_=acc[1][:, 0:129]),
                          reads=[("acc", 1)], writes=[("accA", qb)])
        return fn

    def epilogue(hl):
        allq = [("accA", qb) for qb in range(16)]
        bc = lambda ap: ap.to_broadcast([128, 16, 128])
        S.add("dve", lambda e: e.reciprocal(out=rD[:], in_=accA[:, :, :, 128]), reads=allq, writes=["rD"])
        S.add("dve", lambda e: e.tensor_scalar(out=rD[:, :, 1], in0=rD[:, :, 1], scalar1=sm[:, 0:1], scalar2=-1.0, op0=ALU.mult, op1=ALU.mult),
              reads=["rD", ("sm", 0)], writes=["rD"])
        S.add("dve", lambda e: e.tensor_tensor(out=accA[:, :, 0, 0:128], in0=accA[:, :, 0, 0:128], in1=bc(rD[:, :, 0:1]), op=ALU.mult),
              reads=allq + ["rD"], writes=allq)
        S.add("dve", lambda e: e.tensor_tensor(out=accA[:, :, 1, 0:128], in0=accA[:, :, 1, 0:128], in1=bc(rD[:, :, 1:2]), op=ALU.mult),
              reads=allq + ["rD"], writes=allq)
        S.add("dve", lambda e: e.tensor_tensor(out=accA[:, :, 0, 0:128], in0=accA[:, :, 0, 0:128], in1=accA[:, :, 1, 0:128], op=ALU.add),
              reads=allq, writes=allq)
        S.add("pool", lambda e: e.tensor_tensor(out=accA[:, :, 1, 0:128], in0=accA[:, :, 0, 0:128], in1=accA[:, :, 0, 0:128], op=ALU.mult),
              reads=allq, writes=allq)
        S.add("dve", lambda e: e.reduce_sum(out=ss[:], in_=accA[:, :, 1, 0:128], axis=AX.X), reads=allq, writes=["ssA"])
        S.add("dve", lambda e: e.tensor_scalar(out=ss[:], in0=ss[:], scalar1=1.0 / 128, scalar2=1e-6, op0=ALU.mult, op1=ALU.add),
              reads=["ssA"], writes=["ssA"])
        S.add("act", lambda e: e.activation(out=ss[:], in_=ss[:], func=AF.Sqrt), reads=["ssA"], writes=["ssA"])
        S.add("dve", lambda e: e.reciprocal(out=ss[:], in_=ss[:]), reads=["ssA"], writes=["ssA"])
        S.add("dve", lambda e: e.tensor_scalar(out=ss[:], in0=ss[:], scalar1=1.0 - lam_init, scalar2=None, op0=ALU.mult),
              reads=["ssA"], writes=["ssA"])
        S.add("dve", lambda e: e.tensor_tensor(out=accA[:, :, 0, 0:128], in0=accA[:, :, 0, 0:128], in1=bc(ss[:].unsqueeze(2)), op=ALU.mult),
              reads=allq + ["ssA"], writes=allq)
        S.add("dve", lambda e: e.tensor_tensor(out=ytok[:], in0=accA[:, :, 0, 0:128], in1=bc(spm[:, SL0:SL0 + 128].unsqueeze(1)), op=ALU.mult),
              reads=allq + ["spm"], writes=[("ytok", qb) for qb in range(16)])
        emit_transposes_to_yT(S, T, ytok, 1, hl, "ytok")

    steps = []
    for hl in range(2):
        for qb in range(16):
            for m in range(2):
                for g0 in range(0, qb + 1, 8):
                    g1 = min(qb + 1, g0 + 8)
                    w = len(steps) % 2
                    last = (qb == 15 and m == 1 and g1 == qb + 1)
                    steps.append((mk_scores(hl, qb, m, g0, g1, w), mk_pv(hl, qb, m, g0, g1, w), hl if last else None))
    for i in range(len(steps) + 1):
        if i < len(steps):
            steps[i][0]()
        if i >= 1:
            steps[i - 1][1]()
            if steps[i - 1][2] is not None:
                epilogue(steps[i - 1][2])


def emit_mixer_b(S, nc, sb, io, T):
    qk, vB, ident, sm = T["qk"], T["vB"], T["ident"], T["sm"]
    sc, acc, et, ytok = T["sc"], T["acc"], T["et"], T["ytok"]
    tf = sb.enter_context(nc.sbuf_tensor("toepf", [128, 15, 128], F32))
    thi = sb.enter_context(nc.sbuf_tensor("toephi", [128, 15, 128], BF16))
    tlo = sb.enter_context(nc.sbuf_tensor("toeplo", [128, 15, 128], BF16))
    tf2 = sb.enter_context(nc.sbuf_tensor("toepf2", [128, 15, 128], F32))
    accB = sb.enter_context(nc.sbuf_tensor("accB", [128, 16, 129], F32))
    rB = sb.enter_context(nc.sbuf_tensor("rBB", [128, 16], F32))
    S.add("sp", lambda e: e.dma_start(out=tf[:], in_=io["toep"]), writes=["toepf"], dma_group="toep")
    S.add("dve", lambda e: e.tensor_copy(out=thi[:], in_=tf[:]), reads=["toepf"], writes=["thi"])
    S.add("dve", lambda e: e.tensor_copy(out=tf2[:], in_=thi[:]), reads=["thi"], writes=["toepf2"])
    S.add("dve", lambda e: e.tensor_tensor(out=tf2[:], in0=tf[:], in1=tf2[:], op=ALU.subtract),
          reads=["toepf", "toepf2"], writes=["toepf2"])
    S.add("dve", lambda e: e.tensor_copy(out=tlo[:], in_=tf2[:]), reads=["toepf2"], writes=["tlo"])
    def mk_scores(hl, qb, w):
        qT, kT = qk["Bq%d" % hl], qk["Bk%d" % hl]
        qs = slice(qb * 128, (qb + 1) * 128)
        js = [j for j in range(5) if qb - 4 + j >= 0]

        def fn():
            for j in js:
                kb = qb - 4 + j
                cs = slice(j * 128, (j + 1) * 128)
                ks = slice(kb * 128, (kb + 1) * 128)
                S.add("pe", lambda e, cs=cs, ks=ks: e.matmul(sc[w][:, cs], lhsT=kT[:, ks], rhs=qT[:, qs], start=True, stop=False),
                      reads=[("qk", "Bq%d" % hl), ("qk", "Bk%d" % hl)], writes=[("sc", w)])
                S.add("pe", lambda e, cs=cs, j=j: e.matmul(sc[w][:, cs], lhsT=ident[:], rhs=thi[:, hl * 5 + j, :], start=False, stop=False),
                      reads=["ident", "thi"], writes=[("sc", w)])
                S.add("pe", lambda e, cs=cs, j=j: e.matmul(sc[w][:, cs], lhsT=ident[:], rhs=tlo[:, hl * 5 + j, :], start=False, stop=True),
                      reads=["ident", "tlo"], writes=[("sc", w)])
            c0, c1 = js[0] * 128, 640
            S.add("act", lambda e: e.activation(out=et[w][:, c0:c1], in_=sc[w][:, c0:c1], func=AF.Exp),
                  reads=[("sc", w)], writes=[("et", w)])
        return fn

    def mk_pv(hl, qb, w):
        js = [j for j in range(5) if qb - 4 + j >= 0]

        def fn():
            for j in js:
                kb = qb - 4 + j
                cs = slice(j * 128, (j + 1) * 128)
                S.add("pe", lambda e, cs=cs, kb=kb, j=j: e.matmul(acc[w][:, 0:129], lhsT=et[w][:, cs], rhs=vB[:, kb, hl, :],
                                                                   start=(j == js[0]), stop=(j == 4)),
                      reads=[("et", w), ("vB", kb), "vB1"], writes=[("acc", w)])
            if qb % 2 == 0:
                S.add("act", lambda e: e.copy(out=accB[:, qb, :], in_=acc[w][:, 0:129]),
                      reads=[("acc", w)], writes=[("accB", qb)])
            else:
                S.add("dve", lambda e: e.tensor_copy(out=accB[:, qb, :], in_=acc[w][:, 0:129]),
                      reads=[("acc", w)], writes=[("accB", qb)])
        return fn

    def epilogue(hl):
        allq = [("accB", qb) for qb in range(16)]
        S.add("dve", lambda e: e.reciprocal(out=rB[:], in_=accB[:, :, 128]), reads=allq, writes=["rB"])
        S.add("dve", lambda e: e.tensor_tensor(out=ytok[:], in0=accB[:, :, 0:128], in1=rB[:].unsqueeze(2).to_broadcast([128, 16, 128]), op=ALU.mult),
              reads=allq + ["rB"], writes=[("ytok", qb) for qb in range(16)])
        emit_transposes_to_yT(S, T, ytok, 1, 2 + hl, "ytok")

    steps = []
    for hl in range(3):
        for qb in range(16):
            w = len(steps) % 2
            steps.append((mk_scores(hl, qb, w), mk_pv(hl, qb, w), hl if qb == 15 else None))
    for i in range(len(steps) + 1):
        if i < len(steps):
            steps[i][0]()
        if i >= 1:
            steps[i - 1][1]()
            if steps[i - 1][2] is not None:
                epilogue(steps[i - 1][2])


def emit_mixer_c(S, nc, sc_, io, T):
    qk, vC, og, graw, spm, ident, tri, ones, sm = (T["qk"], T["vC"], T["og"], T["graw"], T["spm"], T["ident"],
                                                     T["tri"], T["ones"], T["sm"])
    sc, acc, misc, tr = T["sc"], T["acc"], T["misc"], T["tr"]
    GB0, NC0 = T["GB0"], T["NC0"]
    g2 = sc_.enter_context(nc.sbuf_tensor("g2", [128, 16, 4], F32))
    lf = sc_.enter_context(nc.sbuf_tensor("lf", [128, 16, 2], F32))
    eq = sc_.enter_context(nc.sbuf_tensor("eq", [128, 16, 2], F32))
    ek = sc_.enter_context(nc.sbuf_tensor("ek", [128, 16, 2], F32))
    dec = sc_.enter_context(nc.sbuf_tensor("dec", [128, 16, 2], F32))
    ktok = sc_.enter_context(nc.sbuf_tensor("ktok", [128, 16, 3, 128], BF16))
    v2 = sc_.enter_context(nc.sbuf_tensor("v2", [128, 16, 2, 193], BF16))
    swT = [sc_.enter_context(nc.sbuf_tensor(f"swT{i}", [128, 128], BF16)) for i in range(2)]
    CTa = sc_.enter_context(nc.sbuf_tensor("CTa", [128, 2, 193], F32))
    CTb = sc_.enter_context(nc.sbuf_tensor("CTb", [128, 193], F32))
    CTa16 = sc_.enter_context(nc.sbuf_tensor("CTa16", [128, 2, 193], BF16))
    CTb16 = sc_.enter_context(nc.sbuf_tensor("CTb16", [128, 193], BF16))
    hbuf = sc_.enter_context(nc.sbuf_tensor("hbuf", [128, 16, 2, 193], F32))
    ssc = sc_.enter_context(nc.sbuf_tensor("ssC", [128, 16, 2], F32))
    ss2 = sc_.enter_context(nc.sbuf_tensor("ss2C", [128, 16, 2], F32))
    tq = [sc_.enter_context(nc.sbuf_tensor(f"tqC{i}", [128, 2, 192], F32)) for i in range(1)]
    ytc = og
    S.add("dve", lambda e: e.tensor_tensor(out=g2[:], in0=graw[:], in1=spm[:, GB0:GB0 + 4].unsqueeze(1).to_broadcast([128, 16, 4]),
                                           op=ALU.add), reads=["graw", "spm"], writes=["g2"])
    S.add("act", lambda e: e.activation(out=lf[:], in_=g2[:, :, 2:4], func=AF.Exp, scale=-1.0), reads=["g2"], writes=["lf"])
    S.add("act", lambda e: e.activation(out=lf[:], in_=lf[:], func=AF.Ln, bias=1.0), reads=["lf"], writes=["lf"])
    S.add("dve", lambda e: e.tensor_scalar(out=lf[:], in0=lf[:], scalar1=-1.0, scalar2=None, op0=ALU.mult), reads=["lf"], writes=["lf"])
    lf2 = lf[:].rearrange("p a b -> p (a b)")
    S.add("pe", lambda e: e.matmul(misc[:, 0:32], lhsT=tri[:], rhs=lf2, start=True, stop=True),
          reads=["tri", "lf"], writes=["misc"])
    S.add("pe", lambda e: e.matmul(misc[:, 32:64], lhsT=ones[:], rhs=lf2, start=True, stop=True),
          reads=["onesA", "lf"], writes=["misc"])
    S.add("act", lambda e: e.activation(out=eq[:].rearrange("p a b -> p (a b)"), in_=misc[:, 0:32], func=AF.Exp),
          reads=["misc"], writes=["eq", "miscrd"])
    S.add("act", lambda e: e.activation(out=dec[:].rearrange("p a b -> p (a b)"), in_=misc[:, 32:64], func=AF.Exp),
          reads=["misc"], writes=["dec", "miscrd2"])
    S.add("act", lambda e: e.activation(out=ek[:].rearrange("p a b -> p (a b)"), in_=misc[:, 0:32], func=AF.Identity, scale=-1.0),
          reads=["misc"], writes=["ek"])
    S.add("dve", lambda e: e.tensor_tensor(out=ek[:], in0=g2[:, :, 0:2], in1=ek[:], op=ALU.add), reads=["g2", "ek"], writes=["ek"])
    import math
    S.add("act", lambda e: e.activation(out=ek[:], in_=ek[:], func=AF.Exp, bias=math.log(CH ** -0.5)), reads=["ek"], writes=["ek"])
    S.add("dve", lambda e: e.tensor_tensor(out=v2[:], in0=vC[:], in1=ek[:].unsqueeze(3).to_broadcast([128, 16, 2, 193]), op=ALU.mult),
          reads=[("vC", tb) for tb in range(16)] + ["vC1", "ek"], writes=["v2"])
    ksrc = [qk["Ck0a"], qk["Ck1a"], qk["Ckb"]]
    for t2 in range(8):
        for i in range(2):
            tb = t2 * 2 + i
            tsl = slice(tb * 128, (tb + 1) * 128)
            for c3 in range(3):
                S.add("pe", lambda e, i=i, c3=c3, tsl=tsl: e.transpose(out=tr[:, (i * 3 + c3) * 128:(i * 3 + c3 + 1) * 128],
                                                                        in_=ksrc[c3][:, tsl], identity=ident[:]),
                      reads=[("qk", "Ck0a"), ("qk", "Ck1a"), ("qk", "Ckb"), "ident"], writes=["tr"])
        S.add("dve", lambda e, t2=t2: e.tensor_copy(out=ktok[:, t2 * 2:t2 * 2 + 2, :, :],
                                                    in_=tr[:, 0:768].rearrange("p (a c d) -> p a c d", a=2, c=3)),
              reads=["tr"], writes=["ktok"])
    S.add("pool", lambda e: e.memset(CTa[:], 0.0), writes=["CTa"])
    S.add("pool", lambda e: e.memset(CTb[:], 0.0), writes=[("CTb", 0), ("CTb", 1)])
    def mk_c(tb, hl, w):
        tsl = slice(tb * 128, (tb + 1) * 128)
        hs = slice(64 * hl, 64 * hl + 64)
        qa, qbb = qk["Cq%da" % hl], qk["Cqb"]
        ka, kbb = qk["Ck%da" % hl], qk["Ckb"]
        rq = [("qk", "Cq%da" % hl), ("qk", "Cqb")]
        rk = [("qk", "Ck%da" % hl), ("qk", "Ckb")]

        def stage1():
            S.add("pe", lambda e, w=w, ka=ka, qa=qa, tsl=tsl: e.matmul(sc[w][:, 0:128], lhsT=ka[:, tsl], rhs=qa[:, tsl], start=True, stop=False),
                  reads=rq + rk, writes=[("sc", w)])
            S.add("pe", lambda e, w=w, kbb=kbb, qbb=qbb, tsl=tsl, hs=hs: e.matmul(sc[w][:, 0:128], lhsT=kbb[hs, tsl], rhs=qbb[hs, tsl], start=False, stop=True),
                  reads=rq + rk, writes=[("sc", w)])
            S.add("dve", lambda e, w=w, tb=tb, hl=hl: e.scalar_tensor_tensor(out=swT[w][:], in0=sc[w][:, 0:128], scalar=ek[:, tb, hl:hl + 1],
                                                                             in1=tri[:], op0=ALU.mult, op1=ALU.mult),
                  reads=[("sc", w), "ek", "tri"], writes=[("swT", w)])

        def stage2():
            S.add("pe", lambda e, w=w, tb=tb, hl=hl: e.matmul(acc[hl][:, 0:193], lhsT=swT[w][:], rhs=vC[:, tb, hl, :], start=True, stop=(tb == 0)),
                  reads=[("swT", w), ("vC", tb), "vC1"], writes=[("acc", hl)])
            if tb > 0:
                S.add("pe", lambda e, hl=hl, qa=qa, tsl=tsl: e.matmul(acc[hl][:, 0:193], lhsT=qa[:, tsl], rhs=CTa16[:, hl, :], start=False, stop=False),
                      reads=rq + ["CTa16"], writes=[("acc", hl)])
                S.add("pe", lambda e, hl=hl, qbb=qbb, tsl=tsl, hs=hs: e.matmul(acc[hl][:, 0:193], lhsT=qbb[hs, tsl], rhs=CTb16[hs, :], start=False, stop=True),
                      reads=rq + [("CTb16", hl)], writes=[("acc", hl)])
            if tb < 15:
                S.add("pe", lambda e, tb=tb, hl=hl: e.matmul(misc[:, 0:193], lhsT=ktok[:, tb, hl, :], rhs=v2[:, tb, hl, :], start=True, stop=True),
                      reads=["ktok", "v2"], writes=["misc"])
                S.add("pe", lambda e, tb=tb, hl=hl: e.matmul(misc[:, 256:449], lhsT=ktok[:, tb, 2, :], rhs=v2[:, tb, hl, :], start=True, stop=True),
                      reads=["ktok", "v2"], writes=["misc"])
                S.add("dve", lambda e, hl=hl: e.tensor_tensor(out=CTa[:, hl, :], in0=CTa[:, hl, :], in1=misc[:, 0:193], op=ALU.add),
                      reads=["misc", "CTa"], writes=["CTa"])
                S.add("dve", lambda e, hs=hs: e.tensor_tensor(out=CTb[hs, :], in0=CTb[hs, :], in1=misc[hs, 256:449], op=ALU.add),
                      reads=["misc", ("CTb", hl)], writes=[("CTb", hl)])
                S.add("dve", lambda e, hl=hl, tb=tb: e.tensor_scalar(out=CTa[:, hl, :], in0=CTa[:, hl, :], scalar1=dec[:, tb, hl:hl + 1], scalar2=None, op0=ALU.mult),
                      reads=["CTa", "dec"], writes=["CTa"])
                S.add("dve", lambda e, hl=hl, tb=tb, hs=hs: e.tensor_scalar(out=CTb[hs, :], in0=CTb[hs, :], scalar1=dec[hs, tb, hl:hl + 1], scalar2=None, op0=ALU.mult),
                      reads=[("CTb", hl), "dec"], writes=[("CTb", hl)])
                S.add("act", lambda e, hl=hl: e.copy(out=CTa16[:, hl, :], in_=CTa[:, hl, :]), reads=["CTa"], writes=["CTa16"])
                S.add("act", lambda e, hs=hs: e.copy(out=CTb16[hs, :], in_=CTb[hs, :]), reads=[("CTb", hl)], writes=[("CTb16", hl)])
            S.add("act", lambda e, hl=hl, tb=tb: e.activation(out=hbuf[:, tb, hl, :], in_=acc[hl][:, 0:193], func=AF.Identity,
                                                              scale=eq[:, tb, hl:hl + 1]),
                  reads=[("acc", hl), "eq"], writes=[("hbuf", tb)])

        return stage1, stage2

    csteps = []
    for tb in range(16):
        for hl in range(2):
            csteps.append(mk_c(tb, hl, len(csteps) % 2))
    for i in range(len(csteps) + 1):
        if i < len(csteps):
            csteps[i][0]()
        if i >= 1:
            csteps[i - 1][1]()
    allh = [("hbuf", tb) for tb in range(16)]
    bch = lambda ap: ap.to_broadcast([128, 16, 2, 192])
    S.add("act", lambda e: e.activation(out=ssc[:], in_=hbuf[:, :, :, 192], func=AF.Abs), reads=allh, writes=["ssC"])
    S.add("dve", lambda e: e.tensor_scalar(out=ssc[:], in0=ssc[:], scalar1=1.0, scalar2=None, op0=ALU.max), reads=["ssC"], writes=["ssC"])
    S.add("dve", lambda e: e.reciprocal(out=ssc[:], in_=ssc[:]), reads=["ssC"], writes=["ssC"])
    S.add("dve", lambda e: e.tensor_tensor(out=hbuf[:, :, :, 0:192], in0=hbuf[:, :, :, 0:192], in1=bch(ssc[:].unsqueeze(3)), op=ALU.mult),
          reads=allh + ["ssC"], writes=allh)
    for tb in range(16):
        q = 0
        S.add("pool", lambda e, tb=tb, q=q: e.tensor_tensor(out=tq[q][:], in0=hbuf[:, tb, :, 0:192], in1=hbuf[:, tb, :, 0:192], op=ALU.mult),
              reads=[("hbuf", tb)], writes=[("tqC", q)])
        S.add("dve", lambda e, tb=tb, q=q: e.reduce_sum(out=ss2[:, tb, :], in_=tq[q][:], axis=AX.X), reads=[("tqC", q)], writes=["ss2"])
    S.add("dve", lambda e: e.tensor_scalar(out=ss2[:], in0=ss2[:], scalar1=1.0 / CH, scalar2=1e-6, op0=ALU.mult, op1=ALU.add), reads=["ss2"], writes=["ss2"])
    S.add("act", lambda e: e.activation(out=ss2[:], in_=ss2[:], func=AF.Sqrt), reads=["ss2"], writes=["ss2"])
    S.add("dve", lambda e: e.reciprocal(out=ss2[:], in_=ss2[:]), reads=["ss2"], writes=["ss2"])
    S.add("dve", lambda e: e.tensor_tensor(out=hbuf[:, :, :, 0:192], in0=hbuf[:, :, :, 0:192], in1=bch(ss2[:].unsqueeze(3)), op=ALU.mult),
          reads=allh + ["ss2"], writes=allh)
    for half in range(2):
        hsl = slice(half * 8, half * 8 + 8)
        kk = [("hbuf", tb) for tb in range(half * 8, half * 8 + 8)]
        S.add("dve", lambda e, hsl=hsl: e.tensor_tensor(out=hbuf[:, hsl, :, 0:192], in0=hbuf[:, hsl, :, 0:192],
                                                        in1=spm[:, NC0:NC0 + 192].unsqueeze(1).unsqueeze(1).to_broadcast([128, 8, 2, 192]), op=ALU.mult),
              reads=kk + ["spm"], writes=kk)
        S.add("pool", lambda e, hsl=hsl: e.tensor_tensor(out=og[:, hsl, :].rearrange("p t (h d) -> p t h d", h=2), in0=hbuf[:, hsl, :, 0:192],
                                                         in1=og[:, hsl, :].rearrange("p t (h d) -> p t h d", h=2), op=ALU.mult),
              reads=kk + [("og", tb) for tb in range(half * 8, half * 8 + 8)],
              writes=[("ytokC", tb) for tb in range(half * 8, half * 8 + 8)] + [("og", tb) for tb in range(half * 8, half * 8 + 8)])
    emit_transposes_to_yT(S, T, ytc, 3, 5, "ytokC")


def build_phase_a(li):
    nc = bass.Bass("TRN2", target_bir_lowering=False)
    def din(name, shape, dt=F32):
        return nc.dram_tensor(name, shape, dt, kind="ExternalInput").ap()
    io = {
        "xT": din("xT", [D, SEQ]),
        "wqk": din("wqk", [D, NQKB * 128]),
        "wv": din("wv", [D, NV]),
        "spm": din("spm", [128, 612]),
        "ident": din("ident", [128, 128]),
        "tri": din("tri", [128, 128]),
        "kaug": din("kaug", [4, 2, SEQ]),
        "qaug": din("qaug", [4, 2, SEQ]),
        "diag": din("diag", [128, 2, 128]),
        "toep": din("toep", [128, 15, 128]),
        "yT": nc.dram_tensor("yT", [1024, SEQ], BF16, kind="ExternalOutput").ap(),
    }
    S = Sched(nc)
    with ExitStack() as es:
        emit_phase_a(S, nc, es, io, li)
        S.emit()
    return nc


def _qk_cols(r):
    cols = np.full((NQKB, 128), -1, np.int64)
    bi = 0
    for hl in range(2):
        h = 2 * r + hl
        cols[bi] = QA + h * 128 + np.arange(128); bi += 1
        cols[bi] = KA + h * 128 + np.arange(128); bi += 1
    for hl in range(3):
        h = 3 * r + hl
        cols[bi] = QB + h * 128 + np.arange(128); bi += 1
        cols[bi] = KB + h * 128 + np.arange(128); bi += 1
    for base in (QC, KC):
        for hl in range(2):
            cols[bi] = base + (2 * r + hl) * CH + np.arange(128); bi += 1
        for hl in range(2):
            cols[bi, 64 * hl:64 * hl + 64] = base + (2 * r + hl) * CH + 128 + np.arange(64)
        bi += 1
    return cols.reshape(-1)


def _v_cols(r):
    return np.concatenate([
        VA_ + 2 * r * 128 + np.arange(256), VB_ + 3 * r * 128 + np.arange(384),
        VC_ + 2 * r * CH + np.arange(384), OC + 2 * r * CH + np.arange(384),
        np.array([IC + 2 * r, IC + 2 * r + 1, FC + 2 * r, FC + 2 * r + 1])])


def pack_phase_a_weights(w_in_l, gate_bias_l, conv_w_l, lam_l, subln_l, normc_l, rel_bias_l, r):
    cols = _qk_cols(r)
    wqk = np.zeros((D, NQKB * 128), np.float32)
    ok = cols >= 0
    wqk[:, ok] = w_in_l[:, cols[ok]]
    wv = np.zeros((D, NV), np.float32)
    wv[:, 0:1412] = w_in_l[:, _v_cols(r)]
    spm = np.zeros((128, 612), np.float32)
    spm[:, 0:4] = gate_bias_l[[2 * r, 2 * r + 1, 4 + 2 * r, 4 + 2 * r + 1]][None, :]
    cb = 0
    for base in (0, 768):
        for hl in range(2):
            ch = base + (2 * r + hl) * CH
            spm[:, 4 + cb * 4:8 + cb * 4] = conv_w_l[:, ch:ch + 128].T; cb += 1
        for hl in range(2):
            ch = base + (2 * r + hl) * CH
            spm[64 * hl:64 * hl + 64, 4 + cb * 4:8 + cb * 4] = conv_w_l[:, ch + 128:ch + 192].T
        cb += 1
    spm[:, 36:164] = subln_l[None, :]
    spm[:, 164:356] = normc_l[None, :]
    spm[:, 356:612] = lam_l.reshape(1, 256)
    kl = np.arange(128)[:, None]
    ql = np.arange(128)[None, :]
    toep = np.zeros((128, 15, 128), np.float32)
    for hl in range(3):
        h = 3 * r + hl
        for j in range(5):
            rel = 128 * (4 - j) + ql - kl
            idx = np.clip(rel, -256, 256) + 256
            dc = 2 * (4 - j) + ql // 64 - kl // 64
            valid = (dc >= 0) & (dc <= 8)
            toep[:, hl * 5 + j, :] = np.where(valid, rel_bias_l[h][idx], np.float32(NEGM))
    return dict(wqk=wqk, wv=wv, spm=spm, toep=toep)


def phase_a_consts(r):
    pos = np.arange(SEQ)
    jb, rr = pos // 128, pos % 128
    kaug = np.zeros((4, 2, SEQ), np.float32)
    qaug = np.zeros((4, 2, SEQ), np.float32)
    diag = np.zeros((128, 2, 128), np.float32)
    kl = np.arange(128)[:, None]
    ql = np.arange(128)[None, :]
    for hl in range(2):
        sl = SLOPES[2 * r + hl]
        kaug[0, hl] = 1.0; kaug[1, hl] = 1.0; kaug[2, hl] = sl * 128 * jb; kaug[3, hl] = sl * rr
        qaug[0, hl] = -sl * 128 * jb; qaug[1, hl] = -sl * rr; qaug[2, hl] = 1.0; qaug[3, hl] = 1.0
        diag[:, hl, :] = np.where(kl // 64 <= ql // 64, -sl * np.abs(ql - kl), NEGM)
    ident = np.eye(128, dtype=np.float32)
    tri = (np.arange(128)[:, None] <= np.arange(128)[None, :]).astype(np.float32)
    return dict(kaug=kaug, qaug=qaug, diag=diag, ident=ident, tri=tri)


_PROG_CACHE = {}


def _prog(key, builder):
    if key not in _PROG_CACHE:
        _PROG_CACHE[key] = builder()
    return _PROG_CACHE[key]


def _wout_perm():
    parts = []
    for r in range(2):
        parts += [2 * r * 128 + np.arange(256), 512 + 3 * r * 128 + np.arange(384), 1280 + 2 * r * CH + np.arange(384)]
    return np.concatenate(parts)


def kernel_unfused(x, w_in, gate_bias, conv_w, lam, subln_a, norm_c, rel_bias, w_out,
                   ln1_g, ln1_b, w_gate, w_up, w_down, ln2_g, ln2_b):
    f = lambda a: np.ascontiguousarray(np.asarray(a, dtype=np.float32))
    x, w_in, gate_bias, conv_w, lam, subln_a, norm_c, rel_bias = map(f, (x, w_in, gate_bias, conv_w, lam, subln_a, norm_c, rel_bias))
    w_out, ln1_g, ln1_b, w_gate, w_up, w_down, ln2_g, ln2_b = map(f, (w_out, ln1_g, ln1_b, w_gate, w_up, w_down, ln2_g, ln2_b))
    cores = list(range(8))
    xT = [np.ascontiguousarray(x[b].T) for b in range(NB)]
    perm = _wout_perm()
    consts = [phase_a_consts(r) for r in range(2)]
    for li in range(DEPTH):
        nca = _prog(("A", li), lambda: build_phase_a(li))
        packs = [pack_phase_a_weights(w_in[li], gate_bias[li], conv_w[li], lam[li], subln_a[li], norm_c[li], rel_bias[li], r)
                 for r in range(2)]
        maps = []
        for c in cores:
            b, r = c // 2, c % 2
            m = dict(packs[r]); m.update(consts[r]); m["xT"] = xT[b]
            maps.append(m)
        ra = run_bass_kernel_spmd(nca, maps, core_ids=cores).results
        ncb = _prog(("B",), lambda: build_phase_b())
        wo = np.ascontiguousarray(w_out[li][perm, :])
        lnp = lnp_pack(ln1_g[li], ln1_b[li], ln2_g[li], ln2_b[li])
        maps = []
        for c in cores:
            b, r = c // 2, c % 2
            yT = np.concatenate([ra[2 * b]["yT"], ra[2 * b + 1]["yT"]], axis=0)
            maps.append({"yT": np.ascontiguousarray(yT[:, r * TOK:(r + 1) * TOK]),
                         "xres": np.ascontiguousarray(xT[b][:, r * TOK:(r + 1) * TOK]),
                         "w_out": wo, "w_gate": w_gate[li], "w_up": w_up[li], "w_down": w_down[li], "lnp": lnp})
        rb = run_bass_kernel_spmd(ncb, maps, core_ids=cores).results
        xT = [np.concatenate([rb[2 * b]["xo"], rb[2 * b + 1]["xo"]], axis=1) for b in range(NB)]
    return np.ascontiguousarray(np.stack([xT[b].T for b in range(NB)], axis=0)).astype(np.float32)


class NCP:
    def __init__(self, nc, sfx):
        self._nc = nc
        self._sfx = sfx

    def sbuf_tensor(self, name, *a, **k):
        return self._nc.sbuf_tensor(name + self._sfx, *a, **k)

    def psum_tensor(self, name, *a, **k):
        return self._nc.psum_tensor(name + self._sfx, *a, **k)

    def __getattr__(self, n):
        return getattr(self._nc, n)


PAIRS = [[0, 1], [2, 3], [4, 5], [6, 7]]
A_IN = [("wqk", [D, NQKB * 128]), ("wv", [D, NV]), ("spm", [128, 612]), ("toep", [128, 15, 128])]
B_IN = [("w_out", [D, D]), ("w_gate", [D, DFF]), ("w_up", [D, DFF]), ("w_down", [DFF, D]), ("lnp", [128, 4, 16])]
C_IN = [("ident", [128, 128]), ("tri", [128, 128]), ("kaug", [4, 2, SEQ]), ("qaug", [4, 2, SEQ]), ("diag", [128, 2, 128]),
        ("sel", [128, 2])]


def build_fused():
    nc = bass.Bass("TRN2", target_bir_lowering=False)
    def din(name, shape, dt=F32):
        return nc.dram_tensor(name, shape, dt, kind="ExternalInput").ap()
    gio = {"xT": din("xT", [D, SEQ]), "xres": din("xres", [D, TOK])}
    for name, shp in C_IN:
        gio[name] = din(name, shp)
    for li in range(DEPTH):
        for name, shp in A_IN + B_IN:
            gio[name + str(li)] = din(name + str(li), shp)
    xo = nc.dram_tensor("xo", [D, TOK], F32, kind="ExternalOutput").ap()
    scr = {}
    for li in range(DEPTH):
        scr["yown", li] = nc.dram_tensor(f"yown{li}", [1024, 1024], BF16).ap()
        scr["ysend", li] = nc.dram_tensor(f"ysend{li}", [1024, 1024], BF16).ap()
        scr["ygath", li] = nc.dram_tensor(f"ygath{li}", [2048, 1024], BF16).ap()
    xsend = [nc.dram_tensor(f"xsend{h}", [D // 2, TOK], BF16).ap() for h in range(2)]
    xgath = [nc.dram_tensor(f"xgath{h}", [D, TOK], BF16).ap() for h in range(2)]
    xres_scr = nc.dram_tensor("xres_scr", [D, TOK], F32).ap()
    S = Sched(nc)
    for li in range(DEPTH):
        with ExitStack() as es:
            io = {k: gio[k] for k, _ in C_IN}
            for name, _ in A_IN:
                io[name] = gio[name + str(li)]
            io["xT"] = gio["xT"]
            io["xgath"] = xgath if li > 0 else None
            io["yown"], io["ysend"] = scr["yown", li], scr["ysend", li]
            emit_phase_a(S, NCP(nc, f"_a{li}"), es, io, li)
        S.add("pool", lambda e, li=li: e.collective_compute("AllGather", ALU.bypass, replica_groups=PAIRS,
                                                            ins=[scr["ysend", li].opt()], outs=[scr["ygath", li].opt()]),
              reads=["ysend"], writes=["ygath"], dma_group=("ccy", li), inc=1)
        with ExitStack() as es:
            io = {"sel": gio["sel"]}
            for name, _ in B_IN:
                io[name] = gio[name + str(li)]
            io["yown"], io["ygath"] = scr["yown", li], scr["ygath", li]
            io["xres"] = gio["xres"] if li == 0 else xres_scr
            if li == 0:
                io["xo"], io["xo_key"], io["xsend"] = xres_scr, "xres_scr", xsend
            else:
                io["xo"], io["xsend"] = xo, None
            if li > 0:
                pass
            emit_phase_b(S, NCP(nc, f"_b{li}"), es, io)
            S.barrier()
        if li == 0:
            for fh in range(2):
                S.add("pool", lambda e, fh=fh: e.collective_compute("AllGather", ALU.bypass, replica_groups=PAIRS,
                                                                    ins=[xsend[fh].opt()], outs=[xgath[fh].opt()]),
                      reads=["xsend"], writes=["xgath"], dma_group=("ccx", fh), inc=1)
    S.barrier()
    S.emit()
    return nc


def kernel(x, w_in, gate_bias, conv_w, lam, subln_a, norm_c, rel_bias, w_out,
           ln1_g, ln1_b, w_gate, w_up, w_down, ln2_g, ln2_b):
    f = lambda a: np.ascontiguousarray(np.asarray(a, dtype=np.float32))
    x, w_in, gate_bias, conv_w, lam, subln_a, norm_c, rel_bias = map(f, (x, w_in, gate_bias, conv_w, lam, subln_a, norm_c, rel_bias))
    w_out, ln1_g, ln1_b, w_gate, w_up, w_down, ln2_g, ln2_b = map(f, (w_out, ln1_g, ln1_b, w_gate, w_up, w_down, ln2_g, ln2_b))
    cores = list(range(8))
    perm = _wout_perm()
    nc = _prog(("F",), build_fused)
    xT = [np.ascontiguousarray(x[b].T) for b in range(NB)]
    shared = {}
    for li in range(DEPTH):
        shared["w_out%d" % li] = np.ascontiguousarray(w_out[li][perm, :])
        shared["w_gate%d" % li] = w_gate[li]
        shared["w_up%d" % li] = w_up[li]
        shared["w_down%d" % li] = w_down[li]
        shared["lnp%d" % li] = lnp_pack(ln1_g[li], ln1_b[li], ln2_g[li], ln2_b[li])
    per_r = []
    for r in range(2):
        m = dict(phase_a_consts(r))
        sel = np.zeros((128, 2), np.float32)
        sel[:, r] = 1.0
        m["sel"] = sel
        for li in range(DEPTH):
            pk = pack_phase_a_weights(w_in[li], gate_bias[li], conv_w[li], lam[li], subln_a[li], norm_c[li], rel_bias[li], r)
            for k, v in pk.items():
                m[k + str(li)] = v
        per_r.append(m)
    maps = []
    for c in cores:
        b, r = c // 2, c % 2
        m = dict(shared); m.update(per_r[r])
        m["xT"] = xT[b]
        m["xres"] = np.ascontiguousarray(xT[b][:, r * TOK:(r + 1) * TOK])
        maps.append(m)
    res = run_bass_kernel_spmd(nc, maps, core_ids=cores).results
    out = np.empty((NB, SEQ, D), np.float32)
    for c in cores:
        b, r = c // 2, c % 2
        out[b, r * TOK:(r + 1) * TOK, :] = res[c]["xo"].T
    return out
```

```python
from contextlib import ExitStack
import numpy as np
import concourse.bass as bass
import concourse.mybir as mybir

F32 = mybir.dt.float32
BF16 = mybir.dt.bfloat16
AF = mybir.ActivationFunctionType
ALU = mybir.AluOpType
AX = mybir.AxisListType

ENGS = ("pe", "act", "dve", "pool", "sp")


class Op:
    __slots__ = ("eng", "fn", "deps", "signal", "dma_group", "dma_n", "idx", "sigcount", "inc")

    def __init__(self, eng, fn):
        self.eng = eng
        self.fn = fn
        self.deps = {}
        self.signal = False
        self.dma_group = None
        self.dma_n = 0
        self.idx = -1
        self.sigcount = 0
        self.inc = 16


class Sched:
    def __init__(self, nc):
        self.nc = nc
        self.streams = {e: [] for e in ENGS}
        self.last_write = {}
        self.readers = {}
        self.dma_counts = {}
        self.group_inc = {}
        self.seen = {e: {} for e in ENGS}

    def _add_dep(self, op, tok):
        if tok is None:
            return
        if tok[0] == "c":
            _, e, i = tok
            if e == op.eng and e != "pool":
                pass
            cur = op.deps.get(("c", e), -1)
            if i > cur:
                op.deps[("c", e)] = i
        else:
            _, g, n = tok
            cur = op.deps.get(("d", g), 0)
            if n > cur:
                op.deps[("d", g)] = n

    def add(self, eng, fn, reads=(), writes=(), dma_group=None, inc=16):
        op = Op(eng, fn)
        op.inc = inc
        if dma_group is not None:
            self.group_inc[dma_group] = inc
        op.idx = len(self.streams[eng])
        for k in reads:
            w = self.last_write.get(k)
            if w is not None:
                self._add_dep(op, w)
        for k in writes:
            w = self.last_write.get(k)
            if w is not None:
                if not (w[0] == "c" and w[1] == eng and dma_group is None):
                    self._add_dep(op, w)
            for r in self.readers.get(k, ()):
                if not (r[0] == "c" and r[1] == eng and dma_group is None):
                    self._add_dep(op, r)
        if dma_group is not None:
            n = self.dma_counts.get(dma_group, 0) + 1
            self.dma_counts[dma_group] = n
            op.dma_group = dma_group
            op.dma_n = n
            tok = ("d", dma_group, n)
        else:
            tok = ("c", eng, op.idx)
        for k in reads:
            self.readers.setdefault(k, []).append(tok)
        for k in writes:
            self.last_write[k] = tok
            self.readers[k] = []
        seen = self.seen[eng]
        for key in list(op.deps.keys()):
            v = op.deps[key]
            if key[0] == "c" and key[1] == eng and dma_group is None:
                pass
            if seen.get(key, -1 if key[0] == "c" else 0) >= v:
                del op.deps[key]
            else:
                seen[key] = v
        self.streams[eng].append(op)
        return op

    def barrier(self, only_active=True):
        toks = []
        for e in ENGS:
            for op in reversed(self.streams[e]):
                if op.fn is not None and op.dma_group is None:
                    toks.append(("c", e, op.idx))
                    break
        dtoks = [("d", g, n) for g, n in self.dma_counts.items()]
        for e in ENGS:
            if only_active and not self.streams[e]:
                continue
            op = Op(e, None)
            op.idx = len(self.streams[e])
            for t in toks:
                if t[1] != e:
                    self._add_dep(op, t)
            for t in dtoks:
                self._add_dep(op, t)
            seen = self.seen[e]
            for key in list(op.deps.keys()):
                v = op.deps[key]
                if seen.get(key, -1 if key[0] == "c" else 0) >= v:
                    del op.deps[key]
                else:
                    seen[key] = v
            self.streams[e].append(op)
        self.last_write.clear()
        self.readers.clear()

    def emit(self, final_waits=()):
        nc = self.nc
        for e in ENGS:
            for op in self.streams[e]:
                for key, v in op.deps.items():
                    if key[0] == "c":
                        self.streams[key[1]][v].signal = True
        for e in ENGS:
            c = 0
            for op in self.streams[e]:
                if op.signal:
                    c += 1
                op.sigcount = c
        with ExitStack() as es:
            sems = {}
            for e in ENGS:
                sems[("c", e)] = es.enter_context(nc.semaphore("s_" + e))
            for g in self.dma_counts:
                sems[("d", g)] = es.enter_context(nc.semaphore("d_" + str(g)))
            block = es.enter_context(nc.Block())
            streams = self.streams

            def run(e, engobj):
                for op in streams[e]:
                    for key, v in op.deps.items():
                        if key[0] == "c":
                            tgt = streams[key[1]][v]
                            assert tgt.signal
                            engobj.wait_ge(sems[key], tgt.sigcount)
                        else:
                            engobj.wait_ge(sems[key], self.group_inc[key[1]] * v)
                    if op.fn is None:
                        continue
                    ins = op.fn(engobj)
                    if op.dma_group is not None:
                        if op.inc == 16:
                            ins.then_inc(sems[("d", op.dma_group)], 16)
                        else:
                            ins.then_inc(sems[("d", op.dma_group)])
                    elif op.signal:
                        ins.then_inc(sems[("c", e)], 1)

            @block.tensor
            def _(eng):
                run("pe", eng)

            @block.scalar
            def _(eng):
                run("act", eng)

            @block.vector
            def _(eng):
                run("dve", eng)

            @block.gpsimd
            def _(eng):
                run("pool", eng)

            @block.sync
            def _(eng):
                run("sp", eng)

from concourse.bass_utils import run_bass_kernel_spmd
import ml_dtypes

D = 2048
SEQ = 2048
NB = 4
TOK = 1024
DFF = 5632
NFF = DFF // 128
DEPTH = 2
ALPHA = (2.0 * DEPTH) ** 0.25
LN_EPS = 1e-5
FF_SPLITS = [(0, 6), (6, 12), (12, 18), (18, 22)]


class PsumRot:
    def __init__(self, banks):
        self.banks = banks
        self.i = 0

    def get(self):
        b = self.banks[self.i % len(self.banks)]
        self.i += 1
        return b


def emit_layernorm(S, nc, res, outT, lnp, gi, bi, ones, sq, stats, tmpv, psab, ntok, tag):
    inv = 1.0 / D
    ntg = ntok // 512
    for tg in range(ntg):
        ts = slice(tg * 512, (tg + 1) * 512)
        mean, rstd = stats[tg]
        (psA, kA), (psB, kB) = psab[tg]
        km, kr = ("lnmean", tg), ("lnrstd", tg)
        for c in range(16):
            q = c % 2
            S.add("act", lambda e, c=c, q=q, ts=ts: e.activation(out=sq[q][:], in_=res[:, c, ts], func=AF.Square),
                  reads=[("res", c, tg)], writes=[("sq", q)])
            S.add("pe", lambda e, c=c, ts=ts, psA=psA: e.matmul(psA[:], lhsT=ones[:], rhs=res[:, c, ts],
                                                                 start=(c == 0), stop=(c == 15)),
                  reads=[("res", c, tg), "ones"], writes=[kA])
            S.add("pe", lambda e, c=c, q=q, psB=psB: e.matmul(psB[:], lhsT=ones[:], rhs=sq[q][:],
                                                               start=(c == 0), stop=(c == 15)),
                  reads=[("sq", q), "ones"], writes=[kB])
        S.add("dve", lambda e, mean=mean, psA=psA: e.tensor_scalar(out=mean[:], in0=psA[:], scalar1=inv, scalar2=None, op0=ALU.mult),
              reads=[kA], writes=[km])
        S.add("dve", lambda e, mean=mean: e.tensor_tensor(out=tmpv[:], in0=mean[:], in1=mean[:], op=ALU.mult),
              reads=[km], writes=["tmpv"])
        S.add("dve", lambda e, rstd=rstd, psB=psB: e.scalar_tensor_tensor(out=rstd[:], in0=psB[:], scalar=inv, in1=tmpv[:],
                                                                          op0=ALU.mult, op1=ALU.subtract),
              reads=[kB, "tmpv"], writes=[kr])
        S.add("dve", lambda e, rstd=rstd: e.tensor_scalar(out=rstd[:], in0=rstd[:], scalar1=LN_EPS, scalar2=None, op0=ALU.add),
              reads=[kr], writes=[kr])
        S.add("act", lambda e, rstd=rstd: e.activation(out=tmpv[:], in_=rstd[:], func=AF.Sqrt),
              reads=[kr], writes=["tmpv"])
        S.add("dve", lambda e, rstd=rstd: e.reciprocal(out=rstd[:], in_=tmpv[:]),
              reads=["tmpv"], writes=[kr])
    for tg in range(ntg):
        ts = slice(tg * 512, (tg + 1) * 512)
        mean, rstd = stats[tg]
        km, kr = ("lnmean", tg), ("lnrstd", tg)
        for c in range(16):
            S.add("dve", lambda e, c=c, ts=ts, mean=mean: e.tensor_tensor(out=res[:, c, ts], in0=res[:, c, ts], in1=mean[:],
                                                                          op=ALU.subtract),
                  reads=[("res", c, tg), km], writes=[("res", c, tg)])
            S.add("pool", lambda e, c=c, ts=ts, rstd=rstd: e.tensor_tensor(out=res[:, c, ts], in0=res[:, c, ts], in1=rstd[:],
                                                                           op=ALU.mult),
                  reads=[("res", c, tg), kr], writes=[("res", c, tg)])
            S.add("act", lambda e, c=c, ts=ts: e.activation(out=res[:, c, ts], in_=res[:, c, ts], func=AF.Identity,
                                                            scale=lnp[:, gi, c:c + 1], bias=lnp[:, bi, c:c + 1]),
                  reads=[("res", c, tg), "lnp"], writes=[("res", c, tg)])
            if outT is not None:
                S.add("act", lambda e, c=c, ts=ts: e.copy(out=outT[:, c, ts], in_=res[:, c, ts]),
                      reads=[("res", c, tg)], writes=[("xT", c, tg)])


def emit_phase_b(S, nc, es, io, ntok=TOK, ff_splits=None):
    ff_splits = ff_splits or FF_SPLITS
    NTG = ntok // 512
    res = es.enter_context(nc.sbuf_tensor("res", [128, 16, ntok], F32))
    x1T = es.enter_context(nc.sbuf_tensor("x1T", [128, 16, ntok], BF16))
    big = es.enter_context(nc.sbuf_tensor("big", [128, 16, ntok], BF16))
    wbuf = [es.enter_context(nc.sbuf_tensor(f"wbuf{i}", [128, 16, 256], BF16)) for i in range(6)]
    wd = [es.enter_context(nc.sbuf_tensor(f"wd{i}", [128, 12, 256], BF16)) for i in range(2)]
    ones = es.enter_context(nc.sbuf_tensor("ones", [128, 128], F32))
    lnp = es.enter_context(nc.sbuf_tensor("lnp_sb", [128, 4, 16], F32))
    sq = [es.enter_context(nc.sbuf_tensor(f"sq{i}", [128, 512], F32)) for i in range(2)]
    stats = [(es.enter_context(nc.sbuf_tensor(f"mean{i}", [128, 512], F32)),
              es.enter_context(nc.sbuf_tensor(f"rstd{i}", [128, 512], F32))) for i in range(NTG)]
    tmpv = es.enter_context(nc.sbuf_tensor("tmpv", [128, 512], F32))
    sg = [es.enter_context(nc.sbuf_tensor(f"sg{i}", [128, 512], F32)) for i in range(2)]
    pbanks = [es.enter_context(nc.psum_tensor(f"pb{i}", [128, 512], F32)) for i in range(8)]
    psab = [((pbanks[6], ("ps", 6)), (pbanks[7], ("ps", 7))), ((pbanks[4], ("ps", 4)), (pbanks[5], ("ps", 5)))][:NTG]
    rot = PsumRot(list(range(6)))

    S.add("pool", lambda e: e.memset(ones[:], 1.0), writes=["ones"])
    S.add("sp", lambda e: e.dma_start(out=lnp[:], in_=io["lnp"]), writes=["lnp"], dma_group="lnp")
    for c4 in range(4):
        S.add("sp", lambda e, c4=c4: e.dma_start(
            out=res[:, c4 * 4:(c4 + 1) * 4, :],
            in_=io["xres"].rearrange("(c p) t -> p c t", p=128)[:, c4 * 4:(c4 + 1) * 4, :]),
            writes=[("res", c, tg) for c in range(c4 * 4, c4 * 4 + 4) for tg in range(NTG)],
            dma_group=("xres", c4))
    if io.get("ygath") is None:
        S.add("sp", lambda e: e.dma_start(out=big[:], in_=io["yT"].rearrange("(c p) t -> p c t", p=128)),
              writes=[("big", c) for c in range(16)], dma_group="yT")
    else:
        selb = es.enter_context(nc.sbuf_tensor("selB", [128, 2], F32))
        S.add("sp", lambda e: e.dma_start(out=selb[:], in_=io["sel"]), writes=["selB"], dma_group="selB")
        if io.get("ygathAB") is None:
            parts = [(io["yown"], io["ygath"], 0, 8, "")]
        else:
            parts = [(io["yownAB"], io["ygathAB"], 0, 5, "AB"), (io["yown"], io["ygath"], 5, 8, "C")]
        for (d_own, d_g, c0, c1, tag) in parts:
            n = c1 - c0
            yo = d_own.rearrange("(c p) t -> p c t", p=128)
            yg = d_g.rearrange("(r c p) t -> p r c t", r=2, p=128)
            for hf in range(2):
                S.add("sp", lambda e, hf=hf, yo=yo, c0=c0, c1=c1: e.dma_start(out=big[:, hf * 8 + c0:hf * 8 + c1, :], in_=yo),
                      reads=["yown" + tag], writes=[("big", c) for c in range(hf * 8 + c0, hf * 8 + c1)], dma_group=("yT" + tag, hf))
                S.add("sp", lambda e, hf=hf, yg=yg, c0=c0, c1=c1: e.dma_start(out=x1T[:, hf * 8 + c0:hf * 8 + c1, :], in_=yg[:, hf, :, :]),
                      reads=["ygath" + tag], writes=[("xT", c, tg) for c in range(hf * 8 + c0, hf * 8 + c1) for tg in range(NTG)],
                      dma_group=("ygl" + tag, hf))
        for hf in range(2):
            hs = slice(hf * 8, hf * 8 + 8)
            kb = [("big", c) for c in range(hf * 8, hf * 8 + 8)]
            kx = [("xT", c, tg) for c in range(hf * 8, hf * 8 + 8) for tg in range(NTG)]
            S.add("dve", lambda e, hs=hs, hf=hf: e.tensor_scalar(out=big[:, hs, :], in0=big[:, hs, :], scalar1=selb[:, hf:hf + 1], scalar2=None, op0=ALU.mult),
                  reads=kb + ["selB"], writes=kb)
            S.add("dve", lambda e, hs=hs, hf=hf: e.scalar_tensor_tensor(out=big[:, hs, :], in0=x1T[:, hs, :], scalar=selb[:, 1 - hf:2 - hf], in1=big[:, hs, :],
                                                                        op0=ALU.mult, op1=ALU.add),
                  reads=kb + kx + ["selB"], writes=kb)

    wctr = [0]

    def load_w(src_ap, tag):
        s = wctr[0] % 6
        wctr[0] += 1
        S.add("pool", lambda e, s=s: e.dma_start(out=wbuf[s][:], in_=src_ap),
              writes=[("wbuf", s)], dma_group=("wbuf", s))
        return s

    wo = io["w_out"].rearrange("(kc p) n -> p kc n", p=128)
    for cg in range(8):
        s = load_w(wo[:, :, cg * 256:(cg + 1) * 256], "wo")
        for cc in range(2):
            col = cg * 2 + cc
            for tg in range(NTG):
                ts = slice(tg * 512, (tg + 1) * 512)
                b = rot.get()
                for kc in range(16):
                    S.add("pe", lambda e, s=s, cc=cc, kc=kc, b=b, ts=ts: e.matmul(
                        pbanks[b][:], lhsT=wbuf[s][:, kc, cc * 128:(cc + 1) * 128], rhs=big[:, kc, ts],
                        start=(kc == 0), stop=(kc == 15)),
                        reads=[("wbuf", s), ("big", kc)], writes=[("ps", b)])
                S.add("dve", lambda e, col=col, b=b, ts=ts: e.scalar_tensor_tensor(
                    out=res[:, col, ts], in0=res[:, col, ts], scalar=ALPHA, in1=pbanks[b][:],
                    op0=ALU.mult, op1=ALU.add),
                    reads=[("ps", b), ("res", col, tg)], writes=[("res", col, tg)])
    emit_layernorm(S, nc, res, x1T, lnp, 0, 1, ones, sq, stats, tmpv, psab, ntok, "ln1")
    wg = io["w_gate"].rearrange("(kc p) n -> p kc n", p=128)
    wu = io["w_up"].rearrange("(kc p) n -> p kc n", p=128)
    sgc = [0]
    wdc = [0]
    for si, (sl0, sl1) in enumerate(ff_splits):
        nch = (sl1 - sl0) * 2
        for sl in range(sl0, sl1):
            sgt = load_w(wg[:, :, sl * 256:(sl + 1) * 256], "wg")
            sut = load_w(wu[:, :, sl * 256:(sl + 1) * 256], "wu")
            for cc in range(2):
                j = (sl - sl0) * 2 + cc
                for tg in range(NTG):
                    ts = slice(tg * 512, (tg + 1) * 512)
                    bg = rot.get()
                    bu = rot.get()
                    for kc in range(16):
                        S.add("pe", lambda e, s=sgt, cc=cc, kc=kc, b=bg, ts=ts: e.matmul(
                            pbanks[b][:], lhsT=wbuf[s][:, kc, cc * 128:(cc + 1) * 128], rhs=x1T[:, kc, ts],
                            start=(kc == 0), stop=(kc == 15)),
                            reads=[("wbuf", sgt), ("xT", kc, tg)], writes=[("ps", bg)])
                    for kc in range(16):
                        S.add("pe", lambda e, s=sut, cc=cc, kc=kc, b=bu, ts=ts: e.matmul(
                            pbanks[b][:], lhsT=wbuf[s][:, kc, cc * 128:(cc + 1) * 128], rhs=x1T[:, kc, ts],
                            start=(kc == 0), stop=(kc == 15)),
                            reads=[("wbuf", sut), ("xT", kc, tg)], writes=[("ps", bu)])
                    q = sgc[0] % 2
                    sgc[0] += 1
                    S.add("act", lambda e, q=q, b=bg: e.activation(out=sg[q][:], in_=pbanks[b][:], func=AF.Silu),
                          reads=[("ps", bg)], writes=[("sg", q)])
                    S.add("dve", lambda e, q=q, b=bu, j=j, ts=ts: e.tensor_tensor(
                        out=big[:, j, ts], in0=sg[q][:], in1=pbanks[b][:], op=ALU.mult),
                        reads=[("sg", q), ("ps", bu)], writes=[("big", j)])
        r0 = sl0 * 256
        for cp in range(8):
            s = wdc[0] % 2
            wdc[0] += 1
            src = io["w_down"][r0:r0 + nch * 128, cp * 256:(cp + 1) * 256].rearrange("(j p) n -> p j n", p=128)
            S.add("pool", lambda e, s=s, src=src, nch=nch: e.dma_start(out=wd[s][:, 0:nch, :], in_=src),
                  writes=[("wd", s)], dma_group=("wd", s))
            for c2 in range(2):
                col = cp * 2 + c2
                for tg in range(NTG):
                    ts = slice(tg * 512, (tg + 1) * 512)
                    b = rot.get()
                    for j in range(nch):
                        S.add("pe", lambda e, s=s, c2=c2, j=j, b=b, ts=ts, nch=nch: e.matmul(
                            pbanks[b][:], lhsT=wd[s][:, j, c2 * 128:(c2 + 1) * 128], rhs=big[:, j, ts],
                            start=(j == 0), stop=(j == nch - 1)),
                            reads=[("wd", s), ("big", j)], writes=[("ps", b)])
                    if si == 0:
                        S.add("dve", lambda e, col=col, b=b, ts=ts: e.scalar_tensor_tensor(
                            out=res[:, col, ts], in0=res[:, col, ts], scalar=ALPHA, in1=pbanks[b][:],
                            op0=ALU.mult, op1=ALU.add),
                            reads=[("ps", b), ("res", col, tg)], writes=[("res", col, tg)])
                    else:
                        S.add("dve", lambda e, col=col, b=b, ts=ts: e.tensor_tensor(
                            out=res[:, col, ts], in0=res[:, col, ts], in1=pbanks[b][:], op=ALU.add),
                            reads=[("ps", b), ("res", col, tg)], writes=[("res", col, tg)])
    xsend = io.get("xsend")
    emit_layernorm(S, nc, res, x1T if xsend is not None else None, lnp, 2, 3, ones, sq, stats, tmpv, psab, ntok, "ln2")
    for c4 in range(4):
        S.add("sp", lambda e, c4=c4: e.dma_start(
            out=io["xo"].rearrange("(c p) t -> p c t", p=128)[:, c4 * 4:(c4 + 1) * 4, :],
            in_=res[:, c4 * 4:(c4 + 1) * 4, :]),
            reads=[("res", c, tg) for c in range(c4 * 4, c4 * 4 + 4) for tg in range(NTG)],
            writes=[io.get("xo_key", "xo_out")], dma_group=("xo", c4))
    if xsend is not None:
        for fh in range(2):
            S.add("sp", lambda e, fh=fh: e.dma_start(out=xsend[fh].rearrange("(c p) t -> p c t", p=128), in_=x1T[:, fh * 8:fh * 8 + 8, :]),
                  reads=[("xT", c, tg) for c in range(fh * 8, fh * 8 + 8) for tg in range(NTG)], writes=["xsend"], dma_group=("xsend", fh))


def build_phase_b(ntok=TOK, ff_splits=None, dff=DFF):
    nc = bass.Bass("TRN2", target_bir_lowering=False)
    io = {
        "yT": nc.dram_tensor("yT", [D, ntok], BF16, kind="ExternalInput").ap(),
        "xres": nc.dram_tensor("xres", [D, ntok], F32, kind="ExternalInput").ap(),
        "w_out": nc.dram_tensor("w_out", [D, D], F32, kind="ExternalInput").ap(),
        "w_gate": nc.dram_tensor("w_gate", [D, dff], F32, kind="ExternalInput").ap(),
        "w_up": nc.dram_tensor("w_up", [D, dff], F32, kind="ExternalInput").ap(),
        "w_down": nc.dram_tensor("w_down", [dff, D], F32, kind="ExternalInput").ap(),
        "lnp": nc.dram_tensor("lnp", [128, 4, 16], F32, kind="ExternalInput").ap(),
        "xo": nc.dram_tensor("xo", [D, ntok], F32, kind="ExternalOutput").ap(),
    }
    S = Sched(nc)
    with ExitStack() as es:
        emit_phase_b(S, nc, es, io, ntok, ff_splits)
        S.barrier()
        S.emit()
    return nc


def lnp_pack(g1, b1, g2, b2):
    return np.ascontiguousarray(np.stack([v.reshape(16, 128).T for v in (g1, b1, g2, b2)], axis=1)).astype(np.float32)


A_HEADS_PER = 2
B_HEADS_PER = 3
C_HEADS_PER = 2
CH = 192
QA, KA, VA_, QB, KB, VB_, QC, KC, VC_, OC, IC, FC = 0, 512, 1024, 1536, 2304, 3072, 3840, 4608, 5376, 6144, 6912, 6916
QK_BLOCKS = (["Aq0", "Ak0", "Aq1", "Ak1", "Bq0", "Bk0", "Bq1", "Bk1", "Bq2", "Bk2"]
             + ["Cq0a", "Cq1a", "Cqb", "Ck0a", "Ck1a", "Ckb"])
NQKB = len(QK_BLOCKS)
V_SEGS = [("Av", 0, 256), ("Bv01", 256, 512), ("Bv2", 512, 640), ("Cv0", 640, 832), ("Cv1", 832, 1024),
          ("Coa", 1024, 1280), ("Cob", 1280, 1536)]
NV = 1536
SLOPES = [2.0 ** (-8.0 * (h + 1) / 4) for h in range(4)]
NEGM = -30000.0


def lam_init_of(li):
    import math
    return 0.8 - 0.6 * math.exp(-0.3 * li)


def emit_phase_a(S, nc, es, io, li):
    lam_init = lam_init_of(li)
    qk = {}
    for name in QK_BLOCKS:
        qk[name] = es.enter_context(nc.sbuf_tensor("qk_" + name, [128, SEQ], BF16))
    vA = es.enter_context(nc.sbuf_tensor("vA", [128, 16, 2, 129], BF16))
    vB = es.enter_context(nc.sbuf_tensor("vB", [128, 16, 3, 129], BF16))
    vC = es.enter_context(nc.sbuf_tensor("vC", [128, 16, 2, 193], BF16))
    og = es.enter_context(nc.sbuf_tensor("og", [128, 16, 384], BF16))
    graw = es.enter_context(nc.sbuf_tensor("graw", [128, 16, 4], F32))
    spm = es.enter_context(nc.sbuf_tensor("spm_sb", [128, 740], F32))
    ident = es.enter_context(nc.sbuf_tensor("ident_sb", [128, 128], BF16))
    tri = es.enter_context(nc.sbuf_tensor("tri_sb", [128, 128], F32))
    ones = es.enter_context(nc.sbuf_tensor("onesA", [128, 128], F32))
    GB0, CW0, SL0, NC0, LM0 = 0, 4, 36, 164, 356

    S.add("sp", lambda e: e.dma_start(out=spm[:, 0:612], in_=io["spm"]), writes=["spm"], dma_group="spm")
    S.add("pool", lambda e: e.dma_start(out=ident[:], in_=io["ident"]), writes=["ident"], dma_group="cst")
    S.add("sp", lambda e: e.dma_start(out=tri[:], in_=io["tri"]), writes=["tri"], dma_group="cst2")
    S.add("pool", lambda e: e.memset(ones[:], 1.0), writes=["onesA"])
    S.add("pool", lambda e: e.memset(vA[:, :, :, 128:129], 1.0), writes=["vA1"])
    S.add("pool", lambda e: e.memset(vB[:, :, :, 128:129], 1.0), writes=["vB1"])
    S.add("pool", lambda e: e.memset(vC[:, :, :, 192:193], 1.0), writes=["vC1"])

    with ExitStack() as ps_:
        xT = ps_.enter_context(nc.sbuf_tensor("xTa", [128, 16, SEQ], BF16))
        wst = [ps_.enter_context(nc.sbuf_tensor(f"wst{i}", [128, 16, 256], BF16)) for i in range(2)]
        craw = ps_.enter_context(nc.sbuf_tensor("craw", [128, SEQ + 4], F32))
        cacc = ps_.enter_context(nc.sbuf_tensor("cacc", [128, 512], F32))
        pb = [ps_.enter_context(nc.psum_tensor(f"pa{i}", [128, 512], F32)) for i in range(8)]
        rot = PsumRot(list(range(8)))
        if io.get("xgath") is None:
            xsrc = io["xT"].rearrange("(c p) t -> p c t", p=128)
            for c4 in range(8):
                S.add("pool", lambda e, c4=c4: e.dma_start(out=xT[:, c4 * 2:c4 * 2 + 2, :], in_=xsrc[:, c4 * 2:c4 * 2 + 2, :]),
                      writes=[("xT", c4 * 2), ("xT", c4 * 2 + 1)], dma_group=("xTl", c4))
        else:
            for fh in range(2):
                xg = io["xgath"][fh].rearrange("(r c p) t -> p r c t", r=2, p=128)
                for c4 in range(2):
                    for rr in range(2):
                        cc0 = fh * 8 + c4 * 4
                        S.add("sp", lambda e, c4=c4, rr=rr, cc0=cc0, xg=xg: e.dma_start(
                            out=xT[:, cc0:cc0 + 4, rr * 1024:(rr + 1) * 1024], in_=xg[:, rr, c4 * 4:c4 * 4 + 4, :]),
                            reads=["xgath"], writes=[("xT", cc0 + i) for i in range(4)], dma_group=("xTl", fh * 4 + c4 * 2 + rr))
        S.add("pool", lambda e: e.memset(craw[:, 0:4], 0.0), writes=["craw_pad"])
        wctr = [0]

        def load_w(src_ap, ncols):
            s = wctr[0] % 2
            wctr[0] += 1
            S.add("pool", lambda e, s=s, src_ap=src_ap, ncols=ncols: e.dma_start(out=wst[s][:, :, 0:ncols], in_=src_ap),
                  writes=[("wst", s)], dma_group=("wst", s))
            return s

        evc = [0]
        wqk = io["wqk"].rearrange("(kc p) n -> p kc n", p=128)
        import os as _os
        _stop = int(_os.environ.get("PA_STOP", "9"))
        for sl in range(NQKB // 2 if _stop >= 1 else 0):
            s = load_w(wqk[:, :, sl * 256:(sl + 1) * 256], 256)
            for bi2 in range(2):
                bidx = sl * 2 + bi2
                name = QK_BLOCKS[bidx]
                isC = name[0] == "C"
                for tg in range(4):
                    ts = slice(tg * 512, (tg + 1) * 512)
                    b = rot.get()
                    for kc in range(16):
                        S.add("pe", lambda e, s=s, bi2=bi2, kc=kc, b=b, ts=ts: e.matmul(
                            pb[b][:], lhsT=wst[s][:, kc, bi2 * 128:(bi2 + 1) * 128], rhs=xT[:, kc, ts],
                            start=(kc == 0), stop=(kc == 15)),
                            reads=[("wst", s), ("xT", kc)], writes=[("ps", b)])
                    if isC:
                        cts = slice(4 + tg * 512, 4 + (tg + 1) * 512)
                        eng = "act" if evc[0] % 2 == 0 else "dve"
                        evc[0] += 1
                        if eng == "act":
                            S.add("act", lambda e, b=b, cts=cts: e.copy(out=craw[:, cts], in_=pb[b][:]),
                                  reads=[("ps", b)], writes=[("craw", tg)])
                        else:
                            S.add("dve", lambda e, b=b, cts=cts: e.tensor_copy(out=craw[:, cts], in_=pb[b][:]),
                                  reads=[("ps", b)], writes=[("craw", tg)])
                    else:
                        if name[1] == "q":
                            scl = 0.125 if name[0] == "A" else 128.0 ** -0.5
                            S.add("act", lambda e, b=b, ts=ts, name=name, scl=scl: e.activation(
                                out=qk[name][:, ts], in_=pb[b][:], func=AF.Identity, scale=scl),
                                reads=[("ps", b)], writes=[("qk", name)])
                        else:
                            S.add("dve", lambda e, b=b, ts=ts, name=name: e.tensor_copy(out=qk[name][:, ts], in_=pb[b][:]),
                                  reads=[("ps", b)], writes=[("qk", name)])
                    if isC:
                        cb = QK_BLOCKS.index(name) - 10
                        crd = [("craw", tg), "craw_pad", "spm"] + ([("craw", tg - 1)] if tg > 0 else [])
                        o0 = tg * 512
                        S.add("dve", lambda e, cb=cb, o0=o0: e.tensor_scalar(
                            out=cacc[:], in0=craw[:, 1 + o0:1 + o0 + 512], scalar1=spm[:, CW0 + cb * 4:CW0 + cb * 4 + 1], scalar2=None,
                            op0=ALU.mult), reads=crd, writes=["cacc"])
                        for j in range(1, 4):
                            S.add("dve", lambda e, cb=cb, j=j, o0=o0: e.scalar_tensor_tensor(
                                out=cacc[:], in0=craw[:, 1 + j + o0:1 + j + o0 + 512], scalar=spm[:, CW0 + cb * 4 + j:CW0 + cb * 4 + j + 1],
                                in1=cacc[:], op0=ALU.mult, op1=ALU.add), reads=crd + ["cacc"], writes=["cacc"])
                        S.add("act", lambda e, name=name, ts=ts: e.activation(out=qk[name][:, ts], in_=cacc[:], func=AF.Silu),
                              reads=["cacc"], writes=[("qk", name)])
        wv = io["wv"].rearrange("(kc p) n -> p kc n", p=128)
        for (sname, c0, c1) in ([V_SEGS[int(i)] for i in _os.environ.get('PA_SEGS', '0123456')] if _stop >= 2 else []):
            n = c1 - c0
            s = load_w(wv[:, :, c0:c1], n)
            for tb in range(16):
                tsl = slice(tb * 128, (tb + 1) * 128)
                b = rot.get()
                for kc in range(16):
                    S.add("pe", lambda e, s=s, kc=kc, b=b, tsl=tsl, n=n: e.matmul(
                        pb[b][:, 0:n], lhsT=xT[:, kc, tsl], rhs=wst[s][:, kc, 0:n],
                        start=(kc == 0), stop=(kc == 15)),
                        reads=[("wst", s), ("xT", kc)], writes=[("ps", b)])
                rd = [("ps", b)]
                if sname == "Av":
                    S.add("dve", lambda e, b=b, tb=tb: e.tensor_copy(
                        out=vA[:, tb, :, 0:128], in_=pb[b][:, 0:256].rearrange("p (h d) -> p h d", h=2)),
                        reads=rd, writes=[("vA", tb)])
                elif sname == "Bv01":
                    S.add("act", lambda e, b=b, tb=tb: e.copy(
                        out=vB[:, tb, 0:2, 0:128], in_=pb[b][:, 0:256].rearrange("p (h d) -> p h d", h=2)),
                        reads=rd, writes=[("vB", tb)])
                elif sname == "Bv2":
                    S.add("dve", lambda e, b=b, tb=tb: e.tensor_copy(out=vB[:, tb, 2, 0:128], in_=pb[b][:, 0:128]),
                          reads=rd, writes=[("vB", tb)])
                elif sname in ("Cv0", "Cv1"):
                    h = int(sname[2])
                    S.add("act", lambda e, b=b, tb=tb, h=h: e.copy(out=vC[:, tb, h, 0:192], in_=pb[b][:, 0:192]),
                          reads=rd, writes=[("vC", tb)])
                elif sname == "Coa":
                    S.add("act", lambda e, b=b, tb=tb: e.activation(out=og[:, tb, 0:256], in_=pb[b][:, 0:256], func=AF.Sigmoid),
                          reads=rd, writes=[("og", tb)])
                else:
                    S.add("act", lambda e, b=b, tb=tb: e.activation(out=og[:, tb, 256:384], in_=pb[b][:, 0:128], func=AF.Sigmoid),
                          reads=rd, writes=[("og", tb), ("psrd", b)])
                    S.add("act", lambda e, b=b, tb=tb: e.copy(out=graw[:, tb, :], in_=pb[b][:, 128:132]),
                          reads=rd + [("psrd", b)], writes=["graw"])
        S.barrier()
    yT = es.enter_context(nc.sbuf_tensor("yTa", [128, 8, SEQ], BF16))
    with ExitStack() as ms:
        T = dict(qk=qk, vA=vA, vB=vB, vC=vC, og=og, graw=graw, spm=spm, ident=ident, tri=tri, ones=ones, yT=yT,
                 GB0=GB0, SL0=SL0, NC0=NC0, LM0=LM0, lam_init=lam_init)
        T["sc"] = [ms.enter_context(nc.psum_tensor(f"sc{i}", [128, 1024], F32)) for i in range(2)]
        T["acc"] = [ms.enter_context(nc.psum_tensor(f"acc{i}", [128, 512], F32)) for i in range(2)]
        T["tr"] = ms.enter_context(nc.psum_tensor("trp", [128, 1024], BF16))
        T["misc"] = ms.enter_context(nc.psum_tensor("miscp", [128, 512], F32))
        T["ytok"] = ms.enter_context(nc.sbuf_tensor("ytok", [128, 16, 128], BF16))
        T["et"] = [ms.enter_context(nc.sbuf_tensor(f"et{i}", [128, 1024], BF16)) for i in range(2)]
        T["sm"] = ms.enter_context(nc.sbuf_tensor("smallA", [128, 64], F32))
        import os as _os
        _mix = _os.environ.get("PA_MIX", "abc")
        if "a" in _mix:
            with ExitStack() as sa:
                emit_mixer_a(S, nc, sa, io, T)
                S.barrier()
        if "b" in _mix:
            with ExitStack() as sb:
                emit_mixer_b(S, nc, sb, io, T)
                S.barrier()
        split = io.get("ysendAB") is not None

        def emit_select(stk, c0, c1, d_own, d_send, tag):
            n = c1 - c0
            sel = stk.enter_context(nc.sbuf_tensor("selA" + tag, [128, 2], F32))
            tO = stk.enter_context(nc.sbuf_tensor("tOwn" + tag, [128, n, 1024], BF16))
            tP = stk.enter_context(nc.sbuf_tensor("tPart" + tag, [128, n, 1024], BF16))
            S.add("sp", lambda e: e.dma_start(out=sel[:], in_=io["sel"]), writes=["selA"], dma_group="selA" + tag)
            ry = [("yT", c) for c in range(c0, c1)] + ["selA"]
            S.add("dve", lambda e: e.tensor_scalar(out=tO[:], in0=yT[:, c0:c1, 0:1024], scalar1=sel[:, 0:1], scalar2=None, op0=ALU.mult),
                  reads=ry, writes=["tOwn"])
            S.add("dve", lambda e: e.scalar_tensor_tensor(out=tO[:], in0=yT[:, c0:c1, 1024:2048], scalar=sel[:, 1:2], in1=tO[:], op0=ALU.mult, op1=ALU.add),
                  reads=ry + ["tOwn"], writes=["tOwn"])
            S.add("dve", lambda e: e.tensor_scalar(out=tP[:], in0=yT[:, c0:c1, 0:1024], scalar1=sel[:, 1:2], scalar2=None, op0=ALU.mult),
                  reads=ry, writes=["tPart"])
            S.add("dve", lambda e: e.scalar_tensor_tensor(out=tP[:], in0=yT[:, c0:c1, 1024:2048], scalar=sel[:, 0:1], in1=tP[:], op0=ALU.mult, op1=ALU.add),
                  reads=ry + ["tPart"], writes=["tPart"])
            S.add("sp", lambda e: e.dma_start(out=d_own.rearrange("(c p) t -> p c t", p=128), in_=tO[:]),
                  reads=["tOwn"], writes=["yown" + tag], dma_group="yown" + tag)
            S.add("sp", lambda e: e.dma_start(out=d_send.rearrange("(c p) t -> p c t", p=128), in_=tP[:]),
                  reads=["tPart"], writes=["ysend" + tag], dma_group="ysend" + tag)

        if split:
            with ExitStack() as sx:
                emit_select(sx, 0, 5, io["yownAB"], io["ysendAB"], "AB")
                S.barrier()
            io["cc_ab"]()
        if "c" in _mix:
            with ExitStack() as sc_:
                emit_mixer_c(S, nc, sc_, io, T)
                S.barrier()
        if io.get("ysend") is None:
            for c in range(8):
                S.add("sp", lambda e, c=c: e.dma_start(out=io["yT"][c * 128:(c + 1) * 128, :], in_=yT[:, c, :]),
                      reads=[("yT", c)], dma_group=("yo", c % 4))
        else:
            if split:
                emit_select(ms, 5, 8, io["yown"], io["ysend"], "C")
            else:
                emit_select(ms, 0, 8, io["yown"], io["ysend"], "")
        S.barrier()


def emit_transposes_to_yT(S, T, src, nblk_feat, chunk0, tag):
    tr, ident, yT = T["tr"], T["ident"], T["yT"]
    for f in range(nblk_feat):
        for half in range(2):
            for t8 in range(8):
                tb = half * 8 + t8
                S.add("pe", lambda e, f=f, tb=tb, t8=t8: e.transpose(
                    out=tr[:, t8 * 128:(t8 + 1) * 128], in_=src[:, tb, f * 128:(f + 1) * 128], identity=ident[:]),
                    reads=[(tag, tb), "ident"], writes=["tr"])
            eng = "dve" if (f + half) % 2 == 0 else "act"
            if eng == "dve":
                S.add("dve", lambda e, f=f, half=half: e.tensor_copy(
                    out=yT[:, chunk0 + f, half * 1024:(half + 1) * 1024], in_=tr[:]),
                    reads=["tr"], writes=[("yT", chunk0 + f)])
            else:
                S.add("act", lambda e, f=f, half=half: e.copy(
                    out=yT[:, chunk0 + f, half * 1024:(half + 1) * 1024], in_=tr[:]),
                    reads=["tr"], writes=[("yT", chunk0 + f)])


def emit_mixer_a(S, nc, sa, io, T):
    qk, vA, spm, ident, sm = T["qk"], T["vA"], T["spm"], T["ident"], T["sm"]
    sc, acc, et, ytok = T["sc"], T["acc"], T["et"], T["ytok"]
    SL0, LM0, lam_init = T["SL0"], T["LM0"], T["lam_init"]
    kaug = sa.enter_context(nc.sbuf_tensor("kaug_sb", [4, 2, SEQ], BF16))
    qaug = sa.enter_context(nc.sbuf_tensor("qaug_sb", [4, 2, SEQ], BF16))
    diag = sa.enter_context(nc.sbuf_tensor("diagA", [128, 2, 128], BF16))
    accA = sa.enter_context(nc.sbuf_tensor("accA", [128, 16, 2, 129], F32))
    rD = sa.enter_context(nc.sbuf_tensor("rDA", [128, 16, 2], F32))
    ss = sa.enter_context(nc.sbuf_tensor("ssA", [128, 16], F32))
    lt = sa.enter_context(nc.sbuf_tensor("ltA", [128, 64], F32))
    S.add("pool", lambda e: e.dma_start(out=kaug[:], in_=io["kaug"]), writes=["kaug"], dma_group="augk")
    S.add("pool", lambda e: e.dma_start(out=qaug[:], in_=io["qaug"]), writes=["qaug"], dma_group="augq")
    S.add("pool", lambda e: e.dma_start(out=diag[:], in_=io["diag"]), writes=["diag"], dma_group="augd")
    for i in range(2):
        S.add("dve", lambda e, i=i: e.tensor_tensor(out=lt[:], in0=spm[:, LM0 + 128 * i:LM0 + 128 * i + 64],
                                                    in1=spm[:, LM0 + 128 * i + 64:LM0 + 128 * i + 128], op=ALU.mult),
              reads=["spm"], writes=["ltA"])
        S.add("dve", lambda e, i=i: e.reduce_sum(out=sm[:, 1 + i:2 + i], in_=lt[:], axis=AX.X),
              reads=["ltA"], writes=[("sm", 1 + i)])
    S.add("act", lambda e: e.activation(out=sm[:, 1:3], in_=sm[:, 1:3], func=AF.Exp),
          reads=[("sm", 1), ("sm", 2)], writes=[("sm", 1), ("sm", 2)])
    S.add("dve", lambda e: e.tensor_tensor(out=sm[:, 0:1], in0=sm[:, 1:2], in1=sm[:, 2:3], op=ALU.subtract),
          reads=[("sm", 1), ("sm", 2)], writes=[("sm", 0)])
    S.add("dve", lambda e: e.tensor_scalar(out=sm[:, 0:1], in0=sm[:, 0:1], scalar1=lam_init, scalar2=None, op0=ALU.add),
          reads=[("sm", 0)], writes=[("sm", 0)])
    def mk_scores(hl, qb, m, g0, g1, w):
        qT, kT = qk["Aq%d" % hl], qk["Ak%d" % hl]
        qs = slice(qb * 128, (qb + 1) * 128)
        pp = slice(m * 64, (m + 1) * 64)

        def fn():
            for kb in range(g0, g1):
                cs = slice((kb - g0) * 128, (kb - g0 + 1) * 128)
                ks = slice(kb * 128, (kb + 1) * 128)
                S.add("pe", lambda e, cs=cs, ks=ks: e.matmul(sc[w][:, cs], lhsT=kT[pp, ks], rhs=qT[pp, qs], start=True, stop=False),
                      reads=[("qk", "Aq%d" % hl), ("qk", "Ak%d" % hl)], writes=[("sc", w)])
                if kb < qb:
                    S.add("pe", lambda e, cs=cs, ks=ks: e.matmul(sc[w][:, cs], lhsT=kaug[:, hl, ks], rhs=qaug[:, hl, qs], start=False, stop=True),
                          reads=["kaug", "qaug"], writes=[("sc", w)])
                else:
                    S.add("pe", lambda e, cs=cs: e.matmul(sc[w][:, cs], lhsT=ident[:], rhs=diag[:, hl, :], start=False, stop=True),
                          reads=["ident", "diag"], writes=[("sc", w)])
            ncol = (g1 - g0) * 128
            S.add("act", lambda e: e.activation(out=et[w][:, 0:ncol], in_=sc[w][:, 0:ncol], func=AF.Exp),
                  reads=[("sc", w)], writes=[("et", w)])
        return fn

    def mk_pv(hl, qb, m, g0, g1, w):
        def fn():
            for kb in range(g0, g1):
                cs = slice((kb - g0) * 128, (kb - g0 + 1) * 128)
                S.add("pe", lambda e, cs=cs, kb=kb: e.matmul(acc[m][:, 0:129], lhsT=et[w][:, cs], rhs=vA[:, kb, hl, :],
                                                              start=(kb == 0), stop=(kb == qb)),
                      reads=[("et", w), ("vA", kb), "vA1"], writes=[("acc", m)])
            if g1 == qb + 1:
                if m == 0:
                    S.add("act", lambda e: e.copy(out=accA[:, qb, 0, :], in_=acc[0][:, 0:129]),
                          reads=[("acc", 0)], writes=[("accA", qb)])
                else:
                    S.add("dve", lambda e: e.tensor_copy(out=accA[:, qb, 1, :], in_=acc[1][:, 0:129]),
                          reads=[("acc", 1)], writes=[("accA", qb)])
        return fn

    def epilogue(hl):
        allq = [("accA", qb) for qb in range(16)]
        bc = lambda ap: ap.to_broadcast([128, 16, 128])
        S.add("dve", lambda e: e.reciprocal(out=rD[:], in_=accA[:, :, :, 128]), reads=allq, writes=["rD"])
        S.add("dve", lambda e: e.tensor_scalar(out=rD[:, :, 1], in0=rD[:, :, 1], scalar1=sm[:, 0:1], scalar2=-1.0, op0=ALU.mult, op1=ALU.mult),
              reads=["rD", ("sm", 0)], writes=["rD"])
        S.add("dve", lambda e: e.tensor_tensor(out=accA[:, :, 0, 0:128], in0=accA[:, :, 0, 0:128], in1=bc(rD[:, :, 0:1]), op=ALU.mult),
              reads=allq + ["rD"], writes=allq)
        S.add("dve", lambda e: e.tensor_tensor(out=accA[:, :, 1, 0:128], in0=accA[:, :, 1, 0:128], in1=bc(rD[:, :, 1:2]), op=ALU.mult),
              reads=allq + ["rD"], writes=allq)
        S.add("dve", lambda e: e.tensor_tensor(out=accA[:, :, 0, 0:128], in0=accA[:, :, 0, 0:128], in1=accA[:, :, 1, 0:128], op=ALU.add),
              reads=allq, writes=allq)
        S.add("pool", lambda e: e.tensor_tensor(out=accA[:, :, 1, 0:128], in0=accA[:, :, 0, 0:128], in1=accA[:, :, 0, 0:128], op=ALU.mult),
              reads=allq, writes=allq)
        S.add("dve", lambda e: e.reduce_sum(out=ss[:], in_=accA[:, :, 1, 0:128], axis=AX.X), reads=allq, writes=["ssA"])
        S.add("dve", lambda e: e.tensor_scalar(out=ss[:], in0=ss[:], scalar1=1.0 / 128, scalar2=1e-6, op0=ALU.mult, op1=ALU.add),
              reads=["ssA"], writes=["ssA"])
        S.add("act", lambda e: e.activation(out=ss[:], in_=ss[:], func=AF.Sqrt), reads=["ssA"], writes=["ssA"])
        S.add("dve", lambda e: e.reciprocal(out=ss[:], in_=ss[:]), reads=["ssA"], writes=["ssA"])
        S.add("dve", lambda e: e.tensor_scalar(out=ss[:], in0=ss[:], scalar1=1.0 - lam_init, scalar2=None, op0=ALU.mult),
              reads=["ssA"], writes=["ssA"])
        S.add("dve", lambda e: e.tensor_tensor(out=accA[:, :, 0, 0:128], in0=accA[:, :, 0, 0:128], in1=bc(ss[:].unsqueeze(2)), op=ALU.mult),
              reads=allq + ["ssA"], writes=allq)
        S.add("dve", lambda e: e.tensor_tensor(out=ytok[:], in0=accA[:, :, 0, 0:128], in1=bc(spm[:, SL0:SL0 + 128].unsqueeze(1)), op=ALU.mult),
              reads=allq + ["spm"], writes=[("ytok", qb) for qb in range(16)])
        emit_transposes_to_yT(S, T, ytok, 1, hl, "ytok")

    steps = []
    for hl in range(2):
        for qb in range(16):
            for m in range(2):
                for g0 in range(0, qb + 1, 8):
                    g1 = min(qb + 1, g0 + 8)
                    w = len(steps) % 2
                    last = (qb == 15 and m == 1 and g1 == qb + 1)
                    steps.append((mk_scores(hl, qb, m, g0, g1, w), mk_pv(hl, qb, m, g0, g1, w), hl if last else None))
    for i in range(len(steps) + 1):
        if i < len(steps):
            steps[i][0]()
        if i >= 1:
            steps[i - 1][1]()
            if steps[i - 1][2] is not None:
                epilogue(steps[i - 1][2])


def emit_mixer_b(S, nc, sb, io, T):
    qk, vB, ident, sm = T["qk"], T["vB"], T["ident"], T["sm"]
    sc, acc, et, ytok = T["sc"], T["acc"], T["et"], T["ytok"]
    tf = sb.enter_context(nc.sbuf_tensor("toepf", [128, 15, 128], F32))
    thi = sb.enter_context(nc.sbuf_tensor("toephi", [128, 15, 128], BF16))
    tlo = sb.enter_context(nc.sbuf_tensor("toeplo", [128, 15, 128], BF16))
    tf2 = sb.enter_context(nc.sbuf_tensor("toepf2", [128, 15, 128], F32))
    accB = sb.enter_context(nc.sbuf_tensor("accB", [128, 16, 129], F32))
    rB = sb.enter_context(nc.sbuf_tensor("rBB", [128, 16], F32))
    S.add("sp", lambda e: e.dma_start(out=tf[:], in_=io["toep"]), writes=["toepf"], dma_group="toep")
    S.add("dve", lambda e: e.tensor_copy(out=thi[:], in_=tf[:]), reads=["toepf"], writes=["thi"])
    S.add("dve", lambda e: e.tensor_copy(out=tf2[:], in_=thi[:]), reads=["thi"], writes=["toepf2"])
    S.add("dve", lambda e: e.tensor_tensor(out=tf2[:], in0=tf[:], in1=tf2[:], op=ALU.subtract),
          reads=["toepf", "toepf2"], writes=["toepf2"])
    S.add("dve", lambda e: e.tensor_copy(out=tlo[:], in_=tf2[:]), reads=["toepf2"], writes=["tlo"])
    def mk_scores(hl, qb, w):
        qT, kT = qk["Bq%d" % hl], qk["Bk%d" % hl]
        qs = slice(qb * 128, (qb + 1) * 128)
        js = [j for j in range(5) if qb - 4 + j >= 0]

        def fn():
            for j in js:
                kb = qb - 4 + j
                cs = slice(j * 128, (j + 1) * 128)
                ks = slice(kb * 128, (kb + 1) * 128)
                S.add("pe", lambda e, cs=cs, ks=ks: e.matmul(sc[w][:, cs], lhsT=kT[:, ks], rhs=qT[:, qs], start=True, stop=False),
                      reads=[("qk", "Bq%d" % hl), ("qk", "Bk%d" % hl)], writes=[("sc", w)])
                S.add("pe", lambda e, cs=cs, j=j: e.matmul(sc[w][:, cs], lhsT=ident[:], rhs=thi[:, hl * 5 + j, :], start=False, stop=False),
                      reads=["ident", "thi"], writes=[("sc", w)])
                S.add("pe", lambda e, cs=cs, j=j: e.matmul(sc[w][:, cs], lhsT=ident[:], rhs=tlo[:, hl * 5 + j, :], start=False, stop=True),
                      reads=["ident", "tlo"], writes=[("sc", w)])
            c0, c1 = js[0] * 128, 640
            S.add("act", lambda e: e.activation(out=et[w][:, c0:c1], in_=sc[w][:, c0:c1], func=AF.Exp),
                  reads=[("sc", w)], writes=[("et", w)])
        return fn

    def mk_pv(hl, qb, w):
        js = [j for j in range(5) if qb - 4 + j >= 0]

        def fn():
            for j in js:
                kb = qb - 4 + j
                cs = slice(j * 128, (j + 1) * 128)
                S.add("pe", lambda e, cs=cs, kb=kb, j=j: e.matmul(acc[w][:, 0:129], lhsT=et[w][:, cs], rhs=vB[:, kb, hl, :],
                                                                   start=(j == js[0]), stop=(j == 4)),
                      reads=[("et", w), ("vB", kb), "vB1"], writes=[("acc", w)])
            if qb % 2 == 0:
                S.add("act", lambda e: e.copy(out=accB[:, qb, :], in_=acc[w][:, 0:129]),
                      reads=[("acc", w)], writes=[("accB", qb)])
            else:
                S.add("dve", lambda e: e.tensor_copy(out=accB[:, qb, :], in_=acc[w][:, 0:129]),
                      reads=[("acc", w)], writes=[("accB", qb)])
        return fn

    def epilogue(hl):
        allq = [("accB", qb) for qb in range(16)]
        S.add("dve", lambda e: e.reciprocal(out=rB[:], in_=accB[:, :, 128]), reads=allq, writes=["rB"])
        S.add("dve", lambda e: e.tensor_tensor(out=ytok[:], in0=accB[:, :, 0:128], in1=rB[:].unsqueeze(2).to_broadcast([128, 16, 128]), op=ALU.mult),
              reads=allq + ["rB"], writes=[("ytok", qb) for qb in range(16)])
        emit_transposes_to_yT(S, T, ytok, 1, 2 + hl, "ytok")

    steps = []
    for hl in range(3):
        for qb in range(16):
            w = len(steps) % 2
            steps.append((mk_scores(hl, qb, w), mk_pv(hl, qb, w), hl if qb == 15 else None))
    for i in range(len(steps) + 1):
        if i < len(steps):
            steps[i][0]()
        if i >= 1:
            steps[i - 1][1]()
            if steps[i - 1][2] is not None:
                epilogue(steps[i - 1][2])


def emit_mixer_c(S, nc, sc_, io, T):
    qk, vC, og, graw, spm, ident, tri, ones, sm = (T["qk"], T["vC"], T["og"], T["graw"], T["spm"], T["ident"],
                                                     T["tri"], T["ones"], T["sm"])
    sc, acc, misc, tr = T["sc"], T["acc"], T["misc"], T["tr"]
    GB0, NC0 = T["GB0"], T["NC0"]
    g2 = sc_.enter_context(nc.sbuf_tensor("g2", [128, 16, 4], F32))
    lf = sc_.enter_context(nc.sbuf_tensor("lf", [128, 16, 2], F32))
    eq = sc_.enter_context(nc.sbuf_tensor("eq", [128, 16, 2], F32))
    ek = sc_.enter_context(nc.sbuf_tensor("ek", [128, 16, 2], F32))
    dec = sc_.enter_context(nc.sbuf_tensor("dec", [128, 16, 2], F32))
    ktok = sc_.enter_context(nc.sbuf_tensor("ktok", [128, 16, 3, 128], BF16))
    v2 = sc_.enter_context(nc.sbuf_tensor("v2", [128, 16, 2, 193], BF16))
    swT = [sc_.enter_context(nc.sbuf_tensor(f"swT{i}", [128, 128], BF16)) for i in range(2)]
    CTa = sc_.enter_context(nc.sbuf_tensor("CTa", [128, 2, 193], F32))
    CTb = sc_.enter_context(nc.sbuf_tensor("CTb", [128, 193], F32))
    CTa16 = sc_.enter_context(nc.sbuf_tensor("CTa16", [128, 2, 193], BF16))
    CTb16 = sc_.enter_context(nc.sbuf_tensor("CTb16", [128, 193], BF16))
    hbuf = sc_.enter_context(nc.sbuf_tensor("hbuf", [128, 16, 2, 193], F32))
    ssc = sc_.enter_context(nc.sbuf_tensor("ssC", [128, 16, 2], F32))
    ss2 = sc_.enter_context(nc.sbuf_tensor("ss2C", [128, 16, 2], F32))
    tq = [sc_.enter_context(nc.sbuf_tensor(f"tqC{i}", [128, 2, 192], F32)) for i in range(1)]
    ytc = og
    S.add("dve", lambda e: e.tensor_tensor(out=g2[:], in0=graw[:], in1=spm[:, GB0:GB0 + 4].unsqueeze(1).to_broadcast([128, 16, 4]),
                                           op=ALU.add), reads=["graw", "spm"], writes=["g2"])
    S.add("act", lambda e: e.activation(out=lf[:], in_=g2[:, :, 2:4], func=AF.Exp, scale=-1.0), reads=["g2"], writes=["lf"])
    S.add("act", lambda e: e.activation(out=lf[:], in_=lf[:], func=AF.Ln, bias=1.0), reads=["lf"], writes=["lf"])
    S.add("dve", lambda e: e.tensor_scalar(out=lf[:], in0=lf[:], scalar1=-1.0, scalar2=None, op0=ALU.mult), reads=["lf"], writes=["lf"])
    lf2 = lf[:].rearrange("p a b -> p (a b)")
    S.add("pe", lambda e: e.matmul(misc[:, 0:32], lhsT=tri[:], rhs=lf2, start=True, stop=True),
          reads=["tri", "lf"], writes=["misc"])
    S.add("pe", lambda e: e.matmul(misc[:, 32:64], lhsT=ones[:], rhs=lf2, start=True, stop=True),
          reads=["onesA", "lf"], writes=["misc"])
    S.add("act", lambda e: e.activation(out=eq[:].rearrange("p a b -> p (a b)"), in_=misc[:, 0:32], func=AF.Exp),
          reads=["misc"], writes=["eq", "miscrd"])
    S.add("act", lambda e: e.activation(out=dec[:].rearrange("p a b -> p (a b)"), in_=misc[:, 32:64], func=AF.Exp),
          reads=["misc"], writes=["dec", "miscrd2"])
    S.add("act", lambda e: e.activation(out=ek[:].rearrange("p a b -> p (a b)"), in_=misc[:, 0:32], func=AF.Identity, scale=-1.0),
          reads=["misc"], writes=["ek"])
    S.add("dve", lambda e: e.tensor_tensor(out=ek[:], in0=g2[:, :, 0:2], in1=ek[:], op=ALU.add), reads=["g2", "ek"], writes=["ek"])
    import math
    S.add("act", lambda e: e.activation(out=ek[:], in_=ek[:], func=AF.Exp, bias=math.log(CH ** -0.5)), reads=["ek"], writes=["ek"])
    S.add("dve", lambda e: e.tensor_tensor(out=v2[:], in0=vC[:], in1=ek[:].unsqueeze(3).to_broadcast([128, 16, 2, 193]), op=ALU.mult),
          reads=[("vC", tb) for tb in range(16)] + ["vC1", "ek"], writes=["v2"])
    ksrc = [qk["Ck0a"], qk["Ck1a"], qk["Ckb"]]
    for t2 in range(8):
        for i in range(2):
            tb = t2 * 2 + i
            tsl = slice(tb * 128, (tb + 1) * 128)
            for c3 in range(3):
                S.add("pe", lambda e, i=i, c3=c3, tsl=tsl: e.transpose(out=tr[:, (i * 3 + c3) * 128:(i * 3 + c3 + 1) * 128],
                                                                        in_=ksrc[c3][:, tsl], identity=ident[:]),
                      reads=[("qk", "Ck0a"), ("qk", "Ck1a"), ("qk", "Ckb"), "ident"], writes=["tr"])
        S.add("dve", lambda e, t2=t2: e.tensor_copy(out=ktok[:, t2 * 2:t2 * 2 + 2, :, :],
                                                    in_=tr[:, 0:768].rearrange("p (a c d) -> p a c d", a=2, c=3)),
              reads=["tr"], writes=["ktok"])
    S.add("pool", lambda e: e.memset(CTa[:], 0.0), writes=["CTa"])
    S.add("pool", lambda e: e.memset(CTb[:], 0.0), writes=[("CTb", 0), ("CTb", 1)])
    def mk_c(tb, hl, w):
        tsl = slice(tb * 128, (tb + 1) * 128)
        hs = slice(64 * hl, 64 * hl + 64)
        qa, qbb = qk["Cq%da" % hl], qk["Cqb"]
        ka, kbb = qk["Ck%da" % hl], qk["Ckb"]
        rq = [("qk", "Cq%da" % hl), ("qk", "Cqb")]
        rk = [("qk", "Ck%da" % hl), ("qk", "Ckb")]

        def stage1():
            S.add("pe", lambda e, w=w, ka=ka, qa=qa, tsl=tsl: e.matmul(sc[w][:, 0:128], lhsT=ka[:, tsl], rhs=qa[:, tsl], start=True, stop=False),
                  reads=rq + rk, writes=[("sc", w)])
            S.add("pe", lambda e, w=w, kbb=kbb, qbb=qbb, tsl=tsl, hs=hs: e.matmul(sc[w][:, 0:128], lhsT=kbb[hs, tsl], rhs=qbb[hs, tsl], start=False, stop=True),
                  reads=rq + rk, writes=[("sc", w)])
            S.add("dve", lambda e, w=w, tb=tb, hl=hl: e.scalar_tensor_tensor(out=swT[w][:], in0=sc[w][:, 0:128], scalar=ek[:, tb, hl:hl + 1],
                                                                             in1=tri[:], op0=ALU.mult, op1=ALU.mult),
                  reads=[("sc", w), "ek", "tri"], writes=[("swT", w)])

        def stage2():
            S.add("pe", lambda e, w=w, tb=tb, hl=hl: e.matmul(acc[hl][:, 0:193], lhsT=swT[w][:], rhs=vC[:, tb, hl, :], start=True, stop=(tb == 0)),
                  reads=[("swT", w), ("vC", tb), "vC1"], writes=[("acc", hl)])
            if tb > 0:
                S.add("pe", lambda e, hl=hl, qa=qa, tsl=tsl: e.matmul(acc[hl][:, 0:193], lhsT=qa[:, tsl], rhs=CTa16[:, hl, :], start=False, stop=False),
                      reads=rq + ["CTa16"], writes=[("acc", hl)])
                S.add("pe", lambda e, hl=hl, qbb=qbb, tsl=tsl, hs=hs: e.matmul(acc[hl][:, 0:193], lhsT=qbb[hs, tsl], rhs=CTb16[hs, :], start=False, stop=True),
                      reads=rq + [("CTb16", hl)], writes=[("acc", hl)])
            if tb < 15:
                S.add("pe", lambda e, tb=tb, hl=hl: e.matmul(misc[:, 0:193], lhsT=ktok[:, tb, hl, :], rhs=v2[:, tb, hl, :], start=True, stop=True),
                      reads=["ktok", "v2"], writes=["misc"])
                S.add("pe", lambda e, tb=tb, hl=hl: e.matmul(misc[:, 256:449], lhsT=ktok[:, tb, 2, :], rhs=v2[:, tb, hl, :], start=True, stop=True),
                      reads=["ktok", "v2"], writes=["misc"])
                S.add("dve", lambda e, hl=hl: e.tensor_tensor(out=CTa[:, hl, :], in0=CTa[:, hl, :], in1=misc[:, 0:193], op=ALU.add),
                      reads=["misc", "CTa"], writes=["CTa"])
                S.add("dve", lambda e, hs=hs: e.tensor_tensor(out=CTb[hs, :], in0=CTb[hs, :], in1=misc[hs, 256:449], op=ALU.add),
                      reads=["misc", ("CTb", hl)], writes=[("CTb", hl)])
                S.add("dve", lambda e, hl=hl, tb=tb: e.tensor_scalar(out=CTa[:, hl, :], in0=CTa[:, hl, :], scalar1=dec[:, tb, hl:hl + 1], scalar2=None, op0=ALU.mult),
                      reads=["CTa", "dec"], writes=["CTa"])
                S.add("dve", lambda e, hl=hl, tb=tb, hs=hs: e.tensor_scalar(out=CTb[hs, :], in0=CTb[hs, :], scalar1=dec[hs, tb, hl:hl + 1], scalar2=None, op0=ALU.mult),
                      reads=[("CTb", hl), "dec"], writes=[("CTb", hl)])
                S.add("act", lambda e, hl=hl: e.copy(out=CTa16[:, hl, :], in_=CTa[:, hl, :]), reads=["CTa"], writes=["CTa16"])
                S.add("act", lambda e, hs=hs: e.copy(out=CTb16[hs, :], in_=CTb[hs, :]), reads=[("CTb", hl)], writes=[("CTb16", hl)])
            S.add("act", lambda e, hl=hl, tb=tb: e.activation(out=hbuf[:, tb, hl, :], in_=acc[hl][:, 0:193], func=AF.Identity,
                                                              scale=eq[:, tb, hl:hl + 1]),
                  reads=[("acc", hl), "eq"], writes=[("hbuf", tb)])

        return stage1, stage2

    csteps = []
    for tb in range(16):
        for hl in range(2):
            csteps.append(mk_c(tb, hl, len(csteps) % 2))
    for i in range(len(csteps) + 1):
        if i < len(csteps):
            csteps[i][0]()
        if i >= 1:
            csteps[i - 1][1]()
    allh = [("hbuf", tb) for tb in range(16)]
    bch = lambda ap: ap.to_broadcast([128, 16, 2, 192])
    S.add("act", lambda e: e.activation(out=ssc[:], in_=hbuf[:, :, :, 192], func=AF.Abs), reads=allh, writes=["ssC"])
    S.add("dve", lambda e: e.tensor_scalar(out=ssc[:], in0=ssc[:], scalar1=1.0, scalar2=None, op0=ALU.max), reads=["ssC"], writes=["ssC"])
    S.add("dve", lambda e: e.reciprocal(out=ssc[:], in_=ssc[:]), reads=["ssC"], writes=["ssC"])
    S.add("dve", lambda e: e.tensor_tensor(out=hbuf[:, :, :, 0:192], in0=hbuf[:, :, :, 0:192], in1=bch(ssc[:].unsqueeze(3)), op=ALU.mult),
          reads=allh + ["ssC"], writes=allh)
    for tb in range(16):
        q = 0
        S.add("pool", lambda e, tb=tb, q=q: e.tensor_tensor(out=tq[q][:], in0=hbuf[:, tb, :, 0:192], in1=hbuf[:, tb, :, 0:192], op=ALU.mult),
              reads=[("hbuf", tb)], writes=[("tqC", q)])
        S.add("dve", lambda e, tb=tb, q=q: e.reduce_sum(out=ss2[:, tb, :], in_=tq[q][:], axis=AX.X), reads=[("tqC", q)], writes=["ss2"])
    S.add("dve", lambda e: e.tensor_scalar(out=ss2[:], in0=ss2[:], scalar1=1.0 / CH, scalar2=1e-6, op0=ALU.mult, op1=ALU.add), reads=["ss2"], writes=["ss2"])
    S.add("act", lambda e: e.activation(out=ss2[:], in_=ss2[:], func=AF.Sqrt), reads=["ss2"], writes=["ss2"])
    S.add("dve", lambda e: e.reciprocal(out=ss2[:], in_=ss2[:]), reads=["ss2"], writes=["ss2"])
    S.add("dve", lambda e: e.tensor_tensor(out=hbuf[:, :, :, 0:192], in0=hbuf[:, :, :, 0:192], in1=bch(ss2[:].unsqueeze(3)), op=ALU.mult),
          reads=allh + ["ss2"], writes=allh)
    for half in range(2):
        hsl = slice(half * 8, half * 8 + 8)
        kk = [("hbuf", tb) for tb in range(half * 8, half * 8 + 8)]
        S.add("dve", lambda e, hsl=hsl: e.tensor_tensor(out=hbuf[:, hsl, :, 0:192], in0=hbuf[:, hsl, :, 0:192],
                                                        in1=spm[:, NC0:NC0 + 192].unsqueeze(1).unsqueeze(1).to_broadcast([128, 8, 2, 192]), op=ALU.mult),
              reads=kk + ["spm"], writes=kk)
        S.add("pool", lambda e, hsl=hsl: e.tensor_tensor(out=og[:, hsl, :].rearrange("p t (h d) -> p t h d", h=2), in0=hbuf[:, hsl, :, 0:192],
                                                         in1=og[:, hsl, :].rearrange("p t (h d) -> p t h d", h=2), op=ALU.mult),
              reads=kk + [("og", tb) for tb in range(half * 8, half * 8 + 8)],
              writes=[("ytokC", tb) for tb in range(half * 8, half * 8 + 8)] + [("og", tb) for tb in range(half * 8, half * 8 + 8)])
    emit_transposes_to_yT(S, T, ytc, 3, 5, "ytokC")


def build_phase_a(li):
    nc = bass.Bass("TRN2", target_bir_lowering=False)
    def din(name, shape, dt=F32):
        return nc.dram_tensor(name, shape, dt, kind="ExternalInput").ap()
    io = {
        "xT": din("xT", [D, SEQ]),
        "wqk": din("wqk", [D, NQKB * 128]),
        "wv": din("wv", [D, NV]),
        "spm": din("spm", [128, 612]),
        "ident": din("ident", [128, 128]),
        "tri": din("tri", [128, 128]),
        "kaug": din("kaug", [4, 2, SEQ]),
        "qaug": din("qaug", [4, 2, SEQ]),
        "diag": din("diag", [128, 2, 128]),
        "toep": din("toep", [128, 15, 128]),
        "yT": nc.dram_tensor("yT", [1024, SEQ], BF16, kind="ExternalOutput").ap(),
    }
    S = Sched(nc)
    with ExitStack() as es:
        emit_phase_a(S, nc, es, io, li)
        S.emit()
    return nc


def _qk_cols(r):
    cols = np.full((NQKB, 128), -1, np.int64)
    bi = 0
    for hl in range(2):
        h = 2 * r + hl
        cols[bi] = QA + h * 128 + np.arange(128); bi += 1
        cols[bi] = KA + h * 128 + np.arange(128); bi += 1
    for hl in range(3):
        h = 3 * r + hl
        cols[bi] = QB + h * 128 + np.arange(128); bi += 1
        cols[bi] = KB + h * 128 + np.arange(128); bi += 1
    for base in (QC, KC):
        for hl in range(2):
            cols[bi] = base + (2 * r + hl) * CH + np.arange(128); bi += 1
        for hl in range(2):
            cols[bi, 64 * hl:64 * hl + 64] = base + (2 * r + hl) * CH + 128 + np.arange(64)
        bi += 1
    return cols.reshape(-1)


def _v_cols(r):
    return np.concatenate([
        VA_ + 2 * r * 128 + np.arange(256), VB_ + 3 * r * 128 + np.arange(384),
        VC_ + 2 * r * CH + np.arange(384), OC + 2 * r * CH + np.arange(384),
        np.array([IC + 2 * r, IC + 2 * r + 1, FC + 2 * r, FC + 2 * r + 1])])


def pack_phase_a_weights(w_in_l, gate_bias_l, conv_w_l, lam_l, subln_l, normc_l, rel_bias_l, r):
    cols = _qk_cols(r)
    wqk = np.zeros((D, NQKB * 128), np.float32)
    ok = cols >= 0
    wqk[:, ok] = w_in_l[:, cols[ok]]
    wv = np.zeros((D, NV), np.float32)
    wv[:, 0:1412] = w_in_l[:, _v_cols(r)]
    spm = np.zeros((128, 612), np.float32)
    spm[:, 0:4] = gate_bias_l[[2 * r, 2 * r + 1, 4 + 2 * r, 4 + 2 * r + 1]][None, :]
    cb = 0
    for base in (0, 768):
        for hl in range(2):
            ch = base + (2 * r + hl) * CH
            spm[:, 4 + cb * 4:8 + cb * 4] = conv_w_l[:, ch:ch + 128].T; cb += 1
        for hl in range(2):
            ch = base + (2 * r + hl) * CH
            spm[64 * hl:64 * hl + 64, 4 + cb * 4:8 + cb * 4] = conv_w_l[:, ch + 128:ch + 192].T
        cb += 1
    spm[:, 36:164] = subln_l[None, :]
    spm[:, 164:356] = normc_l[None, :]
    spm[:, 356:612] = lam_l.reshape(1, 256)
    kl = np.arange(128)[:, None]
    ql = np.arange(128)[None, :]
    toep = np.zeros((128, 15, 128), np.float32)
    for hl in range(3):
        h = 3 * r + hl
        for j in range(5):
            rel = 128 * (4 - j) + ql - kl
            idx = np.clip(rel, -256, 256) + 256
            dc = 2 * (4 - j) + ql // 64 - kl // 64
            valid = (dc >= 0) & (dc <= 8)
            toep[:, hl * 5 + j, :] = np.where(valid, rel_bias_l[h][idx], np.float32(NEGM))
    return dict(wqk=wqk, wv=wv, spm=spm, toep=toep)


def phase_a_consts(r):
    pos = np.arange(SEQ)
    jb, rr = pos // 128, pos % 128
    kaug = np.zeros((4, 2, SEQ), np.float32)
    qaug = np.zeros((4, 2, SEQ), np.float32)
    diag = np.zeros((128, 2, 128), np.float32)
    kl = np.arange(128)[:, None]
    ql = np.arange(128)[None, :]
    for hl in range(2):
        sl = SLOPES[2 * r + hl]
        kaug[0, hl] = 1.0; kaug[1, hl] = 1.0; kaug[2, hl] = sl * 128 * jb; kaug[3, hl] = sl * rr
        qaug[0, hl] = -sl * 128 * jb; qaug[1, hl] = -sl * rr; qaug[2, hl] = 1.0; qaug[3, hl] = 1.0
        diag[:, hl, :] = np.where(kl // 64 <= ql // 64, -sl * np.abs(ql - kl), NEGM)
    ident = np.eye(128, dtype=np.float32)
    tri = (np.arange(128)[:, None] <= np.arange(128)[None, :]).astype(np.float32)
    return dict(kaug=kaug, qaug=qaug, diag=diag, ident=ident, tri=tri)


_PROG_CACHE = {}


def _prog(key, builder):
    if key not in _PROG_CACHE:
        _PROG_CACHE[key] = builder()
    return _PROG_CACHE[key]


def _wout_perm():
    parts = []
    for r in range(2):
        parts += [2 * r * 128 + np.arange(256), 512 + 3 * r * 128 + np.arange(384), 1280 + 2 * r * CH + np.arange(384)]
    return np.concatenate(parts)


def kernel_unfused(x, w_in, gate_bias, conv_w, lam, subln_a, norm_c, rel_bias, w_out,
                   ln1_g, ln1_b, w_gate, w_up, w_down, ln2_g, ln2_b):
    f = lambda a: np.ascontiguousarray(np.asarray(a, dtype=np.float32))
    x, w_in, gate_bias, conv_w, lam, subln_a, norm_c, rel_bias = map(f, (x, w_in, gate_bias, conv_w, lam, subln_a, norm_c, rel_bias))
    w_out, ln1_g, ln1_b, w_gate, w_up, w_down, ln2_g, ln2_b = map(f, (w_out, ln1_g, ln1_b, w_gate, w_up, w_down, ln2_g, ln2_b))
    cores = list(range(8))
    xT = [np.ascontiguousarray(x[b].T) for b in range(NB)]
    perm = _wout_perm()
    consts = [phase_a_consts(r) for r in range(2)]
    for li in range(DEPTH):
        nca = _prog(("A", li), lambda: build_phase_a(li))
        packs = [pack_phase_a_weights(w_in[li], gate_bias[li], conv_w[li], lam[li], subln_a[li], norm_c[li], rel_bias[li], r)
                 for r in range(2)]
        maps = []
        for c in cores:
            b, r = c // 2, c % 2
            m = dict(packs[r]); m.update(consts[r]); m["xT"] = xT[b]
            maps.append(m)
        ra = run_bass_kernel_spmd(nca, maps, core_ids=cores).results
        ncb = _prog(("B",), lambda: build_phase_b())
        wo = np.ascontiguousarray(w_out[li][perm, :])
        lnp = lnp_pack(ln1_g[li], ln1_b[li], ln2_g[li], ln2_b[li])
        maps = []
        for c in cores:
            b, r = c // 2, c % 2
            yT = np.concatenate([ra[2 * b]["yT"], ra[2 * b + 1]["yT"]], axis=0)
            maps.append({"yT": np.ascontiguousarray(yT[:, r * TOK:(r + 1) * TOK]),
                         "xres": np.ascontiguousarray(xT[b][:, r * TOK:(r + 1) * TOK]),
                         "w_out": wo, "w_gate": w_gate[li], "w_up": w_up[li], "w_down": w_down[li], "lnp": lnp})
        rb = run_bass_kernel_spmd(ncb, maps, core_ids=cores).results
        xT = [np.concatenate([rb[2 * b]["xo"], rb[2 * b + 1]["xo"]], axis=1) for b in range(NB)]
    return np.ascontiguousarray(np.stack([xT[b].T for b in range(NB)], axis=0)).astype(np.float32)


class NCP:
    def __init__(self, nc, sfx):
        self._nc = nc
        self._sfx = sfx

    def sbuf_tensor(self, name, *a, **k):
        return self._nc.sbuf_tensor(name + self._sfx, *a, **k)

    def psum_tensor(self, name, *a, **k):
        return self._nc.psum_tensor(name + self._sfx, *a, **k)

    def __getattr__(self, n):
        return getattr(self._nc, n)


PAIRS = [[0, 1], [2, 3], [4, 5], [6, 7]]
A_IN = [("wqk", [D, NQKB * 128]), ("wv", [D, NV]), ("spm", [128, 612]), ("toep", [128, 15, 128])]
B_IN = [("w_out", [D, D]), ("w_gate", [D, DFF]), ("w_up", [D, DFF]), ("w_down", [DFF, D]), ("lnp", [128, 4, 16])]
C_IN = [("ident", [128, 128]), ("tri", [128, 128]), ("kaug", [4, 2, SEQ]), ("qaug", [4, 2, SEQ]), ("diag", [128, 2, 128]),
        ("sel", [128, 2])]


def build_fused():
    nc = bass.Bass("TRN2", target_bir_lowering=False)
    def din(name, shape, dt=F32):
        return nc.dram_tensor(name, shape, dt, kind="ExternalInput").ap()
    gio = {"xT": din("xT", [D, SEQ]), "xres": din("xres", [D, TOK])}
    for name, shp in C_IN:
        gio[name] = din(name, shp)
    for li in range(DEPTH):
        for name, shp in A_IN + B_IN:
            gio[name + str(li)] = din(name + str(li), shp)
    xo = nc.dram_tensor("xo", [D, TOK], F32, kind="ExternalOutput").ap()
    scr = {}
    for li in range(DEPTH):
        scr["yown", li] = nc.dram_tensor(f"yown{li}", [1024, 1024], BF16).ap()
        scr["ysend", li] = nc.dram_tensor(f"ysend{li}", [1024, 1024], BF16).ap()
        scr["ygath", li] = nc.dram_tensor(f"ygath{li}", [2048, 1024], BF16).ap()
        scr["yownAB", li] = nc.dram_tensor(f"yownAB{li}", [640, 1024], BF16).ap()
        scr["ysendAB", li] = nc.dram_tensor(f"ysendAB{li}", [640, 1024], BF16).ap()
        scr["ygathAB", li] = nc.dram_tensor(f"ygathAB{li}", [1280, 1024], BF16).ap()
        scr["yownC", li] = nc.dram_tensor(f"yownC{li}", [384, 1024], BF16).ap()
        scr["ysendC", li] = nc.dram_tensor(f"ysendC{li}", [384, 1024], BF16).ap()
        scr["ygathC", li] = nc.dram_tensor(f"ygathC{li}", [768, 1024], BF16).ap()
    xsend = [nc.dram_tensor(f"xsend{h}", [D // 2, TOK], BF16).ap() for h in range(2)]
    xgath = [nc.dram_tensor(f"xgath{h}", [D, TOK], BF16).ap() for h in range(2)]
    xres_scr = nc.dram_tensor("xres_scr", [D, TOK], F32).ap()
    S = Sched(nc)
    for li in range(DEPTH):
        with ExitStack() as es:
            io = {k: gio[k] for k, _ in C_IN}
            for name, _ in A_IN:
                io[name] = gio[name + str(li)]
            io["xT"] = gio["xT"]
            io["xgath"] = xgath if li > 0 else None
            io["yown"], io["ysend"] = scr["yownC", li], scr["ysendC", li]
            io["yownAB"], io["ysendAB"] = scr["yownAB", li], scr["ysendAB", li]
            io["cc_ab"] = lambda li=li: S.add("pool", lambda e: e.collective_compute(
                "AllGather", ALU.bypass, replica_groups=PAIRS, ins=[scr["ysendAB", li].opt()], outs=[scr["ygathAB", li].opt()]),
                reads=["ysendAB"], writes=["ygathAB"], dma_group=("ccyab", li), inc=1)
            emit_phase_a(S, NCP(nc, f"_a{li}"), es, io, li)
        S.add("pool", lambda e, li=li: e.collective_compute("AllGather", ALU.bypass, replica_groups=PAIRS,
                                                            ins=[scr["ysendC", li].opt()], outs=[scr["ygathC", li].opt()]),
              reads=["ysendC"], writes=["ygathC"], dma_group=("ccy", li), inc=1)
        with ExitStack() as es:
            io = {"sel": gio["sel"]}
            for name, _ in B_IN:
                io[name] = gio[name + str(li)]
            io["yown"], io["ygath"] = scr["yownC", li], scr["ygathC", li]
            io["yownAB"], io["ygathAB"] = scr["yownAB", li], scr["ygathAB", li]
            io["xres"] = gio["xres"] if li == 0 else xres_scr
            if li == 0:
                io["xo"], io["xo_key"], io["xsend"] = xres_scr, "xres_scr", xsend
            else:
                io["xo"], io["xsend"] = xo, None
            if li > 0:
                pass
            emit_phase_b(S, NCP(nc, f"_b{li}"), es, io)
            S.barrier()
        if li == 0:
            for fh in range(2):
                S.add("pool", lambda e, fh=fh: e.collective_compute("AllGather", ALU.bypass, replica_groups=PAIRS,
                                                                    ins=[xsend[fh].opt()], outs=[xgath[fh].opt()]),
                      reads=["xsend"], writes=["xgath"], dma_group=("ccx", fh), inc=1)
    S.barrier()
    S.emit()
    return nc


def kernel(x, w_in, gate_bias, conv_w, lam, subln_a, norm_c, rel_bias, w_out,
           ln1_g, ln1_b, w_gate, w_up, w_down, ln2_g, ln2_b):
    f = lambda a: np.ascontiguousarray(np.asarray(a, dtype=np.float32))
    x, w_in, gate_bias, conv_w, lam, subln_a, norm_c, rel_bias = map(f, (x, w_in, gate_bias, conv_w, lam, subln_a, norm_c, rel_bias))
    w_out, ln1_g, ln1_b, w_gate, w_up, w_down, ln2_g, ln2_b = map(f, (w_out, ln1_g, ln1_b, w_gate, w_up, w_down, ln2_g, ln2_b))
    cores = list(range(8))
    perm = _wout_perm()
    nc = _prog(("F",), build_fused)
    xT = [np.ascontiguousarray(x[b].T) for b in range(NB)]
    shared = {}
    for li in range(DEPTH):
        shared["w_out%d" % li] = np.ascontiguousarray(w_out[li][perm, :])
        shared["w_gate%d" % li] = w_gate[li]
        shared["w_up%d" % li] = w_up[li]
        shared["w_down%d" % li] = w_down[li]
        shared["lnp%d" % li] = lnp_pack(ln1_g[li], ln1_b[li], ln2_g[li], ln2_b[li])
    per_r = []
    for r in range(2):
        m = dict(phase_a_consts(r))
        sel = np.zeros((128, 2), np.float32)
        sel[:, r] = 1.0
        m["sel"] = sel
        for li in range(DEPTH):
            pk = pack_phase_a_weights(w_in[li], gate_bias[li], conv_w[li], lam[li], subln_a[li], norm_c[li], rel_bias[li], r)
            for k, v in pk.items():
                m[k + str(li)] = v
        per_r.append(m)
    maps = []
    for c in cores:
        b, r = c // 2, c % 2
        m = dict(shared); m.update(per_r[r])
        m["xT"] = xT[b]
        m["xres"] = np.ascontiguousarray(xT[b][:, r * TOK:(r + 1) * TOK])
        maps.append(m)
    res = run_bass_kernel_spmd(nc, maps, core_ids=cores).results
    out = np.empty((NB, SEQ, D), np.float32)
    for c in cores:
        b, r = c // 2, c % 2
        out[b, r * TOK:(r + 1) * TOK, :] = res[c]["xo"].T
    return out
```

```python
from contextlib import ExitStack
import numpy as np
import concourse.bass as bass
import concourse.mybir as mybir

F32 = mybir.dt.float32
BF16 = mybir.dt.bfloat16
AF = mybir.ActivationFunctionType
ALU = mybir.AluOpType
AX = mybir.AxisListType

ENGS = ("pe", "act", "dve", "pool", "sp")


class Op:
    __slots__ = ("eng", "fn", "deps", "signal", "dma_group", "dma_n", "idx", "sigcount", "inc")

    def __init__(self, eng, fn):
        self.eng = eng
        self.fn = fn
        self.deps = {}
        self.signal = False
        self.dma_group = None
        self.dma_n = 0
        self.idx = -1
        self.sigcount = 0
        self.inc = 16


class Sched:
    def __init__(self, nc):
        self.nc = nc
        self.streams = {e: [] for e in ENGS}
        self.last_write = {}
        self.readers = {}
        self.dma_counts = {}
        self.group_inc = {}
        self.seen = {e: {} for e in ENGS}

    def _add_dep(self, op, tok):
        if tok is None:
            return
        if tok[0] == "c":
            _, e, i = tok
            if e == op.eng and e != "pool":
                pass
            cur = op.deps.get(("c", e), -1)
            if i > cur:
                op.deps[("c", e)] = i
        else:
            _, g, n = tok
            cur = op.deps.get(("d", g), 0)
            if n > cur:
                op.deps[("d", g)] = n

    def add(self, eng, fn, reads=(), writes=(), dma_group=None, inc=16):
        op = Op(eng, fn)
        op.inc = inc
        if dma_group is not None:
            self.group_inc[dma_group] = inc
        op.idx = len(self.streams[eng])
        for k in reads:
            w = self.last_write.get(k)
            if w is not None:
                self._add_dep(op, w)
        for k in writes:
            w = self.last_write.get(k)
            if w is not None:
                if not (w[0] == "c" and w[1] == eng and dma_group is None and eng == "pe"):
                    self._add_dep(op, w)
            for r in self.readers.get(k, ()):
                if not (r[0] == "c" and r[1] == eng and dma_group is None and eng == "pe"):
                    self._add_dep(op, r)
        if dma_group is not None:
            n = self.dma_counts.get(dma_group, 0) + 1
            self.dma_counts[dma_group] = n
            op.dma_group = dma_group
            op.dma_n = n
            tok = ("d", dma_group, n)
        else:
            tok = ("c", eng, op.idx)
        for k in reads:
            self.readers.setdefault(k, []).append(tok)
        for k in writes:
            self.last_write[k] = tok
            self.readers[k] = []
        seen = self.seen[eng]
        for key in list(op.deps.keys()):
            v = op.deps[key]
            if key[0] == "c" and key[1] == eng and dma_group is None:
                pass
            if seen.get(key, -1 if key[0] == "c" else 0) >= v:
                del op.deps[key]
            else:
                seen[key] = v
        self.streams[eng].append(op)
        return op

    def barrier(self, only_active=True):
        toks = []
        for e in ENGS:
            for op in reversed(self.streams[e]):
                if op.fn is not None and op.dma_group is None:
                    toks.append(("c", e, op.idx))
                    break
        dtoks = [("d", g, n) for g, n in self.dma_counts.items()]
        for e in ENGS:
            if only_active and not self.streams[e]:
                continue
            op = Op(e, None)
            op.idx = len(self.streams[e])
            for t in toks:
                if t[1] != e:
                    self._add_dep(op, t)
            for t in dtoks:
                self._add_dep(op, t)
            seen = self.seen[e]
            for key in list(op.deps.keys()):
                v = op.deps[key]
                if seen.get(key, -1 if key[0] == "c" else 0) >= v:
                    del op.deps[key]
                else:
                    seen[key] = v
            self.streams[e].append(op)
        self.last_write.clear()
        self.readers.clear()

    def emit(self, final_waits=()):
        nc = self.nc
        for e in ENGS:
            for op in self.streams[e]:
                for key, v in op.deps.items():
                    if key[0] == "c":
                        self.streams[key[1]][v].signal = True
        for e in ENGS:
            c = 0
            for op in self.streams[e]:
                if op.signal:
                    c += 1
                op.sigcount = c
        with ExitStack() as es:
            sems = {}
            for e in ENGS:
                sems[("c", e)] = es.enter_context(nc.semaphore("s_" + e))
            for g in self.dma_counts:
                sems[("d", g)] = es.enter_context(nc.semaphore("d_" + str(g)))
            block = es.enter_context(nc.Block())
            streams = self.streams

            def run(e, engobj):
                for op in streams[e]:
                    for key, v in op.deps.items():
                        if key[0] == "c":
                            tgt = streams[key[1]][v]
                            assert tgt.signal
                            engobj.wait_ge(sems[key], tgt.sigcount)
                        else:
                            engobj.wait_ge(sems[key], self.group_inc[key[1]] * v)
                    if op.fn is None:
                        continue
                    ins = op.fn(engobj)
                    if op.dma_group is not None:
                        if op.inc == 16:
                            ins.then_inc(sems[("d", op.dma_group)], 16)
                        else:
                            ins.then_inc(sems[("d", op.dma_group)])
                    elif op.signal:
                        ins.then_inc(sems[("c", e)], 1)

            @block.tensor
            def _(eng):
                run("pe", eng)

            @block.scalar
            def _(eng):
                run("act", eng)

            @block.vector
            def _(eng):
                run("dve", eng)

            @block.gpsimd
            def _(eng):
                run("pool", eng)

            @block.sync
            def _(eng):
                run("sp", eng)

from concourse.bass_utils import run_bass_kernel_spmd
import ml_dtypes

D = 2048
SEQ = 2048
NB = 4
TOK = 1024
DFF = 5632
NFF = DFF // 128
DEPTH = 2
ALPHA = (2.0 * DEPTH) ** 0.25
LN_EPS = 1e-5
FF_SPLITS = [(0, 6), (6, 12), (12, 18), (18, 22)]


class PsumRot:
    def __init__(self, banks):
        self.banks = banks
        self.i = 0

    def get(self):
        b = self.banks[self.i % len(self.banks)]
        self.i += 1
        return b


def emit_layernorm(S, nc, res, outT, lnp, gi, bi, ones, sq, stats, tmpv, psab, ntok, tag):
    inv = 1.0 / D
    ntg = ntok // 512
    for tg in range(ntg):
        ts = slice(tg * 512, (tg + 1) * 512)
        mean, rstd = stats[tg]
        (psA, kA), (psB, kB) = psab[tg]
        km, kr = ("lnmean", tg), ("lnrstd", tg)
        for c in range(16):
            q = c % 2
            S.add("act", lambda e, c=c, q=q, ts=ts: e.activation(out=sq[q][:], in_=res[:, c, ts], func=AF.Square),
                  reads=[("res", c, tg)], writes=[("sq", q)])
            S.add("pe", lambda e, c=c, ts=ts, psA=psA: e.matmul(psA[:], lhsT=ones[:], rhs=res[:, c, ts],
                                                                 start=(c == 0), stop=(c == 15)),
                  reads=[("res", c, tg), "ones"], writes=[kA])
            S.add("pe", lambda e, c=c, q=q, psB=psB: e.matmul(psB[:], lhsT=ones[:], rhs=sq[q][:],
                                                               start=(c == 0), stop=(c == 15)),
                  reads=[("sq", q), "ones"], writes=[kB])
        S.add("dve", lambda e, mean=mean, psA=psA: e.tensor_scalar(out=mean[:], in0=psA[:], scalar1=inv, scalar2=None, op0=ALU.mult),
              reads=[kA], writes=[km])
        S.add("dve", lambda e, mean=mean: e.tensor_tensor(out=tmpv[:], in0=mean[:], in1=mean[:], op=ALU.mult),
              reads=[km], writes=["tmpv"])
        S.add("dve", lambda e, rstd=rstd, psB=psB: e.scalar_tensor_tensor(out=rstd[:], in0=psB[:], scalar=inv, in1=tmpv[:],
                                                                          op0=ALU.mult, op1=ALU.subtract),
              reads=[kB, "tmpv"], writes=[kr])
        S.add("dve", lambda e, rstd=rstd: e.tensor_scalar(out=rstd[:], in0=rstd[:], scalar1=LN_EPS, scalar2=None, op0=ALU.add),
              reads=[kr], writes=[kr])
        S.add("act", lambda e, rstd=rstd: e.activation(out=tmpv[:], in_=rstd[:], func=AF.Sqrt),
              reads=[kr], writes=["tmpv"])
        S.add("dve", lambda e, rstd=rstd: e.reciprocal(out=rstd[:], in_=tmpv[:]),
              reads=["tmpv"], writes=[kr])
    for tg in range(ntg):
        ts = slice(tg * 512, (tg + 1) * 512)
        mean, rstd = stats[tg]
        km, kr = ("lnmean", tg), ("lnrstd", tg)
        for c in range(16):
            S.add("dve", lambda e, c=c, ts=ts, mean=mean: e.tensor_tensor(out=res[:, c, ts], in0=res[:, c, ts], in1=mean[:],
                                                                          op=ALU.subtract),
                  reads=[("res", c, tg), km], writes=[("res", c, tg)])
            S.add("pool", lambda e, c=c, ts=ts, rstd=rstd: e.tensor_tensor(out=res[:, c, ts], in0=res[:, c, ts], in1=rstd[:],
                                                                           op=ALU.mult),
                  reads=[("res", c, tg), kr], writes=[("res", c, tg)])
            S.add("act", lambda e, c=c, ts=ts: e.activation(out=res[:, c, ts], in_=res[:, c, ts], func=AF.Identity,
                                                            scale=lnp[:, gi, c:c + 1], bias=lnp[:, bi, c:c + 1]),
                  reads=[("res", c, tg), "lnp"], writes=[("res", c, tg)])
            if outT is not None:
                S.add("act", lambda e, c=c, ts=ts: e.copy(out=outT[:, c, ts], in_=res[:, c, ts]),
                      reads=[("res", c, tg)], writes=[("xT", c, tg)])


def emit_phase_b(S, nc, es, io, ntok=TOK, ff_splits=None):
    ff_splits = ff_splits or FF_SPLITS
    NTG = ntok // 512
    res = es.enter_context(nc.sbuf_tensor("res", [128, 16, ntok], F32))
    x1T = es.enter_context(nc.sbuf_tensor("x1T", [128, 16, ntok], BF16))
    big = es.enter_context(nc.sbuf_tensor("big", [128, 16, ntok], BF16))
    wbuf = [es.enter_context(nc.sbuf_tensor(f"wbuf{i}", [128, 16, 256], BF16)) for i in range(6)]
    wd = [es.enter_context(nc.sbuf_tensor(f"wd{i}", [128, 12, 256], BF16)) for i in range(2)]
    ones = es.enter_context(nc.sbuf_tensor("ones", [128, 128], F32))
    lnp = es.enter_context(nc.sbuf_tensor("lnp_sb", [128, 4, 16], F32))
    sq = [es.enter_context(nc.sbuf_tensor(f"sq{i}", [128, 512], F32)) for i in range(2)]
    stats = [(es.enter_context(nc.sbuf_tensor(f"mean{i}", [128, 512], F32)),
              es.enter_context(nc.sbuf_tensor(f"rstd{i}", [128, 512], F32))) for i in range(NTG)]
    tmpv = es.enter_context(nc.sbuf_tensor("tmpv", [128, 512], F32))
    sg = [es.enter_context(nc.sbuf_tensor(f"sg{i}", [128, 512], F32)) for i in range(2)]
    pbanks = [es.enter_context(nc.psum_tensor(f"pb{i}", [128, 512], F32)) for i in range(8)]
    psab = [((pbanks[6], ("ps", 6)), (pbanks[7], ("ps", 7))), ((pbanks[4], ("ps", 4)), (pbanks[5], ("ps", 5)))][:NTG]
    rot = PsumRot(list(range(6)))

    S.add("pool", lambda e: e.memset(ones[:], 1.0), writes=["ones"])
    S.add("sp", lambda e: e.dma_start(out=lnp[:], in_=io["lnp"]), writes=["lnp"], dma_group="lnp")
    for c4 in range(4):
        S.add("sp", lambda e, c4=c4: e.dma_start(
            out=res[:, c4 * 4:(c4 + 1) * 4, :],
            in_=io["xres"].rearrange("(c p) t -> p c t", p=128)[:, c4 * 4:(c4 + 1) * 4, :]),
            writes=[("res", c, tg) for c in range(c4 * 4, c4 * 4 + 4) for tg in range(NTG)],
            dma_group=("xres", c4))
    if io.get("ygath") is None:
        S.add("sp", lambda e: e.dma_start(out=big[:], in_=io["yT"].rearrange("(c p) t -> p c t", p=128)),
              writes=[("big", c) for c in range(16)], dma_group="yT")
    else:
        selb = es.enter_context(nc.sbuf_tensor("selB", [128, 2], F32))
        S.add("sp", lambda e: e.dma_start(out=selb[:], in_=io["sel"]), writes=["selB"], dma_group="selB")
        if io.get("ygathAB") is None:
            parts = [(io["yown"], io["ygath"], 0, 8, "")]
        else:
            parts = [(io["yownAB"], io["ygathAB"], 0, 5, "AB"), (io["yown"], io["ygath"], 5, 8, "C")]
        for (d_own, d_g, c0, c1, tag) in parts:
            n = c1 - c0
            yo = d_own.rearrange("(c p) t -> p c t", p=128)
            yg = d_g.rearrange("(r c p) t -> p r c t", r=2, p=128)
            for hf in range(2):
                S.add("sp", lambda e, hf=hf, yo=yo, c0=c0, c1=c1: e.dma_start(out=big[:, hf * 8 + c0:hf * 8 + c1, :], in_=yo),
                      reads=["yown" + tag], writes=[("big", c) for c in range(hf * 8 + c0, hf * 8 + c1)], dma_group=("yT" + tag, hf))
                S.add("sp", lambda e, hf=hf, yg=yg, c0=c0, c1=c1: e.dma_start(out=x1T[:, hf * 8 + c0:hf * 8 + c1, :], in_=yg[:, hf, :, :]),
                      reads=["ygath" + tag], writes=[("xT", c, tg) for c in range(hf * 8 + c0, hf * 8 + c1) for tg in range(NTG)],
                      dma_group=("ygl" + tag, hf))
        for hf in range(2):
            hs = slice(hf * 8, hf * 8 + 8)
            kb = [("big", c) for c in range(hf * 8, hf * 8 + 8)]
            kx = [("xT", c, tg) for c in range(hf * 8, hf * 8 + 8) for tg in range(NTG)]
            S.add("dve", lambda e, hs=hs, hf=hf: e.tensor_scalar(out=big[:, hs, :], in0=big[:, hs, :], scalar1=selb[:, hf:hf + 1], scalar2=None, op0=ALU.mult),
                  reads=kb + ["selB"], writes=kb)
            S.add("dve", lambda e, hs=hs, hf=hf: e.scalar_tensor_tensor(out=big[:, hs, :], in0=x1T[:, hs, :], scalar=selb[:, 1 - hf:2 - hf], in1=big[:, hs, :],
                                                                        op0=ALU.mult, op1=ALU.add),
                  reads=kb + kx + ["selB"], writes=kb)

    wctr = [0]

    def load_w(src_ap, tag):
        s = wctr[0] % 6
        wctr[0] += 1
        S.add("pool", lambda e, s=s: e.dma_start(out=wbuf[s][:], in_=src_ap),
              writes=[("wbuf", s)], dma_group=("wbuf", s))
        return s

    wo = io["w_out"].rearrange("(kc p) n -> p kc n", p=128)
    for cg in range(8):
        s = load_w(wo[:, :, cg * 256:(cg + 1) * 256], "wo")
        for cc in range(2):
            col = cg * 2 + cc
            for tg in range(NTG):
                ts = slice(tg * 512, (tg + 1) * 512)
                b = rot.get()
                for kc in range(16):
                    S.add("pe", lambda e, s=s, cc=cc, kc=kc, b=b, ts=ts: e.matmul(
                        pbanks[b][:], lhsT=wbuf[s][:, kc, cc * 128:(cc + 1) * 128], rhs=big[:, kc, ts],
                        start=(kc == 0), stop=(kc == 15)),
                        reads=[("wbuf", s), ("big", kc)], writes=[("ps", b)])
                S.add("dve", lambda e, col=col, b=b, ts=ts: e.scalar_tensor_tensor(
                    out=res[:, col, ts], in0=res[:, col, ts], scalar=ALPHA, in1=pbanks[b][:],
                    op0=ALU.mult, op1=ALU.add),
                    reads=[("ps", b), ("res", col, tg)], writes=[("res", col, tg)])
    emit_layernorm(S, nc, res, x1T, lnp, 0, 1, ones, sq, stats, tmpv, psab, ntok, "ln1")
    wg = io["w_gate"].rearrange("(kc p) n -> p kc n", p=128)
    wu = io["w_up"].rearrange("(kc p) n -> p kc n", p=128)
    sgc = [0]
    wdc = [0]
    for si, (sl0, sl1) in enumerate(ff_splits):
        nch = (sl1 - sl0) * 2
        for sl in range(sl0, sl1):
            sgt = load_w(wg[:, :, sl * 256:(sl + 1) * 256], "wg")
            sut = load_w(wu[:, :, sl * 256:(sl + 1) * 256], "wu")
            for cc in range(2):
                j = (sl - sl0) * 2 + cc
                for tg in range(NTG):
                    ts = slice(tg * 512, (tg + 1) * 512)
                    bg = rot.get()
                    bu = rot.get()
                    for kc in range(16):
                        S.add("pe", lambda e, s=sgt, cc=cc, kc=kc, b=bg, ts=ts: e.matmul(
                            pbanks[b][:], lhsT=wbuf[s][:, kc, cc * 128:(cc + 1) * 128], rhs=x1T[:, kc, ts],
                            start=(kc == 0), stop=(kc == 15)),
                            reads=[("wbuf", sgt), ("xT", kc, tg)], writes=[("ps", bg)])
                    for kc in range(16):
                        S.add("pe", lambda e, s=sut, cc=cc, kc=kc, b=bu, ts=ts: e.matmul(
                            pbanks[b][:], lhsT=wbuf[s][:, kc, cc * 128:(cc + 1) * 128], rhs=x1T[:, kc, ts],
                            start=(kc == 0), stop=(kc == 15)),
                            reads=[("wbuf", sut), ("xT", kc, tg)], writes=[("ps", bu)])
                    q = sgc[0] % 2
                    sgc[0] += 1
                    S.add("act", lambda e, q=q, b=bg: e.activation(out=sg[q][:], in_=pbanks[b][:], func=AF.Silu),
                          reads=[("ps", bg)], writes=[("sg", q)])
                    S.add("dve", lambda e, q=q, b=bu, j=j, ts=ts: e.tensor_tensor(
                        out=big[:, j, ts], in0=sg[q][:], in1=pbanks[b][:], op=ALU.mult),
                        reads=[("sg", q), ("ps", bu)], writes=[("big", j)])
        r0 = sl0 * 256
        for cp in range(8):
            s = wdc[0] % 2
            wdc[0] += 1
            src = io["w_down"][r0:r0 + nch * 128, cp * 256:(cp + 1) * 256].rearrange("(j p) n -> p j n", p=128)
            S.add("pool", lambda e, s=s, src=src, nch=nch: e.dma_start(out=wd[s][:, 0:nch, :], in_=src),
                  writes=[("wd", s)], dma_group=("wd", s))
            for c2 in range(2):
                col = cp * 2 + c2
                for tg in range(NTG):
                    ts = slice(tg * 512, (tg + 1) * 512)
                    b = rot.get()
                    for j in range(nch):
                        S.add("pe", lambda e, s=s, c2=c2, j=j, b=b, ts=ts, nch=nch: e.matmul(
                            pbanks[b][:], lhsT=wd[s][:, j, c2 * 128:(c2 + 1) * 128], rhs=big[:, j, ts],
                            start=(j == 0), stop=(j == nch - 1)),
                            reads=[("wd", s), ("big", j)], writes=[("ps", b)])
                    if si == 0:
                        S.add("dve", lambda e, col=col, b=b, ts=ts: e.scalar_tensor_tensor(
                            out=res[:, col, ts], in0=res[:, col, ts], scalar=ALPHA, in1=pbanks[b][:],
                            op0=ALU.mult, op1=ALU.add),
                            reads=[("ps", b), ("res", col, tg)], writes=[("res", col, tg)])
                    else:
                        S.add("dve", lambda e, col=col, b=b, ts=ts: e.tensor_tensor(
                            out=res[:, col, ts], in0=res[:, col, ts], in1=pbanks[b][:], op=ALU.add),
                            reads=[("ps", b), ("res", col, tg)], writes=[("res", col, tg)])
    xsend = io.get("xsend")
    emit_layernorm(S, nc, res, x1T if xsend is not None else None, lnp, 2, 3, ones, sq, stats, tmpv, psab, ntok, "ln2")
    for c4 in range(4):
        S.add("sp", lambda e, c4=c4: e.dma_start(
            out=io["xo"].rearrange("(c p) t -> p c t", p=128)[:, c4 * 4:(c4 + 1) * 4, :],
            in_=res[:, c4 * 4:(c4 + 1) * 4, :]),
            reads=[("res", c, tg) for c in range(c4 * 4, c4 * 4 + 4) for tg in range(NTG)],
            writes=[io.get("xo_key", "xo_out")], dma_group=("xo", c4))
    if xsend is not None:
        for fh in range(2):
            S.add("sp", lambda e, fh=fh: e.dma_start(out=xsend[fh].rearrange("(c p) t -> p c t", p=128), in_=x1T[:, fh * 8:fh * 8 + 8, :]),
                  reads=[("xT", c, tg) for c in range(fh * 8, fh * 8 + 8) for tg in range(NTG)], writes=["xsend"], dma_group=("xsend", fh))


def build_phase_b(ntok=TOK, ff_splits=None, dff=DFF):
    nc = bass.Bass("TRN2", target_bir_lowering=False)
    io = {
        "yT": nc.dram_tensor("yT", [D, ntok], BF16, kind="ExternalInput").ap(),
        "xres": nc.dram_tensor("xres", [D, ntok], F32, kind="ExternalInput").ap(),
        "w_out": nc.dram_tensor("w_out", [D, D], F32, kind="ExternalInput").ap(),
        "w_gate": nc.dram_tensor("w_gate", [D, dff], F32, kind="ExternalInput").ap(),
        "w_up": nc.dram_tensor("w_up", [D, dff], F32, kind="ExternalInput").ap(),
        "w_down": nc.dram_tensor("w_down", [dff, D], F32, kind="ExternalInput").ap(),
        "lnp": nc.dram_tensor("lnp", [128, 4, 16], F32, kind="ExternalInput").ap(),
        "xo": nc.dram_tensor("xo", [D, ntok], F32, kind="ExternalOutput").ap(),
    }
    S = Sched(nc)
    with ExitStack() as es:
        emit_phase_b(S, nc, es, io, ntok, ff_splits)
        S.barrier()
        S.emit()
    return nc


def lnp_pack(g1, b1, g2, b2):
    return np.ascontiguousarray(np.stack([v.reshape(16, 128).T for v in (g1, b1, g2, b2)], axis=1)).astype(np.float32)


A_HEADS_PER = 2
B_HEADS_PER = 3
C_HEADS_PER = 2
CH = 192
QA, KA, VA_, QB, KB, VB_, QC, KC, VC_, OC, IC, FC = 0, 512, 1024, 1536, 2304, 3072, 3840, 4608, 5376, 6144, 6912, 6916
QK_BLOCKS = (["Aq0", "Ak0", "Aq1", "Ak1", "Bq0", "Bk0", "Bq1", "Bk1", "Bq2", "Bk2"]
             + ["Cq0a", "Cq1a", "Cqb", "Ck0a", "Ck1a", "Ckb"])
NQKB = len(QK_BLOCKS)
V_SEGS = [("Av", 0, 256), ("Bv01", 256, 512), ("Bv2", 512, 640), ("Cv0", 640, 832), ("Cv1", 832, 1024),
          ("Coa", 1024, 1280), ("Cob", 1280, 1536)]
NV = 1536
SLOPES = [2.0 ** (-8.0 * (h + 1) / 4) for h in range(4)]
NEGM = -30000.0


def lam_init_of(li):
    import math
    return 0.8 - 0.6 * math.exp(-0.3 * li)


def emit_phase_a(S, nc, es, io, li):
    lam_init = lam_init_of(li)
    qk = {}
    for name in QK_BLOCKS:
        qk[name] = es.enter_context(nc.sbuf_tensor("qk_" + name, [128, SEQ], BF16))
    vA = es.enter_context(nc.sbuf_tensor("vA", [128, 16, 2, 129], BF16))
    vB = es.enter_context(nc.sbuf_tensor("vB", [128, 16, 3, 129], BF16))
    vC = es.enter_context(nc.sbuf_tensor("vC", [128, 16, 2, 193], BF16))
    og = es.enter_context(nc.sbuf_tensor("og", [128, 16, 384], BF16))
    graw = es.enter_context(nc.sbuf_tensor("graw", [128, 16, 4], F32))
    spm = es.enter_context(nc.sbuf_tensor("spm_sb", [128, 740], F32))
    ident = es.enter_context(nc.sbuf_tensor("ident_sb", [128, 128], BF16))
    tri = es.enter_context(nc.sbuf_tensor("tri_sb", [128, 128], F32))
    ones = es.enter_context(nc.sbuf_tensor("onesA", [128, 128], F32))
    GB0, CW0, SL0, NC0, LM0 = 0, 4, 36, 164, 356

    S.add("sp", lambda e: e.dma_start(out=spm[:, 0:612], in_=io["spm"]), writes=["spm"], dma_group="spm")
    S.add("pool", lambda e: e.dma_start(out=ident[:], in_=io["ident"]), writes=["ident"], dma_group="cst")
    S.add("sp", lambda e: e.dma_start(out=tri[:], in_=io["tri"]), writes=["tri"], dma_group="cst2")
    S.add("pool", lambda e: e.memset(ones[:], 1.0), writes=["onesA"])
    S.add("pool", lambda e: e.memset(vA[:, :, :, 128:129], 1.0), writes=["vA1"])
    S.add("pool", lambda e: e.memset(vB[:, :, :, 128:129], 1.0), writes=["vB1"])
    S.add("pool", lambda e: e.memset(vC[:, :, :, 192:193], 1.0), writes=["vC1"])

    with ExitStack() as ps_:
        xT = ps_.enter_context(nc.sbuf_tensor("xTa", [128, 16, SEQ], BF16))
        wst = [ps_.enter_context(nc.sbuf_tensor(f"wst{i}", [128, 16, 256], BF16)) for i in range(2)]
        craw = ps_.enter_context(nc.sbuf_tensor("craw", [128, SEQ + 4], F32))
        cacc = ps_.enter_context(nc.sbuf_tensor("cacc", [128, 512], F32))
        pb = [ps_.enter_context(nc.psum_tensor(f"pa{i}", [128, 512], F32)) for i in range(8)]
        rot = PsumRot(list(range(8)))
        if io.get("xgath") is None:
            xsrc = io["xT"].rearrange("(c p) t -> p c t", p=128)
            for c4 in range(8):
                S.add("pool", lambda e, c4=c4: e.dma_start(out=xT[:, c4 * 2:c4 * 2 + 2, :], in_=xsrc[:, c4 * 2:c4 * 2 + 2, :]),
                      writes=[("xT", c4 * 2), ("xT", c4 * 2 + 1)], dma_group=("xTl", c4))
        else:
            for fh in range(2):
                xg = io["xgath"][fh].rearrange("(r c p) t -> p r c t", r=2, p=128)
                for c4 in range(2):
                    for rr in range(2):
                        cc0 = fh * 8 + c4 * 4
                        S.add("sp", lambda e, c4=c4, rr=rr, cc0=cc0, xg=xg: e.dma_start(
                            out=xT[:, cc0:cc0 + 4, rr * 1024:(rr + 1) * 1024], in_=xg[:, rr, c4 * 4:c4 * 4 + 4, :]),
                            reads=["xgath"], writes=[("xT", cc0 + i) for i in range(4)], dma_group=("xTl", fh * 4 + c4 * 2 + rr))
        S.add("pool", lambda e: e.memset(craw[:, 0:4], 0.0), writes=["craw_pad"])
        wctr = [0]

        def load_w(src_ap, ncols):
            s = wctr[0] % 2
            wctr[0] += 1
            S.add("pool", lambda e, s=s, src_ap=src_ap, ncols=ncols: e.dma_start(out=wst[s][:, :, 0:ncols], in_=src_ap),
                  writes=[("wst", s)], dma_group=("wst", s))
            return s

        evc = [0]
        wqk = io["wqk"].rearrange("(kc p) n -> p kc n", p=128)
        import os as _os
        _stop = int(_os.environ.get("PA_STOP", "9"))
        for sl in range(NQKB // 2 if _stop >= 1 else 0):
            s = load_w(wqk[:, :, sl * 256:(sl + 1) * 256], 256)
            for bi2 in range(2):
                bidx = sl * 2 + bi2
                name = QK_BLOCKS[bidx]
                isC = name[0] == "C"
                for tg in range(4):
                    ts = slice(tg * 512, (tg + 1) * 512)
                    b = rot.get()
                    for kc in range(16):
                        S.add("pe", lambda e, s=s, bi2=bi2, kc=kc, b=b, ts=ts: e.matmul(
                            pb[b][:], lhsT=wst[s][:, kc, bi2 * 128:(bi2 + 1) * 128], rhs=xT[:, kc, ts],
                            start=(kc == 0), stop=(kc == 15)),
                            reads=[("wst", s), ("xT", kc)], writes=[("ps", b)])
                    if isC:
                        cts = slice(4 + tg * 512, 4 + (tg + 1) * 512)
                        eng = "act" if evc[0] % 2 == 0 else "dve"
                        evc[0] += 1
                        if eng == "act":
                            S.add("act", lambda e, b=b, cts=cts: e.copy(out=craw[:, cts], in_=pb[b][:]),
                                  reads=[("ps", b)], writes=[("craw", tg)])
                        else:
                            S.add("dve", lambda e, b=b, cts=cts: e.tensor_copy(out=craw[:, cts], in_=pb[b][:]),
                                  reads=[("ps", b)], writes=[("craw", tg)])
                    else:
                        if name[1] == "q":
                            scl = 0.125 if name[0] == "A" else 128.0 ** -0.5
                            S.add("act", lambda e, b=b, ts=ts, name=name, scl=scl: e.activation(
                                out=qk[name][:, ts], in_=pb[b][:], func=AF.Identity, scale=scl),
                                reads=[("ps", b)], writes=[("qk", name)])
                        else:
                            S.add("dve", lambda e, b=b, ts=ts, name=name: e.tensor_copy(out=qk[name][:, ts], in_=pb[b][:]),
                                  reads=[("ps", b)], writes=[("qk", name)])
                    if isC:
                        cb = QK_BLOCKS.index(name) - 10
                        crd = [("craw", tg), "craw_pad", "spm"] + ([("craw", tg - 1)] if tg > 0 else [])
                        o0 = tg * 512
                        S.add("dve", lambda e, cb=cb, o0=o0: e.tensor_scalar(
                            out=cacc[:], in0=craw[:, 1 + o0:1 + o0 + 512], scalar1=spm[:, CW0 + cb * 4:CW0 + cb * 4 + 1], scalar2=None,
                            op0=ALU.mult), reads=crd, writes=["cacc"])
                        for j in range(1, 4):
                            S.add("dve", lambda e, cb=cb, j=j, o0=o0: e.scalar_tensor_tensor(
                                out=cacc[:], in0=craw[:, 1 + j + o0:1 + j + o0 + 512], scalar=spm[:, CW0 + cb * 4 + j:CW0 + cb * 4 + j + 1],
                                in1=cacc[:], op0=ALU.mult, op1=ALU.add), reads=crd + ["cacc"], writes=["cacc"])
                        S.add("act", lambda e, name=name, ts=ts: e.activation(out=qk[name][:, ts], in_=cacc[:], func=AF.Silu),
                              reads=["cacc"], writes=[("qk", name)])
        wv = io["wv"].rearrange("(kc p) n -> p kc n", p=128)
        for (sname, c0, c1) in ([V_SEGS[int(i)] for i in _os.environ.get('PA_SEGS', '0123456')] if _stop >= 2 else []):
            n = c1 - c0
            s = load_w(wv[:, :, c0:c1], n)
            for tb in range(16):
                tsl = slice(tb * 128, (tb + 1) * 128)
                b = rot.get()
                for kc in range(16):
                    S.add("pe", lambda e, s=s, kc=kc, b=b, tsl=tsl, n=n: e.matmul(
                        pb[b][:, 0:n], lhsT=xT[:, kc, tsl], rhs=wst[s][:, kc, 0:n],
                        start=(kc == 0), stop=(kc == 15)),
                        reads=[("wst", s), ("xT", kc)], writes=[("ps", b)])
                rd = [("ps", b)]
                if sname == "Av":
                    S.add("dve", lambda e, b=b, tb=tb: e.tensor_copy(
                        out=vA[:, tb, :, 0:128], in_=pb[b][:, 0:256].rearrange("p (h d) -> p h d", h=2)),
                        reads=rd, writes=[("vA", tb)])
                elif sname == "Bv01":
                    S.add("act", lambda e, b=b, tb=tb: e.copy(
                        out=vB[:, tb, 0:2, 0:128], in_=pb[b][:, 0:256].rearrange("p (h d) -> p h d", h=2)),
                        reads=rd, writes=[("vB", tb)])
                elif sname == "Bv2":
                    S.add("dve", lambda e, b=b, tb=tb: e.tensor_copy(out=vB[:, tb, 2, 0:128], in_=pb[b][:, 0:128]),
                          reads=rd, writes=[("vB", tb)])
                elif sname in ("Cv0", "Cv1"):
                    h = int(sname[2])
                    S.add("act", lambda e, b=b, tb=tb, h=h: e.copy(out=vC[:, tb, h, 0:192], in_=pb[b][:, 0:192]),
                          reads=rd, writes=[("vC", tb)])
                elif sname == "Coa":
                    S.add("act", lambda e, b=b, tb=tb: e.activation(out=og[:, tb, 0:256], in_=pb[b][:, 0:256], func=AF.Sigmoid),
                          reads=rd, writes=[("og", tb)])
                else:
                    S.add("act", lambda e, b=b, tb=tb: e.activation(out=og[:, tb, 256:384], in_=pb[b][:, 0:128], func=AF.Sigmoid),
                          reads=rd, writes=[("og", tb), ("psrd", b)])
                    S.add("act", lambda e, b=b, tb=tb: e.copy(out=graw[:, tb, :], in_=pb[b][:, 128:132]),
                          reads=rd + [("psrd", b)], writes=["graw"])
        S.barrier()
    yT = es.enter_context(nc.sbuf_tensor("yTa", [128, 8, SEQ], BF16))
    with ExitStack() as ms:
        T = dict(qk=qk, vA=vA, vB=vB, vC=vC, og=og, graw=graw, spm=spm, ident=ident, tri=tri, ones=ones, yT=yT,
                 GB0=GB0, SL0=SL0, NC0=NC0, LM0=LM0, lam_init=lam_init)
        T["sc"] = [ms.enter_context(nc.psum_tensor(f"sc{i}", [128, 1024], F32)) for i in range(2)]
        T["acc"] = [ms.enter_context(nc.psum_tensor(f"acc{i}", [128, 512], F32)) for i in range(2)]
        T["tr"] = ms.enter_context(nc.psum_tensor("trp", [128, 1024], BF16))
        T["misc"] = ms.enter_context(nc.psum_tensor("miscp", [128, 512], F32))
        T["ytok"] = ms.enter_context(nc.sbuf_tensor("ytok", [128, 16, 128], BF16))
        T["et"] = [ms.enter_context(nc.sbuf_tensor(f"et{i}", [128, 1024], BF16)) for i in range(2)]
        T["sm"] = ms.enter_context(nc.sbuf_tensor("smallA", [128, 64], F32))
        import os as _os
        _mix = _os.environ.get("PA_MIX", "abc")
        if "a" in _mix:
            with ExitStack() as sa:
                emit_mixer_a(S, nc, sa, io, T)
                S.barrier()
        if "b" in _mix:
            with ExitStack() as sb:
                emit_mixer_b(S, nc, sb, io, T)
                S.barrier()
        split = io.get("ysendAB") is not None

        def emit_select(stk, c0, c1, d_own, d_send, tag):
            n = c1 - c0
            sel = stk.enter_context(nc.sbuf_tensor("selA" + tag, [128, 2], F32))
            tO = stk.enter_context(nc.sbuf_tensor("tOwn" + tag, [128, n, 1024], BF16))
            tP = stk.enter_context(nc.sbuf_tensor("tPart" + tag, [128, n, 1024], BF16))
            S.add("sp", lambda e: e.dma_start(out=sel[:], in_=io["sel"]), writes=["selA"], dma_group="selA" + tag)
            ry = [("yT", c) for c in range(c0, c1)] + ["selA"]
            S.add("dve", lambda e: e.tensor_scalar(out=tO[:], in0=yT[:, c0:c1, 0:1024], scalar1=sel[:, 0:1], scalar2=None, op0=ALU.mult),
                  reads=ry, writes=["tOwn"])
            S.add("dve", lambda e: e.scalar_tensor_tensor(out=tO[:], in0=yT[:, c0:c1, 1024:2048], scalar=sel[:, 1:2], in1=tO[:], op0=ALU.mult, op1=ALU.add),
                  reads=ry + ["tOwn"], writes=["tOwn"])
            S.add("dve", lambda e: e.tensor_scalar(out=tP[:], in0=yT[:, c0:c1, 0:1024], scalar1=sel[:, 1:2], scalar2=None, op0=ALU.mult),
                  reads=ry, writes=["tPart"])
            S.add("dve", lambda e: e.scalar_tensor_tensor(out=tP[:], in0=yT[:, c0:c1, 1024:2048], scalar=sel[:, 0:1], in1=tP[:], op0=ALU.mult, op1=ALU.add),
                  reads=ry + ["tPart"], writes=["tPart"])
            S.add("sp", lambda e: e.dma_start(out=d_own.rearrange("(c p) t -> p c t", p=128), in_=tO[:]),
                  reads=["tOwn"], writes=["yown" + tag], dma_group="yown" + tag)
            S.add("sp", lambda e: e.dma_start(out=d_send.rearrange("(c p) t -> p c t", p=128), in_=tP[:]),
                  reads=["tPart"], writes=["ysend" + tag], dma_group="ysend" + tag)

        if split:
            with ExitStack() as sx:
                emit_select(sx, 0, 5, io["yownAB"], io["ysendAB"], "AB")
                S.barrier()
            io["cc_ab"]()
        if "c" in _mix:
            with ExitStack() as sc_:
                emit_mixer_c(S, nc, sc_, io, T)
                S.barrier()
        if io.get("ysend") is None:
            for c in range(8):
                S.add("sp", lambda e, c=c: e.dma_start(out=io["yT"][c * 128:(c + 1) * 128, :], in_=yT[:, c, :]),
                      reads=[("yT", c)], dma_group=("yo", c % 4))
        else:
            if split:
                emit_select(ms, 5, 8, io["yown"], io["ysend"], "C")
            else:
                emit_select(ms, 0, 8, io["yown"], io["ysend"], "")
        S.barrier()


def emit_transposes_to_yT(S, T, src, nblk_feat, chunk0, tag):
    tr, ident, yT = T["tr"], T["ident"], T["yT"]
    for f in range(nblk_feat):
        for half in range(2):
            for t8 in range(8):
                tb = half * 8 + t8
                S.add("pe", lambda e, f=f, tb=tb, t8=t8: e.transpose(
                    out=tr[:, t8 * 128:(t8 + 1) * 128], in_=src[:, tb, f * 128:(f + 1) * 128], identity=ident[:]),
                    reads=[(tag, tb), "ident"], writes=["tr"])
            eng = "dve" if (f + half) % 2 == 0 else "act"
            if eng == "dve":
                S.add("dve", lambda e, f=f, half=half: e.tensor_copy(
                    out=yT[:, chunk0 + f, half * 1024:(half + 1) * 1024], in_=tr[:]),
                    reads=["tr"], writes=[("yT", chunk0 + f)])
            else:
                S.add("act", lambda e, f=f, half=half: e.copy(
                    out=yT[:, chunk0 + f, half * 1024:(half + 1) * 1024], in_=tr[:]),
                    reads=["tr"], writes=[("yT", chunk0 + f)])


def emit_mixer_a(S, nc, sa, io, T):
    qk, vA, spm, ident, sm = T["qk"], T["vA"], T["spm"], T["ident"], T["sm"]
    sc, acc, et, ytok = T["sc"], T["acc"], T["et"], T["ytok"]
    SL0, LM0, lam_init = T["SL0"], T["LM0"], T["lam_init"]
    kaug = sa.enter_context(nc.sbuf_tensor("kaug_sb", [4, 2, SEQ], BF16))
    qaug = sa.enter_context(nc.sbuf_tensor("qaug_sb", [4, 2, SEQ], BF16))
    diag = sa.enter_context(nc.sbuf_tensor("diagA", [128, 2, 128], BF16))
    accA = sa.enter_context(nc.sbuf_tensor("accA", [128, 16, 2, 129], F32))
    rD = sa.enter_context(nc.sbuf_tensor("rDA", [128, 16, 2], F32))
    ss = sa.enter_context(nc.sbuf_tensor("ssA", [128, 16], F32))
    lt = sa.enter_context(nc.sbuf_tensor("ltA", [128, 64], F32))
    S.add("pool", lambda e: e.dma_start(out=kaug[:], in_=io["kaug"]), writes=["kaug"], dma_group="augk")
    S.add("pool", lambda e: e.dma_start(out=qaug[:], in_=io["qaug"]), writes=["qaug"], dma_group="augq")
    S.add("pool", lambda e: e.dma_start(out=diag[:], in_=io["diag"]), writes=["diag"], dma_group="augd")
    for i in range(2):
        S.add("dve", lambda e, i=i: e.tensor_tensor(out=lt[:], in0=spm[:, LM0 + 128 * i:LM0 + 128 * i + 64],
                                                    in1=spm[:, LM0 + 128 * i + 64:LM0 + 128 * i + 128], op=ALU.mult),
              reads=["spm"], writes=["ltA"])
        S.add("dve", lambda e, i=i: e.reduce_sum(out=sm[:, 1 + i:2 + i], in_=lt[:], axis=AX.X),
              reads=["ltA"], writes=[("sm", 1 + i)])
    S.add("act", lambda e: e.activation(out=sm[:, 1:3], in_=sm[:, 1:3], func=AF.Exp),
          reads=[("sm", 1), ("sm", 2)], writes=[("sm", 1), ("sm", 2)])
    S.add("dve", lambda e: e.tensor_tensor(out=sm[:, 0:1], in0=sm[:, 1:2], in1=sm[:, 2:3], op=ALU.subtract),
          reads=[("sm", 1), ("sm", 2)], writes=[("sm", 0)])
    S.add("dve", lambda e: e.tensor_scalar(out=sm[:, 0:1], in0=sm[:, 0:1], scalar1=lam_init, scalar2=None, op0=ALU.add),
          reads=[("sm", 0)], writes=[("sm", 0)])
    def mk_scores(hl, qb, m, g0, g1, w):
        qT, kT = qk["Aq%d" % hl], qk["Ak%d" % hl]
        qs = slice(qb * 128, (qb + 1) * 128)
        pp = slice(m * 64, (m + 1) * 64)

        def fn():
            for kb in range(g0, g1):
                cs = slice((kb - g0) * 128, (kb - g0 + 1) * 128)
                ks = slice(kb * 128, (kb + 1) * 128)
                S.add("pe", lambda e, cs=cs, ks=ks: e.matmul(sc[w][:, cs], lhsT=kT[pp, ks], rhs=qT[pp, qs], start=True, stop=False),
                      reads=[("qk", "Aq%d" % hl), ("qk", "Ak%d" % hl)], writes=[("sc", w)])
                if kb < qb:
                    S.add("pe", lambda e, cs=cs, ks=ks: e.matmul(sc[w][:, cs], lhsT=kaug[:, hl, ks], rhs=qaug[:, hl, qs], start=False, stop=True),
                          reads=["kaug", "qaug"], writes=[("sc", w)])
                else:
                    S.add("pe", lambda e, cs=cs: e.matmul(sc[w][:, cs], lhsT=ident[:], rhs=diag[:, hl, :], start=False, stop=True),
                          reads=["ident", "diag"], writes=[("sc", w)])
            ncol = (g1 - g0) * 128
            S.add("act", lambda e: e.activation(out=et[w][:, 0:ncol], in_=sc[w][:, 0:ncol], func=AF.Exp),
                  reads=[("sc", w)], writes=[("et", w)])
        return fn

    def mk_pv(hl, qb, m, g0, g1, w):
        def fn():
            for kb in range(g0, g1):
                cs = slice((kb - g0) * 128, (kb - g0 + 1) * 128)
                S.add("pe", lambda e, cs=cs, kb=kb: e.matmul(acc[m][:, 0:129], lhsT=et[w][:, cs], rhs=vA[:, kb, hl, :],
                                                              start=(kb == 0), stop=(kb == qb)),
                      reads=[("et", w), ("vA", kb), "vA1"], writes=[("acc", m)])
            if g1 == qb + 1:
                if m == 0:
                    S.add("act", lambda e: e.copy(out=accA[:, qb, 0, :], in_=acc[0][:, 0:129]),
                          reads=[("acc", 0)], writes=[("accA", qb)])
                else:
                    S.add("dve", lambda e: e.tensor_copy(out=accA[:, qb, 1, :], in_=acc[1][:, 0:129]),
                          reads=[("acc", 1)], writes=[("accA", qb)])
        return fn

    def epilogue(hl):
        allq = [("accA", qb) for qb in range(16)]
        bc = lambda ap: ap.to_broadcast([128, 16, 128])
        S.add("dve", lambda e: e.reciprocal(out=rD[:], in_=accA[:, :, :, 128]), reads=allq, writes=["rD"])
        S.add("dve", lambda e: e.tensor_scalar(out=rD[:, :, 1], in0=rD[:, :, 1], scalar1=sm[:, 0:1], scalar2=-1.0, op0=ALU.mult, op1=ALU.mult),
              reads=["rD", ("sm", 0)], writes=["rD"])
        S.add("dve", lambda e: e.tensor_tensor(out=accA[:, :, 0, 0:128], in0=accA[:, :, 0, 0:128], in1=bc(rD[:, :, 0:1]), op=ALU.mult),
              reads=allq + ["rD"], writes=allq)
        S.add("dve", lambda e: e.tensor_tensor(out=accA[:, :, 1, 0:128], in0=accA[:, :, 1, 0:128], in1=bc(rD[:, :, 1:2]), op=ALU.mult),
              reads=allq + ["rD"], writes=allq)
        S.add("dve", lambda e: e.tensor_tensor(out=accA[:, :, 0, 0:128], in0=accA[:, :, 0, 0:128], in1=accA[:, :, 1, 0:128], op=ALU.add),
              reads=allq, writes=allq)
        S.add("pool", lambda e: e.tensor_tensor(out=accA[:, :, 1, 0:128], in0=accA[:, :, 0, 0:128], in1=accA[:, :, 0, 0:128], op=ALU.mult),
              reads=allq, writes=allq)
        S.add("dve", lambda e: e.reduce_sum(out=ss[:], in_=accA[:, :, 1, 0:128], axis=AX.X), reads=allq, writes=["ssA"])
        S.add("dve", lambda e: e.tensor_scalar(out=ss[:], in0=ss[:], scalar1=1.0 / 128, scalar2=1e-6, op0=ALU.mult, op1=ALU.add),
              reads=["ssA"], writes=["ssA"])
        S.add("act", lambda e: e.activation(out=ss[:], in_=ss[:], func=AF.Sqrt), reads=["ssA"], writes=["ssA"])
        S.add("dve", lambda e: e.reciprocal(out=ss[:], in_=ss[:]), reads=["ssA"], writes=["ssA"])
        S.add("dve", lambda e: e.tensor_scalar(out=ss[:], in0=ss[:], scalar1=1.0 - lam_init, scalar2=None, op0=ALU.mult),
              reads=["ssA"], writes=["ssA"])
        S.add("dve", lambda e: e.tensor_tensor(out=accA[:, :, 0, 0:128], in0=accA[:, :, 0, 0:128], in1=bc(ss[:].unsqueeze(2)), op=ALU.mult),
              reads=allq + ["ssA"], writes=allq)
        S.add("dve", lambda e: e.tensor_tensor(out=ytok[:], in0=accA[:, :, 0, 0:128], in1=bc(spm[:, SL0:SL0 + 128].unsqueeze(1)), op=ALU.mult),
              reads=allq + ["spm"], writes=[("ytok", qb) for qb in range(16)])
        emit_transposes_to_yT(S, T, ytok, 1, hl, "ytok")

    steps = []
    for hl in range(2):
        for qb in range(16):
            for m in range(2):
                for g0 in range(0, qb + 1, 8):
                    g1 = min(qb + 1, g0 + 8)
                    w = len(steps) % 2
                    last = (qb == 15 and m == 1 and g1 == qb + 1)
                    steps.append((mk_scores(hl, qb, m, g0, g1, w), mk_pv(hl, qb, m, g0, g1, w), hl if last else None))
    for i in range(len(steps) + 1):
        if i < len(steps):
            steps[i][0]()
        if i >= 1:
            steps[i - 1][1]()
            if steps[i - 1][2] is not None:
                epilogue(steps[i - 1][2])


def emit_mixer_b(S, nc, sb, io, T):
    qk, vB, ident, sm = T["qk"], T["vB"], T["ident"], T["sm"]
    sc, acc, et, ytok = T["sc"], T["acc"], T["et"], T["ytok"]
    tf = sb.enter_context(nc.sbuf_tensor("toepf", [128, 15, 128], F32))
    thi = sb.enter_context(nc.sbuf_tensor("toephi", [128, 15, 128], BF16))
    tlo = sb.enter_context(nc.sbuf_tensor("toeplo", [128, 15, 128], BF16))
    tf2 = sb.enter_context(nc.sbuf_tensor("toepf2", [128, 15, 128], F32))
    accB = sb.enter_context(nc.sbuf_tensor("accB", [128, 16, 129], F32))
    rB = sb.enter_context(nc.sbuf_tensor("rBB", [128, 16], F32))
    S.add("sp", lambda e: e.dma_start(out=tf[:], in_=io["toep"]), writes=["toepf"], dma_group="toep")
    S.add("dve", lambda e: e.tensor_copy(out=thi[:], in_=tf[:]), reads=["toepf"], writes=["thi"])
    S.add("dve", lambda e: e.tensor_copy(out=tf2[:], in_=thi[:]), reads=["thi"], writes=["toepf2"])
    S.add("dve", lambda e: e.tensor_tensor(out=tf2[:], in0=tf[:], in1=tf2[:], op=ALU.subtract),
          reads=["toepf", "toepf2"], writes=["toepf2"])
    S.add("dve", lambda e: e.tensor_copy(out=tlo[:], in_=tf2[:]), reads=["toepf2"], writes=["tlo"])
    def mk_scores(hl, qb, w):
        qT, kT = qk["Bq%d" % hl], qk["Bk%d" % hl]
        qs = slice(qb * 128, (qb + 1) * 128)
        js = [j for j in range(5) if qb - 4 + j >= 0]

        def fn():
            for j in js:
                kb = qb - 4 + j
                cs = slice(j * 128, (j + 1) * 128)
                ks = slice(kb * 128, (kb + 1) * 128)
                S.add("pe", lambda e, cs=cs, ks=ks: e.matmul(sc[w][:, cs], lhsT=kT[:, ks], rhs=qT[:, qs], start=True, stop=False),
                      reads=[("qk", "Bq%d" % hl), ("qk", "Bk%d" % hl)], writes=[("sc", w)])
                S.add("pe", lambda e, cs=cs, j=j: e.matmul(sc[w][:, cs], lhsT=ident[:], rhs=thi[:, hl * 5 + j, :], start=False, stop=False),
                      reads=["ident", "thi"], writes=[("sc", w)])
                S.add("pe", lambda e, cs=cs, j=j: e.matmul(sc[w][:, cs], lhsT=ident[:], rhs=tlo[:, hl * 5 + j, :], start=False, stop=True),
                      reads=["ident", "tlo"], writes=[("sc", w)])
            c0, c1 = js[0] * 128, 640
            S.add("act", lambda e: e.activation(out=et[w][:, c0:c1], in_=sc[w][:, c0:c1], func=AF.Exp),
                  reads=[("sc", w)], writes=[("et", w)])
        return fn

    def mk_pv(hl, qb, w):
        js = [j for j in range(5) if qb - 4 + j >= 0]

        def fn():
            for j in js:
                kb = qb - 4 + j
                cs = slice(j * 128, (j + 1) * 128)
                S.add("pe", lambda e, cs=cs, kb=kb, j=j: e.matmul(acc[w][:, 0:129], lhsT=et[w][:, cs], rhs=vB[:, kb, hl, :],
                                                                   start=(j == js[0]), stop=(j == 4)),
                      reads=[("et", w), ("vB", kb), "vB1"], writes=[("acc", w)])
            if qb % 2 == 0:
                S.add("act", lambda e: e.copy(out=accB[:, qb, :], in_=acc[w][:, 0:129]),
                      reads=[("acc", w)], writes=[("accB", qb)])
            else:
                S.add("dve", lambda e: e.tensor_copy(out=accB[:, qb, :], in_=acc[w][:, 0:129]),
                      reads=[("acc", w)], writes=[("accB", qb)])
        return fn

    def epilogue(hl):
        allq = [("accB", qb) for qb in range(16)]
        S.add("dve", lambda e: e.reciprocal(out=rB[:], in_=accB[:, :, 128]), reads=allq, writes=["rB"])
        S.add("dve", lambda e: e.tensor_tensor(out=ytok[:], in0=accB[:, :, 0:128], in1=rB[:].unsqueeze(2).to_broadcast([128, 16, 128]), op=ALU.mult),
              reads=allq + ["rB"], writes=[("ytok", qb) for qb in range(16)])
        emit_transposes_to_yT(S, T, ytok, 1, 2 + hl, "ytok")

    steps = []
    for hl in range(3):
        for qb in range(16):
            w = len(steps) % 2
            steps.append((mk_scores(hl, qb, w), mk_pv(hl, qb, w), hl if qb == 15 else None))
    for i in range(len(steps) + 1):
        if i < len(steps):
            steps[i][0]()
        if i >= 1:
            steps[i - 1][1]()
            if steps[i - 1][2] is not None:
                epilogue(steps[i - 1][2])


def emit_mixer_c(S, nc, sc_, io, T):
    qk, vC, og, graw, spm, ident, tri, ones, sm = (T["qk"], T["vC"], T["og"], T["graw"], T["spm"], T["ident"],
                                                     T["tri"], T["ones"], T["sm"])
    sc, acc, misc, tr = T["sc"], T["acc"], T["misc"], T["tr"]
    GB0, NC0 = T["GB0"], T["NC0"]
    g2 = sc_.enter_context(nc.sbuf_tensor("g2", [128, 16, 4], F32))
    lf = sc_.enter_context(nc.sbuf_tensor("lf", [128, 16, 2], F32))
    eq = sc_.enter_context(nc.sbuf_tensor("eq", [128, 16, 2], F32))
    ek = sc_.enter_context(nc.sbuf_tensor("ek", [128, 16, 2], F32))
    dec = sc_.enter_context(nc.sbuf_tensor("dec", [128, 16, 2], F32))
    ktok = sc_.enter_context(nc.sbuf_tensor("ktok", [128, 16, 3, 128], BF16))
    v2 = sc_.enter_context(nc.sbuf_tensor("v2", [128, 16, 2, 193], BF16))
    swT = [sc_.enter_context(nc.sbuf_tensor(f"swT{i}", [128, 128], BF16)) for i in range(2)]
    CTa = sc_.enter_context(nc.sbuf_tensor("CTa", [128, 2, 193], F32))
    CTb = sc_.enter_context(nc.sbuf_tensor("CTb", [128, 193], F32))
    CTa16 = sc_.enter_context(nc.sbuf_tensor("CTa16", [128, 2, 193], BF16))
    CTb16 = sc_.enter_context(nc.sbuf_tensor("CTb16", [128, 193], BF16))
    hbuf = sc_.enter_context(nc.sbuf_tensor("hbuf", [128, 16, 2, 193], F32))
    ssc = sc_.enter_context(nc.sbuf_tensor("ssC", [128, 16, 2], F32))
    ss2 = sc_.enter_context(nc.sbuf_tensor("ss2C", [128, 16, 2], F32))
    tq = [sc_.enter_context(nc.sbuf_tensor(f"tqC{i}", [128, 2, 192], F32)) for i in range(1)]
    ytc = og
    S.add("dve", lambda e: e.tensor_tensor(out=g2[:], in0=graw[:], in1=spm[:, GB0:GB0 + 4].unsqueeze(1).to_broadcast([128, 16, 4]),
                                           op=ALU.add), reads=["graw", "spm"], writes=["g2"])
    S.add("act", lambda e: e.activation(out=lf[:], in_=g2[:, :, 2:4], func=AF.Exp, scale=-1.0), reads=["g2"], writes=["lf"])
    S.add("act", lambda e: e.activation(out=lf[:], in_=lf[:], func=AF.Ln, bias=1.0), reads=["lf"], writes=["lf"])
    S.add("dve", lambda e: e.tensor_scalar(out=lf[:], in0=lf[:], scalar1=-1.0, scalar2=None, op0=ALU.mult), reads=["lf"], writes=["lf"])
    lf2 = lf[:].rearrange("p a b -> p (a b)")
    S.add("pe", lambda e: e.matmul(misc[:, 0:32], lhsT=tri[:], rhs=lf2, start=True, stop=True),
          reads=["tri", "lf"], writes=["misc"])
    S.add("pe", lambda e: e.matmul(misc[:, 32:64], lhsT=ones[:], rhs=lf2, start=True, stop=True),
          reads=["onesA", "lf"], writes=["misc"])
    S.add("act", lambda e: e.activation(out=eq[:].rearrange("p a b -> p (a b)"), in_=misc[:, 0:32], func=AF.Exp),
          reads=["misc"], writes=["eq", "miscrd"])
    S.add("act", lambda e: e.activation(out=dec[:].rearrange("p a b -> p (a b)"), in_=misc[:, 32:64], func=AF.Exp),
          reads=["misc"], writes=["dec", "miscrd2"])
    S.add("act", lambda e: e.activation(out=ek[:].rearrange("p a b -> p (a b)"), in_=misc[:, 0:32], func=AF.Identity, scale=-1.0),
          reads=["misc"], writes=["ek"])
    S.add("dve", lambda e: e.tensor_tensor(out=ek[:], in0=g2[:, :, 0:2], in1=ek[:], op=ALU.add), reads=["g2", "ek"], writes=["ek"])
    import math
    S.add("act", lambda e: e.activation(out=ek[:], in_=ek[:], func=AF.Exp, bias=math.log(CH ** -0.5)), reads=["ek"], writes=["ek"])
    S.add("dve", lambda e: e.tensor_tensor(out=v2[:], in0=vC[:], in1=ek[:].unsqueeze(3).to_broadcast([128, 16, 2, 193]), op=ALU.mult),
          reads=[("vC", tb) for tb in range(16)] + ["vC1", "ek"], writes=["v2"])
    ksrc = [qk["Ck0a"], qk["Ck1a"], qk["Ckb"]]
    for t2 in range(8):
        for i in range(2):
            tb = t2 * 2 + i
            tsl = slice(tb * 128, (tb + 1) * 128)
            for c3 in range(3):
                S.add("pe", lambda e, i=i, c3=c3, tsl=tsl: e.transpose(out=tr[:, (i * 3 + c3) * 128:(i * 3 + c3 + 1) * 128],
                                                                        in_=ksrc[c3][:, tsl], identity=ident[:]),
                      reads=[("qk", "Ck0a"), ("qk", "Ck1a"), ("qk", "Ckb"), "ident"], writes=["tr"])
        S.add("dve", lambda e, t2=t2: e.tensor_copy(out=ktok[:, t2 * 2:t2 * 2 + 2, :, :],
                                                    in_=tr[:, 0:768].rearrange("p (a c d) -> p a c d", a=2, c=3)),
              reads=["tr"], writes=["ktok"])
    S.add("pool", lambda e: e.memset(CTa[:], 0.0), writes=["CTa"])
    S.add("pool", lambda e: e.memset(CTb[:], 0.0), writes=[("CTb", 0), ("CTb", 1)])
    def mk_c(tb, hl, w):
        tsl = slice(tb * 128, (tb + 1) * 128)
        hs = slice(64 * hl, 64 * hl + 64)
        qa, qbb = qk["Cq%da" % hl], qk["Cqb"]
        ka, kbb = qk["Ck%da" % hl], qk["Ckb"]
        rq = [("qk", "Cq%da" % hl), ("qk", "Cqb")]
        rk = [("qk", "Ck%da" % hl), ("qk", "Ckb")]

        def stage1():
            S.add("pe", lambda e, w=w, ka=ka, qa=qa, tsl=tsl: e.matmul(sc[w][:, 0:128], lhsT=ka[:, tsl], rhs=qa[:, tsl], start=True, stop=False),
                  reads=rq + rk, writes=[("sc", w)])
            S.add("pe", lambda e, w=w, kbb=kbb, qbb=qbb, tsl=tsl, hs=hs: e.matmul(sc[w][:, 0:128], lhsT=kbb[hs, tsl], rhs=qbb[hs, tsl], start=False, stop=True),
                  reads=rq + rk, writes=[("sc", w)])
            S.add("dve", lambda e, w=w, tb=tb, hl=hl: e.scalar_tensor_tensor(out=swT[w][:], in0=sc[w][:, 0:128], scalar=ek[:, tb, hl:hl + 1],
                                                                             in1=tri[:], op0=ALU.mult, op1=ALU.mult),
                  reads=[("sc", w), "ek", "tri"], writes=[("swT", w)])

        def stage2():
            S.add("pe", lambda e, w=w, tb=tb, hl=hl: e.matmul(acc[hl][:, 0:193], lhsT=swT[w][:], rhs=vC[:, tb, hl, :], start=True, stop=(tb == 0)),
                  reads=[("swT", w), ("vC", tb), "vC1"], writes=[("acc", hl)])
            if tb > 0:
                S.add("pe", lambda e, hl=hl, qa=qa, tsl=tsl: e.matmul(acc[hl][:, 0:193], lhsT=qa[:, tsl], rhs=CTa16[:, hl, :], start=False, stop=False),
                      reads=rq + ["CTa16"], writes=[("acc", hl)])
                S.add("pe", lambda e, hl=hl, qbb=qbb, tsl=tsl, hs=hs: e.matmul(acc[hl][:, 0:193], lhsT=qbb[hs, tsl], rhs=CTb16[hs, :], start=False, stop=True),
                      reads=rq + [("CTb16", hl)], writes=[("acc", hl)])
            if tb < 15:
                S.add("pe", lambda e, tb=tb, hl=hl: e.matmul(misc[:, 0:193], lhsT=ktok[:, tb, hl, :], rhs=v2[:, tb, hl, :], start=True, stop=True),
                      reads=["ktok", "v2"], writes=["misc"])
                S.add("pe", lambda e, tb=tb, hl=hl: e.matmul(misc[:, 256:449], lhsT=ktok[:, tb, 2, :], rhs=v2[:, tb, hl, :], start=True, stop=True),
                      reads=["ktok", "v2"], writes=["misc"])
                S.add("dve", lambda e, hl=hl: e.tensor_tensor(out=CTa[:, hl, :], in0=CTa[:, hl, :], in1=misc[:, 0:193], op=ALU.add),
                      reads=["misc", "CTa"], writes=["CTa"])
                S.add("dve", lambda e, hs=hs: e.tensor_tensor(out=CTb[hs, :], in0=CTb[hs, :], in1=misc[hs, 256:449], op=ALU.add),
                      reads=["misc", ("CTb", hl)], writes=[("CTb", hl)])
                S.add("dve", lambda e, hl=hl, tb=tb: e.tensor_scalar(out=CTa[:, hl, :], in0=CTa[:, hl, :], scalar1=dec[:, tb, hl:hl + 1], scalar2=None, op0=ALU.mult),
                      reads=["CTa", "dec"], writes=["CTa"])
                S.add("dve", lambda e, hl=hl, tb=tb, hs=hs: e.tensor_scalar(out=CTb[hs, :], in0=CTb[hs, :], scalar1=dec[hs, tb, hl:hl + 1], scalar2=None, op0=ALU.mult),
                      reads=[("CTb", hl), "dec"], writes=[("CTb", hl)])
                S.add("act", lambda e, hl=hl: e.copy(out=CTa16[:, hl, :], in_=CTa[:, hl, :]), reads=["CTa"], writes=["CTa16"])
                S.add("act", lambda e, hs=hs: e.copy(out=CTb16[hs, :], in_=CTb[hs, :]), reads=[("CTb", hl)], writes=[("CTb16", hl)])
            S.add("act", lambda e, hl=hl, tb=tb: e.activation(out=hbuf[:, tb, hl, :], in_=acc[hl][:, 0:193], func=AF.Identity,
                                                              scale=eq[:, tb, hl:hl + 1]),
                  reads=[("acc", hl), "eq"], writes=[("hbuf", tb)])

        return stage1, stage2

    csteps = []
    for tb in range(16):
        for hl in range(2):
            csteps.append(mk_c(tb, hl, len(csteps) % 2))
    for i in range(len(csteps) + 1):
        if i < len(csteps):
            csteps[i][0]()
        if i >= 1:
            csteps[i - 1][1]()
    allh = [("hbuf", tb) for tb in range(16)]
    bch = lambda ap: ap.to_broadcast([128, 16, 2, 192])
    S.add("act", lambda e: e.activation(out=ssc[:], in_=hbuf[:, :, :, 192], func=AF.Abs), reads=allh, writes=["ssC"])
    S.add("dve", lambda e: e.tensor_scalar(out=ssc[:], in0=ssc[:], scalar1=1.0, scalar2=None, op0=ALU.max), reads=["ssC"], writes=["ssC"])
    S.add("dve", lambda e: e.reciprocal(out=ssc[:], in_=ssc[:]), reads=["ssC"], writes=["ssC"])
    S.add("dve", lambda e: e.tensor_tensor(out=hbuf[:, :, :, 0:192], in0=hbuf[:, :, :, 0:192], in1=bch(ssc[:].unsqueeze(3)), op=ALU.mult),
          reads=allh + ["ssC"], writes=allh)
    for tb in range(16):
        q = 0
        S.add("pool", lambda e, tb=tb, q=q: e.tensor_tensor(out=tq[q][:], in0=hbuf[:, tb, :, 0:192], in1=hbuf[:, tb, :, 0:192], op=ALU.mult),
              reads=[("hbuf", tb)], writes=[("tqC", q)])
        S.add("dve", lambda e, tb=tb, q=q: e.reduce_sum(out=ss2[:, tb, :], in_=tq[q][:], axis=AX.X), reads=[("tqC", q)], writes=["ss2"])
    S.add("dve", lambda e: e.tensor_scalar(out=ss2[:], in0=ss2[:], scalar1=1.0 / CH, scalar2=1e-6, op0=ALU.mult, op1=ALU.add), reads=["ss2"], writes=["ss2"])
    S.add("act", lambda e: e.activation(out=ss2[:], in_=ss2[:], func=AF.Sqrt), reads=["ss2"], writes=["ss2"])
    S.add("dve", lambda e: e.reciprocal(out=ss2[:], in_=ss2[:]), reads=["ss2"], writes=["ss2"])
    S.add("dve", lambda e: e.tensor_tensor(out=hbuf[:, :, :, 0:192], in0=hbuf[:, :, :, 0:192], in1=bch(ss2[:].unsqueeze(3)), op=ALU.mult),
          reads=allh + ["ss2"], writes=allh)
    for half in range(2):
        hsl = slice(half * 8, half * 8 + 8)
        kk = [("hbuf", tb) for tb in range(half * 8, half * 8 + 8)]
        S.add("dve", lambda e, hsl=hsl: e.tensor_tensor(out=hbuf[:, hsl, :, 0:192], in0=hbuf[:, hsl, :, 0:192],
                                                        in1=spm[:, NC0:NC0 + 192].unsqueeze(1).unsqueeze(1).to_broadcast([128, 8, 2, 192]), op=ALU.mult),
              reads=kk + ["spm"], writes=kk)
        S.add("pool", lambda e, hsl=hsl: e.tensor_tensor(out=og[:, hsl, :].rearrange("p t (h d) -> p t h d", h=2), in0=hbuf[:, hsl, :, 0:192],
                                                         in1=og[:, hsl, :].rearrange("p t (h d) -> p t h d", h=2), op=ALU.mult),
              reads=kk + [("og", tb) for tb in range(half * 8, half * 8 + 8)],
              writes=[("ytokC", tb) for tb in range(half * 8, half * 8 + 8)] + [("og", tb) for tb in range(half * 8, half * 8 + 8)])
    emit_transposes_to_yT(S, T, ytc, 3, 5, "ytokC")


def build_phase_a(li):
    nc = bass.Bass("TRN2", target_bir_lowering=False)
    def din(name, shape, dt=F32):
        return nc.dram_tensor(name, shape, dt, kind="ExternalInput").ap()
    io = {
        "xT": din("xT", [D, SEQ]),
        "wqk": din("wqk", [D, NQKB * 128]),
        "wv": din("wv", [D, NV]),
        "spm": din("spm", [128, 612]),
        "ident": din("ident", [128, 128]),
        "tri": din("tri", [128, 128]),
        "kaug": din("kaug", [4, 2, SEQ]),
        "qaug": din("qaug", [4, 2, SEQ]),
        "diag": din("diag", [128, 2, 128]),
        "toep": din("toep", [128, 15, 128]),
        "yT": nc.dram_tensor("yT", [1024, SEQ], BF16, kind="ExternalOutput").ap(),
    }
    S = Sched(nc)
    with ExitStack() as es:
        emit_phase_a(S, nc, es, io, li)
        S.emit()
    return nc


def _qk_cols(r):
    cols = np.full((NQKB, 128), -1, np.int64)
    bi = 0
    for hl in range(2):
        h = 2 * r + hl
        cols[bi] = QA + h * 128 + np.arange(128); bi += 1
        cols[bi] = KA + h * 128 + np.arange(128); bi += 1
    for hl in range(3):
        h = 3 * r + hl
        cols[bi] = QB + h * 128 + np.arange(128); bi += 1
        cols[bi] = KB + h * 128 + np.arange(128); bi += 1
    for base in (QC, KC):
        for hl in range(2):
            cols[bi] = base + (2 * r + hl) * CH + np.arange(128); bi += 1
        for hl in range(2):
            cols[bi, 64 * hl:64 * hl + 64] = base + (2 * r + hl) * CH + 128 + np.arange(64)
        bi += 1
    return cols.reshape(-1)


def _v_cols(r):
    return np.concatenate([
        VA_ + 2 * r * 128 + np.arange(256), VB_ + 3 * r * 128 + np.arange(384),
        VC_ + 2 * r * CH + np.arange(384), OC + 2 * r * CH + np.arange(384),
        np.array([IC + 2 * r, IC + 2 * r + 1, FC + 2 * r, FC + 2 * r + 1])])


def pack_phase_a_weights(w_in_l, gate_bias_l, conv_w_l, lam_l, subln_l, normc_l, rel_bias_l, r):
    cols = _qk_cols(r)
    wqk = np.zeros((D, NQKB * 128), np.float32)
    ok = cols >= 0
    wqk[:, ok] = w_in_l[:, cols[ok]]
    wv = np.zeros((D, NV), np.float32)
    wv[:, 0:1412] = w_in_l[:, _v_cols(r)]
    spm = np.zeros((128, 612), np.float32)
    spm[:, 0:4] = gate_bias_l[[2 * r, 2 * r + 1, 4 + 2 * r, 4 + 2 * r + 1]][None, :]
    cb = 0
    for base in (0, 768):
        for hl in range(2):
            ch = base + (2 * r + hl) * CH
            spm[:, 4 + cb * 4:8 + cb * 4] = conv_w_l[:, ch:ch + 128].T; cb += 1
        for hl in range(2):
            ch = base + (2 * r + hl) * CH
            spm[64 * hl:64 * hl + 64, 4 + cb * 4:8 + cb * 4] = conv_w_l[:, ch + 128:ch + 192].T
        cb += 1
    spm[:, 36:164] = subln_l[None, :]
    spm[:, 164:356] = normc_l[None, :]
    spm[:, 356:612] = lam_l.reshape(1, 256)
    kl = np.arange(128)[:, None]
    ql = np.arange(128)[None, :]
    toep = np.zeros((128, 15, 128), np.float32)
    for hl in range(3):
        h = 3 * r + hl
        for j in range(5):
            rel = 128 * (4 - j) + ql - kl
            idx = np.clip(rel, -256, 256) + 256
            dc = 2 * (4 - j) + ql // 64 - kl // 64
            valid = (dc >= 0) & (dc <= 8)
            toep[:, hl * 5 + j, :] = np.where(valid, rel_bias_l[h][idx], np.float32(NEGM))
    return dict(wqk=wqk, wv=wv, spm=spm, toep=toep)


def phase_a_consts(r):
    pos = np.arange(SEQ)
    jb, rr = pos // 128, pos % 128
    kaug = np.zeros((4, 2, SEQ), np.float32)
    qaug = np.zeros((4, 2, SEQ), np.float32)
    diag = np.zeros((128, 2, 128), np.float32)
    kl = np.arange(128)[:, None]
    ql = np.arange(128)[None, :]
    for hl in range(2):
        sl = SLOPES[2 * r + hl]
        kaug[0, hl] = 1.0; kaug[1, hl] = 1.0; kaug[2, hl] = sl * 128 * jb; kaug[3, hl] = sl * rr
        qaug[0, hl] = -sl * 128 * jb; qaug[1, hl] = -sl * rr; qaug[2, hl] = 1.0; qaug[3, hl] = 1.0
        diag[:, hl, :] = np.where(kl // 64 <= ql // 64, -sl * np.abs(ql - kl), NEGM)
    ident = np.eye(128, dtype=np.float32)
    tri = (np.arange(128)[:, None] <= np.arange(128)[None, :]).astype(np.float32)
    return dict(kaug=kaug, qaug=qaug, diag=diag, ident=ident, tri=tri)


_PROG_CACHE = {}


def _prog(key, builder):
    if key not in _PROG_CACHE:
        _PROG_CACHE[key] = builder()
    return _PROG_CACHE[key]


def _wout_perm():
    parts = []
    for r in range(2):
        parts += [2 * r * 128 + np.arange(256), 512 + 3 * r * 128 + np.arange(384), 1280 + 2 * r * CH + np.arange(384)]
    return np.concatenate(parts)


def kernel_unfused(x, w_in, gate_bias, conv_w, lam, subln_a, norm_c, rel_bias, w_out,
                   ln1_g, ln1_b, w_gate, w_up, w_down, ln2_g, ln2_b):
    f = lambda a: np.ascontiguousarray(np.asarray(a, dtype=np.float32))
    x, w_in, gate_bias, conv_w, lam, subln_a, norm_c, rel_bias = map(f, (x, w_in, gate_bias, conv_w, lam, subln_a, norm_c, rel_bias))
    w_out, ln1_g, ln1_b, w_gate, w_up, w_down, ln2_g, ln2_b = map(f, (w_out, ln1_g, ln1_b, w_gate, w_up, w_down, ln2_g, ln2_b))
    cores = list(range(8))
    xT = [np.ascontiguousarray(x[b].T) for b in range(NB)]
    perm = _wout_perm()
    consts = [phase_a_consts(r) for r in range(2)]
    for li in range(DEPTH):
        nca = _prog(("A", li), lambda: build_phase_a(li))
        packs = [pack_phase_a_weights(w_in[li], gate_bias[li], conv_w[li], lam[li], subln_a[li], norm_c[li], rel_bias[li], r)
                 for r in range(2)]
        maps = []
        for c in cores:
            b, r = c // 2, c % 2
            m = dict(packs[r]); m.update(consts[r]); m["xT"] = xT[b]
            maps.append(m)
        ra = run_bass_kernel_spmd(nca, maps, core_ids=cores).results
        ncb = _prog(("B",), lambda: build_phase_b())
        wo = np.ascontiguousarray(w_out[li][perm, :])
        lnp = lnp_pack(ln1_g[li], ln1_b[li], ln2_g[li], ln2_b[li])
        maps = []
        for c in cores:
            b, r = c // 2, c % 2
            yT = np.concatenate([ra[2 * b]["yT"], ra[2 * b + 1]["yT"]], axis=0)
            maps.append({"yT": np.ascontiguousarray(yT[:, r * TOK:(r + 1) * TOK]),
                         "xres": np.ascontiguousarray(xT[b][:, r * TOK:(r + 1) * TOK]),
                         "w_out": wo, "w_gate": w_gate[li], "w_up": w_up[li], "w_down": w_down[li], "lnp": lnp})
        rb = run_bass_kernel_spmd(ncb, maps, core_ids=cores).results
        xT = [np.concatenate([rb[2 * b]["xo"], rb[2 * b + 1]["xo"]], axis=1) for b in range(NB)]
    return np.ascontiguousarray(np.stack([xT[b].T for b in range(NB)], axis=0)).astype(np.float32)


class NCP:
    def __init__(self, nc, sfx):
        self._nc = nc
        self._sfx = sfx

    def sbuf_tensor(self, name, *a, **k):
        return self._nc.sbuf_tensor(name + self._sfx, *a, **k)

    def psum_tensor(self, name, *a, **k):
        return self._nc.psum_tensor(name + self._sfx, *a, **k)

    def __getattr__(self, n):
        return getattr(self._nc, n)


PAIRS = [[0, 1], [2, 3], [4, 5], [6, 7]]
A_IN = [("wqk", [D, NQKB * 128]), ("wv", [D, NV]), ("spm", [128, 612]), ("toep", [128, 15, 128])]
B_IN = [("w_out", [D, D]), ("w_gate", [D, DFF]), ("w_up", [D, DFF]), ("w_down", [DFF, D]), ("lnp", [128, 4, 16])]
C_IN = [("ident", [128, 128]), ("tri", [128, 128]), ("kaug", [4, 2, SEQ]), ("qaug", [4, 2, SEQ]), ("diag", [128, 2, 128]),
        ("sel", [128, 2])]


def build_fused():
    nc = bass.Bass("TRN2", target_bir_lowering=False)
    def din(name, shape, dt=F32):
        return nc.dram_tensor(name, shape, dt, kind="ExternalInput").ap()
    gio = {"xT": din("xT", [D, SEQ]), "xres": din("xres", [D, TOK])}
    for name, shp in C_IN:
        gio[name] = din(name, shp)
    for li in range(DEPTH):
        for name, shp in A_IN + B_IN:
            gio[name + str(li)] = din(name + str(li), shp)
    xo = nc.dram_tensor("xo", [D, TOK], F32, kind="ExternalOutput").ap()
    scr = {}
    for li in range(DEPTH):
        scr["yown", li] = nc.dram_tensor(f"yown{li}", [1024, 1024], BF16).ap()
        scr["ysend", li] = nc.dram_tensor(f"ysend{li}", [1024, 1024], BF16).ap()
        scr["ygath", li] = nc.dram_tensor(f"ygath{li}", [2048, 1024], BF16).ap()
        scr["yownAB", li] = nc.dram_tensor(f"yownAB{li}", [640, 1024], BF16).ap()
        scr["ysendAB", li] = nc.dram_tensor(f"ysendAB{li}", [640, 1024], BF16).ap()
        scr["ygathAB", li] = nc.dram_tensor(f"ygathAB{li}", [1280, 1024], BF16).ap()
        scr["yownC", li] = nc.dram_tensor(f"yownC{li}", [384, 1024], BF16).ap()
        scr["ysendC", li] = nc.dram_tensor(f"ysendC{li}", [384, 1024], BF16).ap()
        scr["ygathC", li] = nc.dram_tensor(f"ygathC{li}", [768, 1024], BF16).ap()
    xsend = [nc.dram_tensor(f"xsend{h}", [D // 2, TOK], BF16).ap() for h in range(2)]
    xgath = [nc.dram_tensor(f"xgath{h}", [D, TOK], BF16).ap() for h in range(2)]
    xres_scr = nc.dram_tensor("xres_scr", [D, TOK], F32).ap()
    S = Sched(nc)
    for li in range(DEPTH):
        with ExitStack() as es:
            io = {k: gio[k] for k, _ in C_IN}
            for name, _ in A_IN:
                io[name] = gio[name + str(li)]
            io["xT"] = gio["xT"]
            io["xgath"] = xgath if li > 0 else None
            io["yown"], io["ysend"] = scr["yownC", li], scr["ysendC", li]
            io["yownAB"], io["ysendAB"] = scr["yownAB", li], scr["ysendAB", li]
            io["cc_ab"] = lambda li=li: S.add("pool", lambda e: e.collective_compute(
                "AllGather", ALU.bypass, replica_groups=PAIRS, ins=[scr["ysendAB", li].opt()], outs=[scr["ygathAB", li].opt()]),
                reads=["ysendAB"], writes=["ygathAB"], dma_group=("ccyab", li), inc=1)
            emit_phase_a(S, NCP(nc, f"_a{li}"), es, io, li)
        S.add("pool", lambda e, li=li: e.collective_compute("AllGather", ALU.bypass, replica_groups=PAIRS,
                                                            ins=[scr["ysendC", li].opt()], outs=[scr["ygathC", li].opt()]),
              reads=["ysendC"], writes=["ygathC"], dma_group=("ccy", li), inc=1)
        with ExitStack() as es:
            io = {"sel": gio["sel"]}
            for name, _ in B_IN:
                io[name] = gio[name + str(li)]
            io["yown"], io["ygath"] = scr["yownC", li], scr["ygathC", li]
            io["yownAB"], io["ygathAB"] = scr["yownAB", li], scr["ygathAB", li]
            io["xres"] = gio["xres"] if li == 0 else xres_scr
            if li == 0:
                io["xo"], io["xo_key"], io["xsend"] = xres_scr, "xres_scr", xsend
            else:
                io["xo"], io["xsend"] = xo, None
            if li > 0:
                pass
            emit_phase_b(S, NCP(nc, f"_b{li}"), es, io)
            S.barrier()
        if li == 0:
            for fh in range(2):
                S.add("pool", lambda e, fh=fh: e.collective_compute("AllGather", ALU.bypass, replica_groups=PAIRS,
                                                                    ins=[xsend[fh].opt()], outs=[xgath[fh].opt()]),
                      reads=["xsend"], writes=["xgath"], dma_group=("ccx", fh), inc=1)
    S.barrier()
    S.emit()
    return nc


def kernel(x, w_in, gate_bias, conv_w, lam, subln_a, norm_c, rel_bias, w_out,
           ln1_g, ln1_b, w_gate, w_up, w_down, ln2_g, ln2_b):
    f = lambda a: np.ascontiguousarray(np.asarray(a, dtype=np.float32))
    x, w_in, gate_bias, conv_w, lam, subln_a, norm_c, rel_bias = map(f, (x, w_in, gate_bias, conv_w, lam, subln_a, norm_c, rel_bias))
    w_out, ln1_g, ln1_b, w_gate, w_up, w_down, ln2_g, ln2_b = map(f, (w_out, ln1_g, ln1_b, w_gate, w_up, w_down, ln2_g, ln2_b))
    cores = list(range(8))
    perm = _wout_perm()
    nc = _prog(("F",), build_fused)
    xT = [np.ascontiguousarray(x[b].T) for b in range(NB)]
    shared = {}
    for li in range(DEPTH):
        shared["w_out%d" % li] = np.ascontiguousarray(w_out[li][perm, :])
        shared["w_gate%d" % li] = w_gate[li]
        shared["w_up%d" % li] = w_up[li]
        shared["w_down%d" % li] = w_down[li]
        shared["lnp%d" % li] = lnp_pack(ln1_g[li], ln1_b[li], ln2_g[li], ln2_b[li])
    per_r = []
    for r in range(2):
        m = dict(phase_a_consts(r))
        sel = np.zeros((128, 2), np.float32)
        sel[:, r] = 1.0
        m["sel"] = sel
        for li in range(DEPTH):
            pk = pack_phase_a_weights(w_in[li], gate_bias[li], conv_w[li], lam[li], subln_a[li], norm_c[li], rel_bias[li], r)
            for k, v in pk.items():
                m[k + str(li)] = v
        per_r.append(m)
    maps = []
    for c in cores:
        b, r = c // 2, c % 2
        m = dict(shared); m.update(per_r[r])
        m["xT"] = xT[b]
        m["xres"] = np.ascontiguousarray(xT[b][:, r * TOK:(r + 1) * TOK])
        maps.append(m)
    res = run_bass_kernel_spmd(nc, maps, core_ids=cores).results
    out = np.empty((NB, SEQ, D), np.float32)
    for c in cores:
        b, r = c // 2, c % 2
        out[b, r * TOK:(r + 1) * TOK, :] = res[c]["xo"].T
    return out
```
